# Optimizing a Trainium2 kernel written in Bass

```python
import jax, jax.numpy as jnp
from jax import lax
import numpy as np

D_MODEL = 1024
BATCH = 8
SEQ = 4096
DEPTH = 1

HEAD_DIM = 64
DIL_GROUPS = ((128, 1), (512, 4), (2048, 16))
N_DIL_GROUPS = 3
DIL_HEADS = 4
DIL_WIDTH = N_DIL_GROUPS * DIL_HEADS * HEAD_DIM
DIL_OUT = DIL_HEADS * HEAD_DIM
FOX_HEADS = 8
FOX_WIDTH = FOX_HEADS * HEAD_DIM
MEM_HEADS = 4
MEM_HEAD_DIM = 128
MEM_WIDTH = MEM_HEADS * MEM_HEAD_DIM
MEM_LEN = 256
ROT_DIM = HEAD_DIM // 4
ROPE_THETA = 500000.0
D_FF = 4 * D_MODEL
N_BRANCH = 3
BLOCK_Q = 128
IN_SPLITS = (DIL_WIDTH, DIL_WIDTH, DIL_WIDTH, FOX_WIDTH, FOX_WIDTH, FOX_WIDTH, FOX_HEADS, MEM_WIDTH)
IN_WIDTH = 3 * DIL_WIDTH + 3 * FOX_WIDTH + FOX_HEADS + MEM_WIDTH
EPS = 1e-6

kernel_name = "hybrid_gated_dilated_fox_memory_layer"


def rmsnorm(x, g):
    xf = x.astype(jnp.float32)
    y = xf * lax.rsqrt(jnp.mean(xf * xf, axis=-1, keepdims=True) + EPS)
    return (y * g.astype(jnp.float32)).astype(x.dtype)


def partial_rope(x, cos, sin):
    half = ROT_DIM // 2
    c = cos[None, :, None, None, :].astype(x.dtype)
    s = sin[None, :, None, None, :].astype(x.dtype)
    x1 = x[..., :half]
    x2 = x[..., half:ROT_DIM]
    return jnp.concatenate([x1 * c - x2 * s, x2 * c + x1 * s, x[..., ROT_DIM:]], axis=-1)


def banded_causal_attention(q, k, v, n_back):
    G, N, H, E = q.shape
    C = n_back
    nb = -(-N // C)
    pad = nb * C - N
    padw = ((0, 0), (0, pad), (0, 0), (0, 0))
    qb = jnp.pad(q.astype(jnp.float32), padw).reshape(G, nb, C, H, E)
    kb = jnp.pad(k.astype(jnp.float32), padw).reshape(G, nb, C, H, E)
    vb = jnp.pad(v.astype(jnp.float32), padw).reshape(G, nb, C, H, E)
    k_prev = jnp.concatenate([jnp.zeros_like(kb[:, :1]), kb[:, :-1]], axis=1)
    v_prev = jnp.concatenate([jnp.zeros_like(vb[:, :1]), vb[:, :-1]], axis=1)
    kk = jnp.concatenate([k_prev, kb], axis=2)
    vv = jnp.concatenate([v_prev, vb], axis=2)
    s = jnp.einsum('gbqhe,gbkhe->gbhqk', qb, kk) * (HEAD_DIM ** -0.5)
    qi = jnp.arange(C)[:, None]
    kj = jnp.arange(2 * C)[None, :]
    dist = qi + C - kj
    in_band = (dist >= 0) & (dist <= n_back)
    blk_ok = (jnp.arange(nb)[:, None] > 0) | (jnp.arange(2 * C)[None, :] >= C)
    mask = in_band[None, :, :] & blk_ok[:, None, :]
    s = jnp.where(mask[None, :, None], s, -jnp.inf)
    m = jnp.max(s, axis=-1, keepdims=True)
    p = jnp.exp(s - m)
    den = jnp.sum(p, axis=-1, keepdims=True)
    out = jnp.einsum('gbhqk,gbkhe->gbqhe', p, vv) / den.transpose(0, 1, 3, 2, 4)
    lse = (m + jnp.log(den))[..., 0].transpose(0, 1, 3, 2)
    out = out.reshape(G, nb * C, H, E)[:, :N]
    lse = lse.reshape(G, nb * C, H)[:, :N]
    return out, lse


def dilated_group(q, k, v, window, dilation):
    B, T, H, E = q.shape
    n = T // dilation

    def to_sub(a):
        return a.reshape(B, n, dilation, H, E).transpose(0, 2, 1, 3, 4).reshape(B * dilation, n, H, E)

    out, lse = banded_causal_attention(to_sub(q), to_sub(k), to_sub(v), window // dilation)
    out = out.reshape(B, dilation, n, H, E).transpose(0, 2, 1, 3, 4).reshape(B, T, H, E)
    lse = lse.reshape(B, dilation, n, H).transpose(0, 2, 1, 3).reshape(B, T, H)
    return out, lse


def fox_attention(q, k, v, logf):
    B, T, H, E = q.shape
    nq = T // BLOCK_Q
    kf = k.astype(jnp.float32)
    vf = v.astype(jnp.float32)
    c = jnp.cumsum(logf.astype(jnp.float32), axis=1)
    q_blocks = q.astype(jnp.float32).reshape(B, nq, BLOCK_Q, H, E).transpose(1, 0, 2, 3, 4)
    c_blocks = c.reshape(B, nq, BLOCK_Q, H).transpose(1, 0, 2, 3)
    starts = jnp.arange(nq) * BLOCK_Q
    c_keys = c.transpose(0, 2, 1)
    key_pos = jnp.arange(T)

    def one_block(args):
        qi, ci, start = args
        s = jnp.einsum('bqhe,bkhe->bhqk', qi, kf) * (HEAD_DIM ** -0.5)
        s = s + ci.transpose(0, 2, 1)[..., None] - c_keys[:, :, None, :]
        qpos = start + jnp.arange(BLOCK_Q)
        s = jnp.where(key_pos[None, :] <= qpos[:, None], s, -jnp.inf)
        p = jax.nn.softmax(s, axis=-1)
        return jnp.einsum('bhqk,bkhe->bqhe', p, vf)

    out = lax.map(one_block, (q_blocks, c_blocks, starts))
    return out.transpose(1, 0, 2, 3, 4).reshape(B, T, H * E)


def setup_inputs(seed: int = 0) -> dict:
    key = jax.random.key(seed)
    ks = jax.random.split(key, 24)
    L, D = DEPTH, D_MODEL

    def nrm(k, shape, fan_in):
        return jax.random.normal(k, shape, jnp.float32) * (fan_in ** -0.5)

    def gain(k, shape):
        return 1.0 + 0.05 * jax.random.normal(k, shape, jnp.float32)

    return {
        "x": jax.random.normal(ks[0], (BATCH, SEQ, D), jnp.float32),
        "mem": jax.random.normal(ks[1], (BATCH, MEM_LEN, D), jnp.float32),
        "g_mix": gain(ks[2], (L, D)),
        "w_in": nrm(ks[3], (L, D, IN_WIDTH), D),
        "b_f": 3.0 + 0.1 * jax.random.normal(ks[4], (L, FOX_HEADS), jnp.float32),
        "g_qA": gain(ks[5], (L, HEAD_DIM)),
        "g_kA": gain(ks[6], (L, HEAD_DIM)),
        "g_qB": gain(ks[7], (L, HEAD_DIM)),
        "g_kB": gain(ks[8], (L, HEAD_DIM)),
        "g_mem": gain(ks[9], (L, D)),
        "w_mem_kv": nrm(ks[10], (L, D, 2 * MEM_WIDTH), D),
        "g_qM": gain(ks[11], (L, MEM_HEAD_DIM)),
        "g_kM": gain(ks[12], (L, MEM_HEAD_DIM)),
        "w_gate": nrm(ks[13], (L, D, N_BRANCH * D), D),
        "b_gate": 0.1 * jax.random.normal(ks[14], (L, N_BRANCH * D), jnp.float32),
        "w_br_a": nrm(ks[15], (L, DIL_OUT, D), DIL_OUT),
        "w_br_b": nrm(ks[16], (L, FOX_WIDTH, D), FOX_WIDTH),
        "w_br_m": nrm(ks[17], (L, MEM_WIDTH, D), MEM_WIDTH),
        "w_out": nrm(ks[18], (L, D, D), D),
        "g_mlp": gain(ks[19], (L, D)),
        "w_up": nrm(ks[20], (L, D, D_FF), D),
        "w_down": nrm(ks[21], (L, D_FF, D), D_FF),
    }


def reference(x, mem, g_mix, w_in, b_f, g_qA, g_kA, g_qB, g_kB, g_mem, w_mem_kv, g_qM, g_kM,
              w_gate, b_gate, w_br_a, w_br_b, w_br_m, w_out, g_mlp, w_up, w_down):
    B, T, D = x.shape
    dt = x.dtype
    inv_freq = ROPE_THETA ** (-jnp.arange(0, ROT_DIM, 2, dtype=jnp.float32) / ROT_DIM)
    ang = jnp.arange(T, dtype=jnp.float32)[:, None] * inv_freq[None, :]
    cos, sin = jnp.cos(ang), jnp.sin(ang)
    offs = []
    acc = 0
    for w in IN_SPLITS[:-1]:
        acc += w
        offs.append(acc)

    for l in range(DEPTH):
        h = rmsnorm(x, g_mix[l])
        proj = h @ w_in[l]
        qa, ka, va, qb, kb, vb, fl, qm = jnp.split(proj, offs, axis=-1)

        qa = partial_rope(rmsnorm(qa.reshape(B, T, N_DIL_GROUPS, DIL_HEADS, HEAD_DIM), g_qA[l]), cos, sin)
        ka = partial_rope(rmsnorm(ka.reshape(B, T, N_DIL_GROUPS, DIL_HEADS, HEAD_DIM), g_kA[l]), cos, sin)
        va = va.reshape(B, T, N_DIL_GROUPS, DIL_HEADS, HEAD_DIM)
        outs, lses = [], []
        for gi, (win, dil) in enumerate(DIL_GROUPS):
            o, s = dilated_group(qa[:, :, gi], ka[:, :, gi], va[:, :, gi], win, dil)
            outs.append(o)
            lses.append(s)
        alpha = jax.nn.softmax(jnp.stack(lses, axis=0), axis=0)
        ya = jnp.sum(alpha[..., None] * jnp.stack(outs, axis=0), axis=0).reshape(B, T, DIL_OUT).astype(dt)

        qb = rmsnorm(qb.reshape(B, T, FOX_HEADS, HEAD_DIM), g_qB[l])
        kb = rmsnorm(kb.reshape(B, T, FOX_HEADS, HEAD_DIM), g_kB[l])
        vb = vb.reshape(B, T, FOX_HEADS, HEAD_DIM)
        logf = jax.nn.log_sigmoid(fl.astype(jnp.float32) + b_f[l].astype(jnp.float32))
        yb = fox_attention(qb, kb, vb, logf).astype(dt)

        mem_n = rmsnorm(mem, g_mem[l])
        km, vm = jnp.split(mem_n @ w_mem_kv[l], 2, axis=-1)
        km = rmsnorm(km.reshape(B, MEM_LEN, MEM_HEADS, MEM_HEAD_DIM), g_kM[l]).astype(jnp.float32)
        vm = vm.reshape(B, MEM_LEN, MEM_HEADS, MEM_HEAD_DIM).astype(jnp.float32)
        qm = rmsnorm(qm.reshape(B, T, MEM_HEADS, MEM_HEAD_DIM), g_qM[l]).astype(jnp.float32)
        sm = jnp.einsum('bthe,bshe->bhts', qm, km) * (MEM_HEAD_DIM ** -0.5)
        pm = jax.nn.softmax(sm, axis=-1)
        ym = jnp.einsum('bhts,bshe->bthe', pm, vm).reshape(B, T, MEM_WIDTH).astype(dt)

        gates = jax.nn.sigmoid(h @ w_gate[l] + b_gate[l]).reshape(B, T, N_BRANCH, D)
        merged = (gates[:, :, 0] * (ya @ w_br_a[l])
                  + gates[:, :, 1] * (yb @ w_br_b[l])
                  + gates[:, :, 2] * (ym @ w_br_m[l]))
        x = x + merged @ w_out[l]

        h2 = rmsnorm(x, g_mlp[l])
        x = x + jnp.square(jax.nn.relu(h2 @ w_up[l])) @ w_down[l]
    return x
```

```python
import contextlib
import numpy as np
import ml_dtypes
import concourse.bass as bass
import concourse.mybir as mybir
from concourse.bass_utils import run_bass_kernel_spmd

F32 = mybir.dt.float32
BF16 = mybir.dt.bfloat16
ALU = mybir.AluOpType
AF = mybir.ActivationFunctionType

T = 4096
D = 1024
NT = T // 128
NTG = T // 512
EPS = 1e-6
IN_W = 4360
OFF_QA, OFF_KA, OFF_VA, OFF_QB, OFF_KB, OFF_VB, OFF_FL, OFF_QM = 0, 768, 1536, 2304, 2816, 3328, 3840, 3848
DIL = ((128, 1), (512, 4), (2048, 16))


class R:
    __slots__ = ("name", "lw", "readers")

    def __init__(self, name=""):
        self.name = name
        self.lw = None
        self.readers = []


class _Op:
    __slots__ = ("eng", "fn", "reads", "writes", "dma", "deps", "needs_inc", "cnt", "dsem", "dval", "bar", "nobar")

    def __init__(self, eng, fn, reads, writes, dma, nobar=False):
        self.eng = eng
        self.fn = fn
        self.reads = reads
        self.writes = writes
        self.dma = dma
        self.deps = {}
        self.needs_inc = False
        self.cnt = None
        self.bar = False
        self.nobar = nobar


DSEM_POOLS = {"sp": 30, "pool": 10, "act": 2, "pe": 1, "dve": 1}


class Prog:
    ENGS = ("pe", "act", "dve", "pool", "sp")

    def __init__(self, nc):
        self.nc = nc
        self.ops = []

    def op(self, eng, fn, reads=(), writes=(), dma=False, nobar=False):
        self.ops.append(_Op(eng, fn, list(reads), list(writes), dma, nobar))

    def dma(self, out, in_, reads=(), writes=(), eng="sp", nobar=False, **kw):
        self.op(eng, lambda e: e.dma_start(out=out, in_=in_, **kw), reads, writes, dma=True, nobar=nobar)

    def barrier(self):
        o = _Op(None, None, [], [], False)
        o.bar = True
        self.ops.append(o)

    def finalize(self):
        nc = self.nc
        ops = self.ops
        last_on = {}
        last_dma_on_sem = {}
        pending_bar = {e: None for e in self.ENGS}
        ndma = {e: 0 for e in self.ENGS}
        base = {}
        acc = 0
        for e in self.ENGS:
            base[e] = acc
            acc += DSEM_POOLS[e]
        nsem_total = acc
        for i, op in enumerate(ops):
            if op.bar:
                deps = dict((j, "BAR") for j in last_on.values())
                for j in last_dma_on_sem.values():
                    deps[j] = "BAR"
                for e in self.ENGS:
                    pending_bar[e] = deps
                continue
            deps = {}
            if pending_bar[op.eng] is not None and not op.nobar:
                deps.update(pending_bar[op.eng])
                pending_bar[op.eng] = None
            for r in op.reads:
                if r.lw is not None:
                    deps[r.lw] = "RAW"
            for r in op.writes:
                if r.lw is not None and deps.get(r.lw) != "RAW":
                    deps[r.lw] = "WAW"
                for j in r.readers:
                    deps.setdefault(j, "WAR")
            deps.pop(i, None)
            for r in op.reads:
                r.readers.append(i)
            for r in op.writes:
                r.lw = i
                r.readers = []
            need = {}
            for j, kind in deps.items():
                oj = ops[j]
                if oj.dma:
                    need[j] = kind
                    continue
                if oj.eng == op.eng and not op.dma:
                    if op.eng == "pe" or kind == "BAR":
                        continue
                need[j] = kind
                oj.needs_inc = True
            op.deps = need
            if op.dma:
                k = ndma[op.eng]
                npool = DSEM_POOLS[op.eng]
                op.dsem = base[op.eng] + k % npool
                op.dval = 16 * (k // npool + 1)
                if not op.nobar:
                    last_dma_on_sem[op.dsem] = i
                ndma[op.eng] = k + 1
            elif not op.nobar:
                last_on[op.eng] = i
        cnt = {e: 0 for e in self.ENGS}
        for op in ops:
            if op.bar or op.dma:
                continue
            if op.needs_inc:
                cnt[op.eng] += 1
                op.cnt = cnt[op.eng]
        self.ndma = ndma
        self.counts = cnt
        final_vals = {}
        for op in ops:
            if not op.bar and op.dma:
                final_vals[op.dsem] = max(final_vals.get(op.dsem, 0), op.dval)
        with contextlib.ExitStack() as st:
            esem = {e: st.enter_context(nc.semaphore("s_" + e)) for e in self.ENGS}
            dsem = [st.enter_context(nc.semaphore("d_%d" % k)) for k in range(nsem_total)]
            block = st.enter_context(nc.Block())
            handles = {"pe": block.tensor, "act": block.scalar, "dve": block.vector, "pool": block.gpsimd,
                       "sp": block.sync}

            def make_body(ename):
                def body(eng):
                    waited = {}

                    def wait(sem, key, val):
                        if waited.get(key, 0) >= val:
                            return
                        eng.wait_ge(sem, val)
                        waited[key] = val

                    for op in ops:
                        if op.bar or op.eng != ename:
                            continue
                        for j in op.deps:
                            oj = ops[j]
                            if oj.dma:
                                wait(dsem[oj.dsem], ("d", oj.dsem), oj.dval)
                            else:
                                wait(esem[oj.eng], ("e", oj.eng), oj.cnt)
                        if op.dma:
                            if op.dval > 16:
                                wait(dsem[op.dsem], ("d", op.dsem), op.dval - 16)
                            ins = op.fn(eng)
                            ins.then_inc(dsem[op.dsem], 16)
                        else:
                            ins = op.fn(eng)
                            if op.needs_inc:
                                ins.then_inc(esem[ename], 1)
                    if ename == "sp":
                        for k, v in sorted(final_vals.items()):
                            wait(dsem[k], ("d", k), v)
                return body

            for ename in self.ENGS:
                handles[ename](make_body(ename))


def _consts():
    bf = ml_dtypes.bfloat16
    ident = np.eye(128, dtype=np.float32)
    bones = np.zeros((128, 128), np.float32)
    bones[:64, :64] = 1.0
    bones[64:, 64:] = 1.0
    ones = np.ones((128, 128), np.float32)
    rot = np.zeros((128, 128), np.float32)
    for hb in (0, 64):
        for e in range(8):
            rot[hb + e + 8, hb + e] = -1.0
            rot[hb + e, hb + e + 8] = 1.0
    j = np.arange(128)[:, None]
    i = np.arange(128)[None, :]
    mcur = (j <= i).astype(np.float32)
    mprev = (j >= i).astype(np.float32)
    mask4 = np.concatenate([mcur, mprev, mcur, mprev], axis=1)
    cb = np.concatenate([ident, bones, ones, rot, mask4], axis=1).astype(bf)
    inv_freq = (500000.0 ** (-(np.arange(0, 16, 2, dtype=np.float32) / np.float32(16)))).astype(np.float32)
    ang = np.arange(T, dtype=np.float32)[:, None] * inv_freq[None, :]
    cos, sin = np.cos(ang).astype(np.float32), np.sin(ang).astype(np.float32)
    ctab = np.ones((128, T), np.float32)
    stab = np.zeros((128, T), np.float32)
    for hb in (0, 64):
        for e in range(16):
            ctab[hb + e] = cos[:, e % 8]
            stab[hb + e] = sin[:, e % 8]
    tabs = np.zeros((3, 2, 128, T), np.float32)
    for g, (win, d) in enumerate(DIL):
        nsub = T // d
        pos = np.arange(T)
        tok = (pos % nsub) * d + pos // nsub
        tabs[g, 0] = ctab[:, tok]
        tabs[g, 1] = stab[:, tok]
    return cb, tabs


C_ID, C_BONES, C_ONES, C_ROT, C_MASK = 0, 128, 256, 384, 512


def build(debug=False):
    nc = bass.Bass("TRN2", target_bir_lowering=False)
    P = Prog(nc)
    st = contextlib.ExitStack()

    def MM(out, lhsT, rhs, start, stop, reads, writes):
        P.op("pe", lambda e: e.matmul(out, lhsT=lhsT, rhs=rhs, start=start, stop=stop), reads, writes)

    def TR(out, in_, identity, reads, writes):
        P.op("pe", lambda e: e.transpose(out=out, in_=in_, identity=identity), reads, writes)

    def ACT(out, in_, func, reads, writes, **kw):
        P.op("act", lambda e: e.activation(out=out, in_=in_, func=func, **kw), reads, writes)

    def TT(eng, out, in0, in1, op, reads, writes):
        P.op(eng, lambda e: e.tensor_tensor(out=out, in0=in0, in1=in1, op=op), reads, writes)

    def TS(eng, out, in0, s1, s2, op0, op1, reads, writes):
        if s2 is None:
            P.op(eng, lambda e: e.tensor_scalar(out=out, in0=in0, scalar1=s1, scalar2=None, op0=op0), reads, writes)
        else:
            P.op(eng, lambda e: e.tensor_scalar(out=out, in0=in0, scalar1=s1, scalar2=s2, op0=op0, op1=op1), reads,
                 writes)

    def STT(out, in0, scalar, in1, op0, op1, reads, writes):
        P.op("dve", lambda e: e.scalar_tensor_tensor(out=out, in0=in0, scalar=scalar, in1=in1, op0=op0, op1=op1),
             reads, writes)

    def RECIP(out, in_, reads, writes):
        P.op("dve", lambda e: e.reciprocal(out=out, in_=in_), reads, writes)

    def COPY(eng, out, in_, reads, writes):
        P.op(eng, lambda e: e.tensor_copy(out=out, in_=in_), reads, writes)

    def MEMSET(eng, ap, val, writes):
        P.op(eng, lambda e: e.memset(ap, val), [], writes)

    def skew(n_items, stages, lags, reverse=False):
        order = list(zip(stages, lags))
        if reverse:
            order = order[::-1]
        for step in range(n_items + max(lags)):
            for fn, lag in order:
                i = step - lag
                if 0 <= i < n_items:
                    fn(i)

    def din(name, shape, dt=F32):
        return nc.dram_tensor(name, list(shape), dt, kind="ExternalInput").ap()

    x_d = din("x", [T, D])
    mem_d = din("mem", [256, D])
    g_mix_d = din("g_mix", [D])
    w_in_d = din("w_in", [D, IN_W])
    b_f_d = din("b_f", [8])
    g_qA_d = din("g_qA", [64])
    g_kA_d = din("g_kA", [64])
    g_qB_d = din("g_qB", [64])
    g_kB_d = din("g_kB", [64])
    g_mem_d = din("g_mem", [D])
    w_mkv_d = din("w_mem_kv", [D, 1024])
    g_qM_d = din("g_qM", [128])
    g_kM_d = din("g_kM", [128])
    w_gate_d = din("w_gate", [D, 3072])
    b_gate_d = din("b_gate", [3072])
    w_bra_d = din("w_br_a", [256, D])
    w_brb_d = din("w_br_b", [512, D])
    w_brm_d = din("w_br_m", [512, D])
    w_out_d = din("w_out", [D, D])
    g_mlp_d = din("g_mlp", [D])
    w_up_d = din("w_up", [D, 4096])
    w_down_d = din("w_down", [4096, D])
    cb_d = din("cb", [128, 1024], BF16)
    tabs_d = din("tabs", [3, 2, 128, T])
    out_d = nc.dram_tensor("out", [T, D], F32, kind="ExternalOutput").ap()

    def scratch(name, shape, dt=BF16, dbg=False):
        kind = "ExternalOutput" if (debug and dbg) else "Internal"
        return nc.dram_tensor(name, list(shape), dt, kind=kind).ap()

    ysc = scratch("ysc", [1280, T], dbg=True)
    hsc = scratch("hsc", [128, 8, T])
    augsc = scratch("augsc", [8, 2, 6, T])
    wsc = {
        "in": scratch("wsc_in", [128, 8, IN_W]),
        "mkv": scratch("wsc_mkv", [128, 8, 1024]),
        "gate": scratch("wsc_gate", [8, 128, 8, 384]),
        "br": scratch("wsc_br", [8, 128, 10, 128]),
        "out": scratch("wsc_out", [128, 8, 1024]),
        "up": scratch("wsc_up", [128, 8, 4096]),
        "down": scratch("wsc_down", [128, 32, 1024]),
    }

    def sbt(name, shape, dt=F32):
        return st.enter_context(nc.sbuf_tensor(name, list(shape), dt))

    cb = sbt("cb_sb", [128, 1024], BF16)
    r_cb = R("cb")
    gcols = sbt("gcols", [128, 64])
    r_g = R("gcols")
    ARENA_COLS = 183 * 512
    arena = sbt("arena", [128, ARENA_COLS], BF16)

    class Arena:
        def __init__(self):
            self.top = 0

        def alloc(self, ncols, dt=BF16, parts=128):
            nb = ncols * (2 if dt == BF16 else 4)
            nb = (nb + 63) // 64 * 64
            o = self.top
            self.top += nb // 2
            assert self.top <= ARENA_COLS, ("arena overflow", self.top)
            ap = arena[:, o:o + nb // 2]
            if dt != BF16:
                ap = ap.bitcast(dt)
            return ap[:, 0:ncols]

    A = Arena()
    banks = [st.enter_context(nc.psum_tensor("bank%d" % i, [128, 512], F32)) for i in range(8)]
    rb = [R("bank%d" % i) for i in range(8)]

    ident = cb[:, C_ID:C_ID + 128]
    bones = cb[:, C_BONES:C_BONES + 128]
    ones_bf = cb[:, C_ONES:C_ONES + 128]
    rotm = cb[:, C_ROT:C_ROT + 128]
    mask4 = cb[:, C_MASK:C_MASK + 512]

    P.dma(cb[:], cb_d, writes=[r_cb])

    GC_GMIX, GC_GMLP, GC_GMEM = 0, 8, 16
    GC_QA, GC_KA, GC_QB, GC_KB, GC_QM, GC_KM = 24, 25, 26, 27, 28, 29
    GC_BGATE = 32
    GC_NBF = 56
    GC_RAW = 57
    r_gs = []

    def _rgn():
        r_gs.append(R())
        return r_gs[-1]

    if True:
        P.dma(gcols[:, GC_GMIX:GC_GMIX + 8], g_mix_d.rearrange("(kc p) -> p kc", p=128), writes=[_rgn()], allow_slow_non_contiguous=True)
        P.dma(gcols[:, GC_GMLP:GC_GMLP + 8], g_mlp_d.rearrange("(kc p) -> p kc", p=128), writes=[_rgn()], allow_slow_non_contiguous=True)
        P.dma(gcols[:, GC_GMEM:GC_GMEM + 8], g_mem_d.rearrange("(kc p) -> p kc", p=128), writes=[_rgn()], allow_slow_non_contiguous=True)
        P.dma(gcols[:, GC_BGATE:GC_BGATE + 24], b_gate_d.rearrange("(c p) -> p c", p=128), writes=[_rgn()], allow_slow_non_contiguous=True)
        for col, gd in ((0, g_qA_d), (1, g_kA_d), (2, g_qB_d), (3, g_kB_d)):
            for hb in (0, 64):
                P.dma(gcols[hb:hb + 64, GC_RAW + col:GC_RAW + col + 1], gd.rearrange("(p o) -> p o", o=1),
                      writes=[_rgn()], allow_slow_non_contiguous=True)
        P.dma(gcols[:, GC_RAW + 4:GC_RAW + 5], g_qM_d.rearrange("(p o) -> p o", o=1), writes=[_rgn()], allow_slow_non_contiguous=True)
        P.dma(gcols[:, GC_RAW + 5:GC_RAW + 6], g_kM_d.rearrange("(p o) -> p o", o=1), writes=[_rgn()], allow_slow_non_contiguous=True)
        P.dma(gcols[0:8, GC_NBF:GC_NBF + 1], b_f_d.rearrange("(p o) -> p o", o=1), writes=[_rgn()], allow_slow_non_contiguous=True)

    def gscale(dst, src, s):
        P.op("dve", lambda e: e.tensor_scalar(out=gcols[:, dst:dst + 1], in0=gcols[:, src:src + 1], scalar1=float(s),
                                               scalar2=None, op0=ALU.mult), [r_g] + r_gs, [r_g])
    gscale(GC_QA, GC_RAW + 0, 0.125)
    gscale(GC_KA, GC_RAW + 1, 1.0)
    gscale(GC_QB, GC_RAW + 2, 0.125)
    gscale(GC_KB, GC_RAW + 3, 1.0)
    gscale(GC_QM, GC_RAW + 4, 128.0 ** -0.5)
    gscale(GC_KM, GC_RAW + 5, 1.0)
    P.op("dve", lambda e: e.tensor_scalar(out=gcols[0:8, GC_NBF:GC_NBF + 1], in0=gcols[0:8, GC_NBF:GC_NBF + 1],
                                           scalar1=-1.0, scalar2=None, op0=ALU.mult), [r_g], [r_g])

    Rw = {}
    jobs = []

    def chunks_of(n, cw):
        return [(c, min(c + cw, n)) for c in range(0, n, cw)]

    WCH = {"gate": chunks_of(3072, 512),
           "br": chunks_of(1024, 512), "out": chunks_of(1024, 512), "up": chunks_of(4096, 512),
           "down": chunks_of(1024, 512)}

    def add_jobs(name, src, nkc, gc, src_row0=0, kc0=0):
        for (c0, c1) in WCH[name]:
            for kc in range(nkc):
                r = R("w_%s_%d_%d" % (name, kc0 + kc, c0))
                Rw[(name, kc0 + kc, c0)] = r
                if name == "gate":
                    br_, cc0 = c0 // 1024, (c0 % 1024) // 128
                    dst = wsc[name][cc0:cc0 + 4, :, kc0 + kc, br_ * 128:(br_ + 1) * 128].rearrange("c p n -> p c n")
                elif name == "br":
                    cc0 = c0 // 128
                    dst = wsc[name][cc0:cc0 + 4, :, kc0 + kc, :].rearrange("c p n -> p c n")
                else:
                    dst = wsc[name][:, kc0 + kc, c0:c1]
                jobs.append((src[src_row0 + kc * 128:src_row0 + (kc + 1) * 128, c0:c1],
                             None if gc is None else gc + kc, dst, r, c1 - c0))

    add_jobs("gate", w_gate_d, 8, GC_GMIX)
    add_jobs("br", w_bra_d, 2, None, kc0=0)
    add_jobs("br", w_brb_d, 4, None, kc0=2)
    add_jobs("br", w_brm_d, 4, None, kc0=6)
    add_jobs("out", w_out_d, 8, None)
    add_jobs("up", w_up_d, 8, GC_GMLP)
    add_jobs("down", w_down_d, 32, None)

    NSTG = 3
    stg32 = [sbt("stg32_%d" % i, [128, 1024]) for i in range(NSTG)]
    r_s32 = [R() for _ in range(NSTG)]
    NSTGJ = 4
    jst32 = [sbt("jst32_%d" % i, [128, 512]) for i in range(NSTGJ)]
    jst16 = [sbt("jst16_%d" % i, [128, 512], BF16) for i in range(2)]
    r_j32 = [R() for _ in range(NSTGJ)]
    r_j16 = [R() for _ in range(2)]
    job_state = {"i": 0, "s": 0}

    def _job_load(i):
        src, gc, dst, r, cw = jobs[i]
        s = i % NSTGJ
        P.dma(jst32[s][:, 0:cw], src, writes=[r_j32[s]], eng="pool", nobar=True)

    def pump(n):
        for _ in range(n):
            i = job_state["i"]
            if i >= len(jobs):
                return
            if i == 0:
                _job_load(0)
                if len(jobs) > 1:
                    _job_load(1)
            if i + 2 < len(jobs):
                _job_load(i + 2)
            job_state["i"] = i + 1
            src, gc, dst, r, cw = jobs[i]
            s = i % NSTGJ
            a32, a16 = jst32[s][:, 0:cw], jst16[i % 2][:, 0:cw]
            if gc is None:
                P.op("pool", lambda e, a16=a16, a32=a32: e.tensor_copy(out=a16, in_=a32), [r_j32[s]], [r_j16[i % 2]],
                     nobar=True)
            else:
                P.op("pool", lambda e, a16=a16, a32=a32, gc=gc, cw=cw: e.tensor_tensor(
                    out=a16, in0=a32, in1=gcols[:, gc:gc + 1].to_broadcast([128, cw]), op=ALU.mult),
                    [r_j32[s], r_g], [r_j16[i % 2]], nobar=True)
            if len(dst.shape) == 3:
                a16 = a16.rearrange("p (c n) -> p c n", n=128)
            P.dma(dst, a16, reads=[r_j16[i % 2]], writes=[r], eng="pool", nobar=True)

    def wdirect(dst, src, gc, ncols, nk=8, **kw):
        s = job_state["s"] % NSTG
        job_state["s"] += 1
        a32 = stg32[s][:, 0:nk * ncols].rearrange("p (k c) -> p k c", k=nk)
        P.dma(a32, src.rearrange("(k p) c -> p k c", p=128), writes=[r_s32[s]], **kw)
        return a32, s

    def wdirect_cast(dst, a32, s, gc, ncols, writes, nk=8):
        if gc is None:
            P.op("dve", lambda e: e.tensor_copy(out=dst, in_=a32), [r_s32[s]], writes)
        else:
            P.op("dve", lambda e: e.tensor_tensor(
                out=dst, in0=a32, in1=gcols[:, gc:gc + nk].unsqueeze(2).to_broadcast([128, nk, ncols]), op=ALU.mult),
                [r_s32[s], r_g], writes)

    def wload_in(dst, c0, ncols, writes, **kw):
        a32, s = wdirect(dst, w_in_d[:, c0:c0 + ncols], GC_GMIX, ncols, **kw)
        wdirect_cast(dst, a32, s, GC_GMIX, ncols, writes)

    def wload_in_dma(c0, ncols):
        return wdirect(None, w_in_d[:, c0:c0 + ncols], GC_GMIX, ncols)

    def wload_in_cast(dst, tok, ncols, writes):
        wdirect_cast(dst, tok[0], tok[1], GC_GMIX, ncols, writes)

    def load_w(dst, name, kc0, kc1, c0, c1):
        reads = []
        for (a, b) in WCH[name]:
            if a < c1 and b > c0:
                for kc in range(kc0, kc1):
                    reads.append(Rw[(name, kc, a)])
        if name in ("gate", "br"):
            return None, reads
        return wsc[name][:, kc0:kc1, c0:c1], reads

    hT = A.alloc(8 * T).rearrange("p (k t) -> p k t", k=8)
    r_hT = [R("hT%d" % i) for i in range(NT)]
    mark_h = A.top

    def hT_reads(t0, t1):
        return [r_hT[i] for i in range(t0 // 128, (t1 - 1) // 128 + 1)]

    def rms_rows_to_T(src, src_reads, ss_col, rs_col, stats, r_stats, xn, r_xn, bank_i, dstT, dst_writes, junk,
                      r_junk):
        ACT(junk, src, AF.Square, src_reads + [r_junk], [r_junk, r_stats], accum_out=stats[:, ss_col:ss_col + 1])
        ACT(stats[:, rs_col:rs_col + 1], stats[:, ss_col:ss_col + 1], AF.Sqrt, [r_stats], [r_stats], scale=1.0 / D,
            bias=stats[:, 0:1])
        RECIP(stats[:, rs_col:rs_col + 1], stats[:, rs_col:rs_col + 1], [r_stats], [r_stats])
        TS("dve", xn, src, stats[:, rs_col:rs_col + 1], None, ALU.mult, None, src_reads + [r_stats], [r_xn])
        rows_to_T(xn, r_xn, bank_i, dstT, dst_writes)

    def rms_part1(src, src_reads, ss_col, rs_col, stats, r_stats, xn, r_xn, junk, r_junk):
        ACT(junk, src, AF.Square, src_reads + [r_junk], [r_junk, r_stats], accum_out=stats[:, ss_col:ss_col + 1])
        ACT(stats[:, rs_col:rs_col + 1], stats[:, ss_col:ss_col + 1], AF.Ln, [r_stats], [r_stats], scale=1.0 / D,
            bias=stats[:, 0:1])
        ACT(stats[:, rs_col:rs_col + 1], stats[:, rs_col:rs_col + 1], AF.Exp, [r_stats], [r_stats], scale=-0.5)
        TS("dve", xn, src, stats[:, rs_col:rs_col + 1], None, ALU.mult, None, src_reads + [r_stats], [r_xn])

    def rows_to_T(xn, r_xn, bank_i, dstT, dst_writes):
        bk = banks[bank_i][:].bitcast(BF16)
        for k in range(8):
            TR(bk[:, k * 128:(k + 1) * 128], xn[:, k * 128:(k + 1) * 128], ident, [r_xn, r_cb], [rb[bank_i]])
        COPY("dve", dstT, bk[:, 0:1024].rearrange("p (k t) -> p k t", k=8), [rb[bank_i]], dst_writes)

    statsA = A.alloc(2 * NT + 4, F32)
    r_statsA = R("statsA")
    MEMSET("dve", statsA[:, 0:1], EPS, [r_statsA])
    NXB = 6
    xts = [A.alloc(D, F32) for _ in range(NXB)]
    r_xts = [R() for _ in range(NXB)]
    r_xh = [[R(), R()] for _ in range(NXB)]
    xns = [A.alloc(D) for _ in range(2)]
    r_xns = [R() for _ in range(2)]
    junkA = A.alloc(D)
    r_junkA = R()
    pass
    wfl = A.alloc(8 * 8).rearrange("p (k c) -> p k c", k=8)
    r_wfl = R()
    spb = [A.alloc(512, F32) for _ in range(2)]
    csb = [A.alloc(512, F32) for _ in range(2)]
    r1b = [A.alloc(512, F32) for _ in range(2)]
    posb = [A.alloc(3 * 512).rearrange("p (j t) -> p j t", j=3) for _ in range(2)]
    negb = [A.alloc(3 * 512).rearrange("p (j t) -> p j t", j=3) for _ in range(2)]
    r_spb, r_csb, r_r1b, r_posb, r_negb = [R(), R()], [R(), R()], [R(), R()], [R(), R()], [R(), R()]
    onesr = A.alloc(T)
    r_onesr = R()
    ra_ = []

    def p2_setup():
        wload_in(wfl, OFF_FL, 8, [r_wfl], allow_slow_non_contiguous=True)
        MEMSET("dve", onesr[0:8, :], 1.0, [r_onesr])
        for j in range(3):
            ra_.append(R())
            P.dma(augsc[:, 0, 3 + j, :], onesr[0:8, :], reads=[r_onesr], writes=[ra_[-1]])
            ra_.append(R())
            P.dma(augsc[:, 1, j, :], onesr[0:8, :], reads=[r_onesr], writes=[ra_[-1]])

    def p2_tg(tg):
        b_ = 2 + tg % 2
        k = tg % 2
        sp_, cs_, r1_, pos_, neg_ = spb[k][0:8, :], csb[k][0:8, :], r1b[k][0:8, :], posb[k][0:8, :, :], negb[k][0:8, :, :]
        for kc in range(8):
            MM(banks[b_][0:8, :], wfl[:, kc, :], hT[:, kc, tg * 512:(tg + 1) * 512], kc == 0, kc == 7,
               [r_wfl] + hT_reads(tg * 512, tg * 512 + 512), [rb[b_]])
        ACT(sp_, banks[b_][0:8, :], AF.Exp, [rb[b_], r_g], [r_spb[k]], scale=-1.0, bias=gcols[0:8, GC_NBF:GC_NBF + 1])
        ACT(sp_, sp_, AF.Ln, [r_spb[k]], [r_spb[k]], scale=1.0, bias=1.0)
        init = 0.0 if tg == 0 else csb[1 - k][0:8, 511:512]
        rd = [r_spb[k]] + ([] if tg == 0 else [r_csb[1 - k]])
        P.op("dve", lambda e: e.tensor_tensor_scan(out=cs_, data0=sp_, data1=sp_, initial=init, op0=ALU.add,
                                                    op1=ALU.max), rd, [r_csb[k]])
        COPY("dve", pos_[:, 0, :], cs_, [r_csb[k]], [r_posb[k]])
        TT("dve", r1_, cs_, pos_[:, 0, :], ALU.subtract, [r_csb[k], r_posb[k]], [r_r1b[k]])
        COPY("dve", pos_[:, 1, :], r1_, [r_r1b[k]], [r_posb[k]])
        TT("dve", sp_, r1_, pos_[:, 1, :], ALU.subtract, [r_r1b[k], r_posb[k]], [r_spb[k]])
        COPY("dve", pos_[:, 2, :], sp_, [r_spb[k]], [r_posb[k]])
        TS("dve", neg_, pos_, -1.0, None, ALU.mult, None, [r_posb[k]], [r_negb[k]])
        ts_ = slice(tg * 512, (tg + 1) * 512)
        ra_.append(R())
        P.dma(augsc[:, 0, 0:3, ts_], neg_, reads=[r_negb[k]], writes=[ra_[-1]])
        ra_.append(R())
        P.dma(augsc[:, 1, 3:6, ts_], pos_, reads=[r_posb[k]], writes=[ra_[-1]])

    def a3(i):
        if i % 4 == 3:
            p2_tg(i // 4)

    def a0(i):
        for hf in range(2):
            P.dma(xts[i % NXB][:, hf * 512:(hf + 1) * 512], x_d[i * 128:(i + 1) * 128, hf * 512:(hf + 1) * 512],
                  writes=[r_xh[i % NXB][hf]])
        if i == 4:
            p2_setup()

    def a1(i):
        rms_part1(xts[i % NXB], r_xh[i % NXB], 1 + 2 * i, 2 + 2 * i, statsA, r_statsA, xns[i % 2], r_xns[i % 2], junkA,
                  r_junkA)

    def a2(i):
        rows_to_T(xns[i % 2], r_xns[i % 2], i % 2, hT[:, :, i * 128:(i + 1) * 128], [r_hT[i]])

    skew(NT, [a0, a1, a2, a3], [0, 4, 5, 6])
    r_hsc = [R("hsc%d" % k) for k in range(8)]
    hsc_todo = list(range(8))

    def spill_hsc(n):
        for _ in range(n):
            if hsc_todo:
                k = hsc_todo.pop(0)
                P.dma(hsc[:, k, :], hT[:, k, :], reads=r_hT, writes=[r_hsc[k]])
    P.barrier()
    A.top = mark_h

    def proj_fm(bank_i, wt, wreads, tg):
        for kc in range(8):
            MM(banks[bank_i][:, :], wt[:, kc, :], hT[:, kc, tg * 512:(tg + 1) * 512], kc == 0, kc == 7,
               wreads + hT_reads(tg * 512, tg * 512 + 512), [rb[bank_i]])

    def head_norm_a(qbank, ssbank, onesmat, tm, ncol=512):
        ACT(tm["sq"][:, 0:ncol], banks[qbank][:, 0:ncol], AF.Square, [rb[qbank]], [tm["r_sq"]])
        MM(banks[ssbank][:, 0:ncol], onesmat, tm["sq"][:, 0:ncol], True, True, [tm["r_sq"], r_cb], [rb[ssbank]])

    def head_norm_b(ssbank, inv_n, tm, ncol=512):
        ACT(tm["ms"][:, 0:ncol], banks[ssbank][:, 0:ncol], AF.Ln, [rb[ssbank]], [tm["r_ms"]], scale=float(inv_n), bias=EPS)
        ACT(tm["rstd"][:, 0:ncol], tm["ms"][:, 0:ncol], AF.Exp, [tm["r_ms"]], [tm["r_rstd"]], scale=-0.5)

    def head_norm(qbank, ssbank, onesmat, inv_n, tm, ncol=512):
        head_norm_a(qbank, ssbank, onesmat, tm, ncol)
        head_norm_b(ssbank, inv_n, tm, ncol)

    def alloc_norm_tmps(n=2, rope=False):
        t = []
        for _ in range(n):
            dct = dict(sq=A.alloc(512), r_sq=R(), ms=A.alloc(512, F32), r_ms=R(), rstd=A.alloc(512, F32),
                       r_rstd=R())
            dct.update(qn=A.alloc(512), r_qn=R())
            if rope:
                dct.update(t1=A.alloc(512, F32), r_t1=R(), t2=A.alloc(512, F32), r_t2=R())
            t.append(dct)
        return t

    mark_p = A.top

    _rysc = {}

    def ry(key):
        if key not in _rysc:
            _rysc[key] = R("ysc" + str(key))
        return _rysc[key]

    def walloc():
        return A.alloc(8 * 128).rearrange("p (k c) -> p k c", k=8)


    def phase_dil():
        ctabs = [A.alloc(512, F32) for _ in range(2)]
        stabs = [A.alloc(512, F32) for _ in range(2)]
        r_tabs = [R(), R()]
        tmps = alloc_norm_tmps(2, rope=True)
        QT = A.alloc(T)
        KT = A.alloc(T)
        r_QT = [R() for _ in range(NTG)]
        r_KT = [R() for _ in range(NTG)]
        Vt = A.alloc(32 * 256).rearrange("p (b h c) -> p b h c", b=32, h=2)
        r_Vt = [R() for _ in range(8)]
        MEMSET("dve", Vt[:, :, :, 64:128], 1.0, r_Vt)
        acc = [A.alloc(T, F32) for _ in range(2)]
        r_acc = [[R() for _ in range(NTG)] for _ in range(2)]
        wq, wk, wv = [walloc(), walloc()], [walloc(), walloc()], [walloc(), walloc()]
        r_wq, r_wk, r_wv = [R(), R()], [R(), R()], [R(), R()]
        PT = [A.alloc(512) for _ in range(4)]
        r_PT = [R() for _ in range(4)]
        recs = [A.alloc(1024, F32)] * 2
        r_recs = [R()] * 2
        yt = [A.alloc(1024) for _ in range(2)]
        r_yt = [R(), R()]
        iters = [(hp, g) for hp in range(2) for g in range(3)]

        wtok = {}

        def issue_w_dma(k):
            hp_, g_ = iters[k]
            c0_ = g_ * 256 + hp_ * 128
            wtok[k] = [wload_in_dma(off + c0_, 128) for off in (OFF_QA, OFF_KA, OFF_VA)]

        def issue_w_cast(k):
            wb_ = k % 2
            for tok, (wt_, rw_) in zip(wtok[k], ((wq[wb_], r_wq[wb_]), (wk[wb_], r_wk[wb_]), (wv[wb_], r_wv[wb_]))):
                wload_in_cast(wt_, tok, 128, [rw_])

        issue_w_dma(0)
        issue_w_cast(0)
        pending_norm = []
        late_units = []

        def mk_unit(hp, hh, c):
            def unit():
                sl = slice(c * 1024, (c + 1) * 1024)
                ra = [r_acc[hh][2 * c], r_acc[hh][2 * c + 1]]
                rc, r_rc = recs[c % 2], r_recs[c % 2]
                ACT(rc[0:64, :], acc[hh][64:128, sl], AF.Ln, ra, [r_rc])
                ACT(rc[0:64, :], rc[0:64, :], AF.Exp, [r_rc], [r_rc], scale=-1.0)
                yb = yt[c % 2]
                TT("dve", yb[0:64, :], acc[hh][0:64, sl], rc[0:64, :], ALU.mult, ra + [r_rc], [r_yt[c % 2]])
                row0 = (hp * 2 + hh) * 64
                P.dma(ysc[row0:row0 + 64, sl], yb[0:64, :], reads=[r_yt[c % 2]],
                      writes=[ry(("a", hp * 2 + hh, 2 * c)), ry(("a", hp * 2 + hh, 2 * c + 1))])
            return unit
        for it, (hp, g) in enumerate(iters):
            if True:
                win, d = DIL[g]
                nsub = T // d
                nb = nsub // 128
                wb = it % 2
                if it + 1 < len(iters):
                    issue_w_dma(it + 1)
                spill_hsc(2)
                pump(0)
                chains = [(wq[wb], r_wq[wb], GC_QA, QT, r_QT), (wk[wb], r_wk[wb], GC_KA, KT, r_KT)]

                def st0(i):
                    wt, rw, gc, dst, r_dst = chains[i // NTG]
                    proj_fm(i % 3, wt, [rw], i % NTG)

                def st1a(i):
                    head_norm_a(i % 3, 3 + i % 2, bones, tmps[i % 2])

                def st1(i):
                    wt, rw, gc, dst, r_dst = chains[i // NTG]
                    tm = tmps[i % 2]
                    qb_ = i % 3
                    tg = i % NTG
                    P.dma(ctabs[i % 2], tabs_d[0, 0, :, tg * 512:(tg + 1) * 512], writes=[r_tabs[i % 2]])
                    P.dma(stabs[i % 2], tabs_d[0, 1, :, tg * 512:(tg + 1) * 512], writes=[r_tabs[i % 2]])
                    head_norm_b(3 + i % 2, 1.0 / 64, tm)
                    STT(tm["qn"], banks[qb_][:, :], gcols[:, gc:gc + 1], tm["rstd"], ALU.mult, ALU.mult,
                        [rb[qb_], tm["r_rstd"], r_g], [tm["r_qn"]])

                def st2(i):
                    wt, rw, gc, dst, r_dst = chains[i // NTG]
                    tg = i % NTG
                    tm = tmps[i % 2]
                    rb_ = 5 + i % 2
                    ctab, stab, r_tab = ctabs[i % 2], stabs[i % 2], r_tabs[i % 2]
                    MM(banks[rb_][:, :], rotm, tm["qn"], True, True, [tm["r_qn"], r_cb], [rb[rb_]])
                    TT("dve", tm["t1"], banks[rb_][:, :], stab, ALU.mult, [rb[rb_], r_tab], [tm["r_t1"]])
                    TT("dve", tm["t2"], tm["qn"], ctab, ALU.mult, [tm["r_qn"], r_tab], [tm["r_t2"]])
                    n0 = tg * 512 // d
                    dv = dst.rearrange("p (r n) -> p r n", r=d)[:, :, n0:n0 + 512 // d]
                    wr = sorted(set((r * nsub + n0) // 512 for r in range(d)))
                    TT("pool", dv, tm["t1"].rearrange("p (n r) -> p r n", r=d),
                       tm["t2"].rearrange("p (n r) -> p r n", r=d), ALU.add, [tm["r_t1"], tm["r_t2"]],
                       [r_dst[w] for w in wr])

                def sv(i):
                    if i % 2:
                        return
                    B4 = i // 2
                    bk = 7
                    for q in range(4):
                        B = B4 * 4 + q
                        r, b = B // nb, B % nb
                        t0 = r + d * 128 * b
                        t1 = t0 + d * 127 + 1
                        for kc in range(8):
                            MM(banks[bk][:, q * 128:(q + 1) * 128], hT[:, kc, t0:t1:d], wv[wb][:, kc, :], kc == 0,
                               kc == 7, [r_wv[wb]] + hT_reads(t0, t1), [rb[bk]])
                    ACT(Vt[:, B4 * 4:(B4 + 1) * 4, :, 0:64],
                        banks[bk][:, :].rearrange("p (b h c) -> p b h c", b=4, h=2), AF.Copy, [rb[bk]], [r_Vt[B4]])

                def sn(i):
                    if pending_norm:
                        pending_norm.pop(0)()

                skew(2 * NTG, [st0, st1a, st1, sn, st2, sv], [0, 1, 2, 2, 3, 1])
                if it + 1 < len(iters):
                    issue_w_cast(it + 1)
                for hh in range(2):
                    hb = hh * 64

                    def emit_S(pi):
                        sbk = pi % 4
                        for s_, B in enumerate((2 * pi, 2 * pi + 1)):
                            n = 256 if B < 31 else 128
                            MM(banks[sbk][:, s_ * 256:s_ * 256 + n], KT[hb:hb + 64, B * 128:(B + 1) * 128],
                               QT[hb:hb + 64, B * 128:B * 128 + n], True, True,
                               [r_KT[B // 4], r_QT[B // 4], r_QT[(B * 128 + n - 1) // 512]], [rb[sbk]])
                        wdt = 512 if (2 * pi + 1) < 31 else 384
                        ACT(PT[sbk][:, 0:wdt], banks[sbk][:, 0:wdt], AF.Exp, [rb[sbk]], [r_PT[sbk]])
                        TT("dve", PT[sbk][:, 0:wdt], PT[sbk][:, 0:wdt], mask4[:, 0:wdt], ALU.mult, [r_PT[sbk], r_cb],
                           [r_PT[sbk]])

                    def emit_PV(Bq):
                        ob = 4 + (Bq // 4) % 4
                        col = (Bq % 4) * 128
                        terms = []
                        if (Bq % nb) != 0:
                            Bk = Bq - 1
                            terms.append((Bk, (Bk // 2) % 4, (Bk % 2) * 256 + 128))
                        terms.append((Bq, (Bq // 2) % 4, (Bq % 2) * 256))
                        for ti, (Bk, pt, pc) in enumerate(terms):
                            MM(banks[ob][:, col:col + 128], Vt[:, Bk, hh, :], PT[pt][:, pc:pc + 128], ti == 0,
                               ti == len(terms) - 1, [r_Vt[Bk // 4], r_PT[pt]], [rb[ob]])

                    def emit_evac(q4):
                        ob = 4 + q4 % 4
                        Av = acc[hh].rearrange("p (n r) -> p r n", r=d)
                        p0 = 512 * q4
                        if nsub >= 512:
                            r_, n0 = p0 // nsub, p0 % nsub
                            av = Av[:, r_, n0:n0 + 512]
                            src_ = banks[ob][:, :]
                            toks = [r_ + d * n0, r_ + d * (n0 + 511)]
                        else:
                            rr = 512 // nsub
                            r_ = p0 // nsub
                            av = Av[:, r_:r_ + rr, :]
                            src_ = banks[ob][:, :].rearrange("p (r n) -> p r n", r=rr)
                            toks = [r_, r_ + rr - 1 + d * (nsub - 1)]
                        ra = [r_acc[hh][w] for w in range(toks[0] // 512, toks[1] // 512 + 1)]
                        if g == 0:
                            ACT(av, src_, AF.Copy, [rb[ob]], ra)
                        else:
                            TT("dve", av, src_, av, ALU.add, [rb[ob]] + ra, ra)

                    emit_S(0)
                    emit_S(1)
                    for pi in range(16):
                        if pi + 2 < 16:
                            emit_S(pi + 2)
                        for Bq in (2 * pi, 2 * pi + 1):
                            emit_PV(Bq)
                            if Bq % 4 == 3:
                                emit_evac(Bq // 4)
                        if late_units and pi % 3 == 2:
                            late_units.pop(0)()
                    while late_units:
                        late_units.pop(0)()
                    if g == 2 and hh == 0:
                        for c in range(4):
                            late_units.append(mk_unit(hp, 0, c))
                if g == 2:
                    for c in range(4):
                        pending_norm.append(mk_unit(hp, 1, c))
        while pending_norm:
            pending_norm.pop(0)()

    phase_dil()
    P.barrier()
    A.top = mark_p

    def phase_fox():
        tmps = alloc_norm_tmps(2)
        QK = {}
        for nm in ("QA", "QB", "KA", "KB"):
            QK[nm] = (A.alloc(T), [R() for _ in range(NTG)], R())
        Vt = A.alloc(32 * 256).rearrange("p (b h c) -> p b h c", b=32, h=2)
        r_Vt = [R() for _ in range(8)]
        MEMSET("dve", Vt[:, :, :, 64:128], 1.0, r_Vt)
        wq, wk, wv = [walloc(), walloc()], [walloc(), walloc()], [walloc(), walloc()]
        r_wq, r_wk, r_wv = [R(), R()], [R(), R()], [R(), R()]
        NPT = 6
        PT = [A.alloc(512) for _ in range(NPT)]
        r_PT = [R() for _ in range(NPT)]
        rec = A.alloc(512, F32)
        r_rec = R()
        yt = [A.alloc(512) for _ in range(2)]
        r_yt = [R(), R()]
        def issue_w(hp_):
            wb_ = hp_ % 2
            wload_in(wq[wb_], OFF_QB + hp_ * 128, 128, [r_wq[wb_]])
            wload_in(wk[wb_], OFF_KB + hp_ * 128, 128, [r_wk[wb_]])
            wload_in(wv[wb_], OFF_VB + hp_ * 128, 128, [r_wv[wb_]])

        issue_w(0)
        for hp in range(4):
            wb = hp % 2
            for hh, (qn_, kn_) in enumerate((("QA", "KA"), ("QB", "KB"))):
                h = hp * 2 + hh
                P.dma(QK[qn_][0][64:70, :], augsc[h, 0, :, :], reads=ra_, writes=[QK[qn_][2]])
                P.dma(QK[kn_][0][64:70, :], augsc[h, 1, :, :], reads=ra_, writes=[QK[kn_][2]])
            chains = [(wq[wb], r_wq[wb], GC_QB, "QA", "QB"), (wk[wb], r_wk[wb], GC_KB, "KA", "KB")]

            def st0(i):
                wt, rw, gc, nA, nB = chains[i // NTG]
                proj_fm(i % 3, wt, [rw], i % NTG)

            def st1a(i):
                head_norm_a(i % 3, 3 + i % 2, bones, tmps[i % 2])

            def st2(i):
                wt, rw, gc, nA, nB = chains[i // NTG]
                tg = i % NTG
                COPY("dve", QK[nB][0][0:64, tg * 512:(tg + 1) * 512], tmps[i % 2]["qn"][64:128, :],
                     [tmps[i % 2]["r_qn"]], [QK[nB][1][tg]])

            def st1(i):
                wt, rw, gc, nA, nB = chains[i // NTG]
                tg = i % NTG
                tm = tmps[i % 2]
                qb_ = i % 3
                head_norm_b(3 + i % 2, 1.0 / 64, tm)
                sl = slice(tg * 512, (tg + 1) * 512)
                STT(QK[nA][0][0:64, sl], banks[qb_][0:64, :], gcols[0:64, gc:gc + 1], tm["rstd"][0:64, :], ALU.mult,
                    ALU.mult, [rb[qb_], tm["r_rstd"], r_g], [QK[nA][1][tg]])
                STT(tm["qn"][64:128, :], banks[qb_][64:128, :], gcols[64:128, gc:gc + 1], tm["rstd"][64:128, :],
                    ALU.mult, ALU.mult, [rb[qb_], tm["r_rstd"], r_g], [tm["r_qn"]])

            skew(2 * NTG, [st0, st1a, st1, st2], [0, 1, 2, 3])
            if hp + 1 < 4:
                issue_w(hp + 1)
            for B4 in range(8):
                bk = 6 + B4 % 2
                for q in range(4):
                    B = B4 * 4 + q
                    for kc in range(8):
                        MM(banks[bk][:, q * 128:(q + 1) * 128], hT[:, kc, B * 128:(B + 1) * 128], wv[wb][:, kc, :],
                           kc == 0, kc == 7, [r_wv[wb], r_hT[B]], [rb[bk]])
                ACT(Vt[:, B4 * 4:(B4 + 1) * 4, :, 0:64], banks[bk][:, :].rearrange("p (b h c) -> p b h c", b=4, h=2),
                    AF.Copy, [rb[bk]], [r_Vt[B4]])
            for hh, (qn_, kn_) in enumerate((("QA", "KA"), ("QB", "KB"))):
                h = hp * 2 + hh
                Qt, rQ, rQa = QK[qn_]
                Kt, rK, rKa = QK[kn_]
                blocks = []
                for tg in range(NTG):
                    for kb in range(4 * tg + 4):
                        blocks.append((tg, kb))
                nblk = len(blocks)

                def emit_S(bi):
                    tg, kb = blocks[bi]
                    j = max(0, kb - 4 * tg)
                    n = 512 - 128 * j
                    q0 = tg * 512 + 128 * j
                    sbk = bi % NPT
                    MM(banks[sbk][:, 0:n], Kt[0:70, kb * 128:(kb + 1) * 128], Qt[0:70, q0:q0 + n], True, True,
                       [rK[kb // 4], rKa, rQ[tg], rQa], [rb[sbk]])
                    ACT(PT[sbk][:, 0:n], banks[sbk][:, 0:n], AF.Exp, [rb[sbk]], [r_PT[sbk]])
                    if kb >= 4 * tg:
                        TT("dve", PT[sbk][:, 0:128], PT[sbk][:, 0:128], mask4[:, 0:128], ALU.mult, [r_PT[sbk], r_cb],
                           [r_PT[sbk]])

                def emit_PV(bi):
                    tg, kb = blocks[bi]
                    j = max(0, kb - 4 * tg)
                    n = 512 - 128 * j
                    sbk = bi % NPT
                    ob = 6 + tg % 2
                    last = (kb == 4 * tg + 3)
                    MM(banks[ob][:, 128 * j:512], Vt[:, kb, hh, :], PT[sbk][:, 0:n], kb == 0, last,
                       [r_Vt[kb // 4], r_PT[sbk]], [rb[ob]])
                    if last:
                        RECIP(rec[0:64, :], banks[ob][64:128, :], [rb[ob]], [r_rec])
                        yb = yt[tg % 2]
                        TT("dve", yb[0:64, :], banks[ob][0:64, :], rec[0:64, :], ALU.mult, [rb[ob], r_rec],
                           [r_yt[tg % 2]])
                        P.dma(ysc[256 + h * 64:256 + (h + 1) * 64, tg * 512:(tg + 1) * 512], yb[0:64, :],
                              reads=[r_yt[tg % 2]], writes=[ry(("b", h, tg))])

                LA = 4
                for bi in range(min(LA, nblk)):
                    emit_S(bi)
                for bi in range(nblk):
                    if bi + LA < nblk:
                        emit_S(bi + LA)
                    emit_PV(bi)
                pump(24)

    phase_fox()
    P.barrier()
    A.top = mark_p

    def phase_mem():
        tmps = alloc_norm_tmps(2)
        statsM = A.alloc(8, F32)
        r_statsM = R()
        MEMSET("dve", statsM[:, 0:1], EPS, [r_statsM])
        memT = A.alloc(8 * 256).rearrange("p (k t) -> p k t", k=8)
        r_memT = [R(), R()]
        mt = [A.alloc(D, F32) for _ in range(2)]
        r_mt = [R(), R()]
        mn = [A.alloc(D) for _ in range(2)]
        r_mn = [R(), R()]
        junk = A.alloc(D)
        r_junk = R()
        for i in range(2):
            P.dma(mt[i], mem_d[i * 128:(i + 1) * 128, :], writes=[r_mt[i]])
            rms_rows_to_T(mt[i], [r_mt[i]], 1 + 2 * i, 2 + 2 * i, statsM, r_statsM, mn[i], r_mn[i], 6,
                          memT[:, :, i * 128:(i + 1) * 128], [r_memT[i]], junk, r_junk)
        wkv = A.alloc(8 * 1024).rearrange("p (k c) -> p k c", k=8)
        r_wkv = R()
        for cc in range(8):
            a32_, s_ = wdirect(None, w_mkv_d[:, cc * 128:(cc + 1) * 128], GC_GMEM, 128)
            wdirect_cast(wkv[:, :, cc * 128:(cc + 1) * 128], a32_, s_, GC_GMEM, 128, [r_wkv])
        KmT = A.alloc(4 * 256).rearrange("p (h t) -> p h t", h=4)
        r_KmT = [R() for _ in range(4)]
        Vm = A.alloc(2 * 512).rearrange("p (b c) -> p b c", b=2)
        r_Vm = R()
        for h in range(4):
            tm = tmps[h % 2]
            qb_, sb_ = h % 2, 2 + h % 2
            for kc in range(8):
                MM(banks[qb_][:, 0:256], wkv[:, kc, h * 128:(h + 1) * 128], memT[:, kc, :], kc == 0, kc == 7,
                   [r_wkv] + r_memT, [rb[qb_]])
            head_norm(qb_, sb_, ones_bf, 1.0 / 128, tm, ncol=256)
            STT(KmT[:, h, :], banks[qb_][:, 0:256], gcols[:, GC_KM:GC_KM + 1], tm["rstd"][:, 0:256], ALU.mult, ALU.mult,
                [rb[qb_], tm["r_rstd"], r_g], [r_KmT[h]])
        for kb in range(2):
            bk = 6 + kb
            for kc in range(8):
                MM(banks[bk][:, :], memT[:, kc, kb * 128:(kb + 1) * 128], wkv[:, kc, 512:1024], kc == 0, kc == 7,
                   [r_wkv, r_memT[kb]], [rb[bk]])
            ACT(Vm[:, kb, :], banks[bk][:, :], AF.Copy, [rb[bk]], [r_Vm])
        wqm = [walloc(), walloc()]
        r_wqm = [R(), R()]
        PT = [A.alloc(512) for _ in range(4)]
        r_PT = [R() for _ in range(4)]
        rec = A.alloc(512, F32)
        r_rec = R()
        yt = [A.alloc(512) for _ in range(2)]
        r_yt = [R(), R()]
        QmT = [A.alloc(512) for _ in range(2)]
        r_QmT = [R(), R()]
        wqm_all = [walloc(), walloc()]
        r_wqm_all = [R(), R()]
        for h in range(2):
            wload_in(wqm[h], OFF_QM + h * 128, 128, [r_wqm[h]])
        for h in range(2):
            wload_in(wqm_all[h], OFF_QM + (2 + h) * 128, 128, [r_wqm_all[h]])
        wts = [(wqm[0], r_wqm[0]), (wqm[1], r_wqm[1]), (wqm_all[0], r_wqm_all[0]), (wqm_all[1], r_wqm_all[1])]
        n_it = 4 * NTG

        def m0(i):
            h, tg = i // NTG, i % NTG
            proj_fm(i % 3, wts[h][0], [wts[h][1]], tg)

        def m1a(i):
            head_norm_a(i % 3, 3, ones_bf, tmps[i % 2])

        def m1b(i):
            tm = tmps[i % 2]
            head_norm_b(3, 1.0 / 128, tm)
            STT(QmT[i % 2], banks[i % 3][:, :], gcols[:, GC_QM:GC_QM + 1], tm["rstd"], ALU.mult, ALU.mult,
                [rb[i % 3], tm["r_rstd"], r_g], [r_QmT[i % 2]])

        def m2(i):
            h = i // NTG
            for kb in range(2):
                sbk = 4 + kb
                pt = (i * 2 + kb) % 4
                MM(banks[sbk][:, :], KmT[:, h, kb * 128:(kb + 1) * 128], QmT[i % 2], True, True,
                   [r_KmT[h], r_QmT[i % 2]], [rb[sbk]])
                ACT(PT[pt], banks[sbk][:, :], AF.Exp, [rb[sbk]], [r_PT[pt]])

        def m3(i):
            h, tg = i // NTG, i % NTG
            for kb in range(2):
                pt = (i * 2 + kb) % 4
                MM(banks[6][:, :], Vm[:, kb, h * 128:(h + 1) * 128], PT[pt], kb == 0, kb == 1, [r_Vm, r_PT[pt]],
                   [rb[6]])
            for kb in range(2):
                pt = (i * 2 + kb) % 4
                MM(banks[7][:, :], ones_bf, PT[pt], kb == 0, kb == 1, [r_cb, r_PT[pt]], [rb[7]])

        def m3b(i):
            h, tg = i // NTG, i % NTG
            if i % 3 != 2:
                ACT(rec, banks[7][:, :], AF.Ln, [rb[7]], [r_rec])
                ACT(rec, rec, AF.Exp, [r_rec], [r_rec], scale=-1.0)
            else:
                RECIP(rec, banks[7][:, :], [rb[7]], [r_rec])
            yb = yt[i % 2]
            TT("dve", yb, banks[6][:, :], rec, ALU.mult, [rb[6], r_rec], [r_yt[i % 2]])
            P.dma(ysc[768 + h * 128:768 + (h + 1) * 128, tg * 512:(tg + 1) * 512], yb, reads=[r_yt[i % 2]],
                  writes=[ry(("m", h, tg))])
            if tg == NTG - 1:
                pump(8)

        skew(n_it, [m3b, m1b, m2, m0, m1a, m3], [5, 2, 3, 0, 1, 4])

    phase_mem()
    pump(10000)
    P.barrier()
    A.top = 0

    def phase_D():
        statsD = A.alloc(16, F32)
        r_statsD = R()
        MEMSET("dve", statsD[:, 0:1], EPS, [r_statsD])
        hTg = [A.alloc(8 * 512).rearrange("p (k t) -> p k t", k=8) for _ in range(2)]
        r_hTg = [R(), R()]
        ytile = A.alloc(10 * 512).rearrange("p (k t) -> p k t", k=10)
        r_ytile = R()
        xt = A.alloc(4 * D, F32).rearrange("p (a c) -> p a c", a=4)
        r_xt = [R() for _ in range(4)]
        gw = [A.alloc(8 * 384).rearrange("p (k c) -> p k c", k=8) for _ in range(2)]
        r_gw = [R(), R()]
        bw = [A.alloc(10 * 128).rearrange("p (k c) -> p k c", k=10) for _ in range(2)]
        r_bw = [R(), R()]
        G = [A.alloc(512) for _ in range(3)]
        r_G = [R() for _ in range(3)]
        tt_ = [A.alloc(512, F32) for _ in range(3)]
        r_tt = [R() for _ in range(3)]
        mergedT = A.alloc(8 * 512).rearrange("p (k t) -> p k t", k=8)
        r_merged = [R() for _ in range(8)]
        wo = [A.alloc(8 * 512).rearrange("p (k c) -> p k c", k=8) for _ in range(2)]
        r_wo = [R(), R()]
        h2 = [A.alloc(D) for _ in range(2)]
        r_h2 = [R(), R()]
        junk = A.alloc(D)
        r_junk = R()
        h2T = A.alloc(8 * 512).rearrange("p (k t) -> p k t", k=8)
        r_h2T = [R() for _ in range(4)]
        wu = [A.alloc(8 * 512).rearrange("p (k c) -> p k c", k=8) for _ in range(3)]
        r_wu = [R(), R(), R()]
        aT = A.alloc(32 * 512).rearrange("p (f t) -> p f t", f=32)
        r_aT = [R() for _ in range(32)]
        rl = [A.alloc(512) for _ in range(2)]
        r_rl = [R(), R()]
        wd = [A.alloc(4 * 1024).rearrange("p (f c) -> p f c", f=4) for _ in range(2)]
        r_wd = [R(), R()]

        def load_tg_inputs(tg):
            hb_ = tg % 2
            ts = slice(tg * 512, (tg + 1) * 512)
            P.dma(hTg[hb_], hsc[:, :, ts], reads=r_hsc, writes=[r_hTg[hb_]])
            P.dma(ytile, ysc[:, ts].rearrange("(k p) t -> p k t", p=128),
                  reads=[r for (k_, r) in _rysc.items() if k_[2] == tg], writes=[r_ytile])

        def load_chunk_w(c):
            wbi = c % 2
            rd = []
            for br in range(3):
                rd += load_w(None, "gate", 0, 8, br * 1024 + c * 128, br * 1024 + (c + 1) * 128)[1]
            P.dma(gw[wbi], wsc["gate"][c], reads=rd, writes=[r_gw[wbi]])
            rd = load_w(None, "br", 0, 10, c * 128, (c + 1) * 128)[1]
            P.dma(bw[wbi], wsc["br"][c], reads=rd, writes=[r_bw[wbi]])

        def load_x(tg):
            for a in range(4):
                P.dma(xt[:, a, :], x_d[tg * 512 + a * 128:tg * 512 + (a + 1) * 128, :], writes=[r_xt[a]])

        load_tg_inputs(0)
        load_chunk_w(0)
        load_chunk_w(1)
        load_x(0)
        for tg in range(NTG):
            hb_ = tg % 2
            for c in range(8):
                wbi = c % 2
                for br in range(3):
                    for kc in range(8):
                        MM(banks[br][:, :], gw[wbi][:, kc, br * 128:(br + 1) * 128], hTg[hb_][:, kc, :], kc == 0, kc == 7,
                           [r_gw[wbi], r_hTg[hb_]], [rb[br]])
                    col = GC_BGATE + br * 8 + c
                    ACT(G[br], banks[br][:, :], AF.Sigmoid, [rb[br], r_g], [r_G[br]], bias=gcols[:, col:col + 1])
                kr = ((0, 2), (2, 6), (6, 10))
                for br in range(3):
                    k0, k1 = kr[br]
                    for kc in range(k0, k1):
                        MM(banks[3 + br][:, :], bw[wbi][:, kc, :], ytile[:, kc, :], kc == k0, kc == k1 - 1,
                           [r_bw[wbi], r_ytile], [rb[3 + br]])
                    TT("dve", tt_[br], banks[3 + br][:, :], G[br], ALU.mult, [rb[3 + br], r_G[br]], [r_tt[br]])
                TT("pool", tt_[0], tt_[0], tt_[1], ALU.add, [r_tt[0], r_tt[1]], [r_tt[0]])
                TT("pool", mergedT[:, c, :], tt_[0], tt_[2], ALU.add, [r_tt[0], r_tt[2]], [r_merged[c]])
                if c + 2 < 8:
                    load_chunk_w(c + 2)
            for ch in range(2):
                src, rd = load_w(None, "out", 0, 8, ch * 512, (ch + 1) * 512)
                P.dma(wo[ch], src, reads=rd, writes=[r_wo[ch]])
            for g_ in range(2):
                src, rd = load_w(None, "up", 0, 8, g_ * 512, (g_ + 1) * 512)
                P.dma(wu[g_], src, reads=rd, writes=[r_wu[g_]])

            def d0(a):
                for ch in range(2):
                    bk = 4 + 2 * (a % 2) + ch
                    for kc in range(8):
                        MM(banks[bk][:, :], mergedT[:, kc, a * 128:(a + 1) * 128], wo[ch][:, kc, :], kc == 0, kc == 7,
                           [r_merged[kc], r_wo[ch]], [rb[bk]])
                    xs = xt[:, a, ch * 512:(ch + 1) * 512]
                    TT("dve", xs, banks[bk][:, :], xs, ALU.add, [rb[bk], r_xt[a]], [r_xt[a]])

            def d1(a):
                rms_part1(xt[:, a, :], [r_xt[a]], 1 + 2 * a, 2 + 2 * a, statsD, r_statsD, h2[a % 2], r_h2[a % 2], junk,
                          r_junk)

            def d2(a):
                rows_to_T(h2[a % 2], r_h2[a % 2], a % 2, h2T[:, :, a * 128:(a + 1) * 128], [r_h2T[a]])

            skew(4, [d0, d1, d2], [0, 1, 2])
            for fg in range(8):
                wbi = fg % 3
                if fg + 2 < 8:
                    src, rd = load_w(None, "up", 0, 8, (fg + 2) * 512, (fg + 3) * 512)
                    P.dma(wu[(fg + 2) % 3], src, reads=rd, writes=[r_wu[(fg + 2) % 3]])
                if fg in (4, 6):
                    g_ = (fg - 4) // 2
                    src, rd = load_w(None, "down", g_ * 4, g_ * 4 + 4, 0, 1024)
                    P.dma(wd[g_], src, reads=rd, writes=[r_wd[g_]])
                for f4 in range(4):
                    fc = fg * 4 + f4
                    bk = 2 + fc % 4
                    for kc in range(8):
                        MM(banks[bk][:, :], wu[wbi][:, kc, f4 * 128:(f4 + 1) * 128], h2T[:, kc, :], kc == 0, kc == 7,
                           [r_wu[wbi]] + r_h2T, [rb[bk]])
                    ACT(rl[fc % 2], banks[bk][:, :], AF.Relu, [rb[bk]], [r_rl[fc % 2]])
                    TT("dve", aT[:, fc, :], rl[fc % 2], rl[fc % 2], ALU.mult, [r_rl[fc % 2]], [r_aT[fc]])
            if tg + 1 < NTG:
                load_tg_inputs(tg + 1)
                load_chunk_w(0)
                load_chunk_w(1)
            for fg in range(8):
                wbi = fg % 2
                for a in range(4):
                    for ch in range(2):
                        bk = a * 2 + ch
                        for f4 in range(4):
                            fc = fg * 4 + f4
                            MM(banks[bk][:, :], aT[:, fc, a * 128:(a + 1) * 128], wd[wbi][:, f4, ch * 512:(ch + 1) * 512],
                               fc == 0, fc == 31, [r_aT[fc], r_wd[wbi]], [rb[bk]])
                if fg + 2 < 8:
                    src, rd = load_w(None, "down", (fg + 2) * 4, (fg + 2) * 4 + 4, 0, 1024)
                    P.dma(wd[wbi], src, reads=rd, writes=[r_wd[wbi]])
            for a in range(4):
                for ch in range(2):
                    bk = a * 2 + ch
                    xs = xt[:, a, ch * 512:(ch + 1) * 512]
                    TT("dve", xs, banks[bk][:, :], xs, ALU.add, [rb[bk], r_xt[a]], [r_xt[a]])
                P.dma(out_d[tg * 512 + a * 128:tg * 512 + (a + 1) * 128, :], xt[:, a, :], reads=[r_xt[a]], writes=[R()])
            if tg + 1 < NTG:
                load_x(tg + 1)

    phase_D()

    P.finalize()
    st.close()
    return nc


_CACHE = {}


def _get_nc(debug=False):
    if debug not in _CACHE:
        _CACHE[debug] = build(debug)
    return _CACHE[debug]


def make_in_maps(inputs, cores):
    cb, tabs = _consts()
    shared = {}
    for k, v in inputs.items():
        if k in ("x", "mem"):
            continue
        a = np.ascontiguousarray(np.asarray(v, dtype=np.float32))
        shared[k] = a[0]
    shared["cb"] = cb
    shared["tabs"] = tabs
    maps = []
    for c in cores:
        m = dict(shared)
        m["x"] = np.ascontiguousarray(np.asarray(inputs["x"][c], dtype=np.float32))
        m["mem"] = np.ascontiguousarray(np.asarray(inputs["mem"][c], dtype=np.float32))
        maps.append(m)
    return maps


def kernel(**inputs):
    nc = _get_nc(False)
    cores = list(range(8))
    in_maps = make_in_maps(inputs, cores)
    res = run_bass_kernel_spmd(nc, in_maps, core_ids=cores)
    out = np.stack([np.asarray(r["out"], dtype=np.float32) for r in res.results], axis=0)
    return out
```

```python
import contextlib
import numpy as np
import ml_dtypes
import concourse.bass as bass
import concourse.mybir as mybir
from concourse.bass_utils import run_bass_kernel_spmd

F32 = mybir.dt.float32
BF16 = mybir.dt.bfloat16
ALU = mybir.AluOpType
AF = mybir.ActivationFunctionType

T = 4096
D = 1024
NT = T // 128
NTG = T // 512
EPS = 1e-6
IN_W = 4360
OFF_QA, OFF_KA, OFF_VA, OFF_QB, OFF_KB, OFF_VB, OFF_FL, OFF_QM = 0, 768, 1536, 2304, 2816, 3328, 3840, 3848
DIL = ((128, 1), (512, 4), (2048, 16))


class R:
    __slots__ = ("name", "lw", "readers")

    def __init__(self, name=""):
        self.name = name
        self.lw = None
        self.readers = []


class _Op:
    __slots__ = ("eng", "fn", "reads", "writes", "dma", "deps", "needs_inc", "cnt", "dsem", "dval", "bar", "nobar")

    def __init__(self, eng, fn, reads, writes, dma, nobar=False):
        self.eng = eng
        self.fn = fn
        self.reads = reads
        self.writes = writes
        self.dma = dma
        self.deps = {}
        self.needs_inc = False
        self.cnt = None
        self.bar = False
        self.nobar = nobar


DSEM_POOLS = {"sp": 30, "pool": 10, "act": 2, "pe": 1, "dve": 1}


class Prog:
    ENGS = ("pe", "act", "dve", "pool", "sp")

    def __init__(self, nc):
        self.nc = nc
        self.ops = []

    def op(self, eng, fn, reads=(), writes=(), dma=False, nobar=False):
        self.ops.append(_Op(eng, fn, list(reads), list(writes), dma, nobar))

    def dma(self, out, in_, reads=(), writes=(), eng="sp", nobar=False, **kw):
        self.op(eng, lambda e: e.dma_start(out=out, in_=in_, **kw), reads, writes, dma=True, nobar=nobar)

    def barrier(self):
        o = _Op(None, None, [], [], False)
        o.bar = True
        self.ops.append(o)

    def finalize(self):
        nc = self.nc
        ops = self.ops
        last_on = {}
        last_dma_on_sem = {}
        pending_bar = {e: None for e in self.ENGS}
        ndma = {e: 0 for e in self.ENGS}
        base = {}
        acc = 0
        for e in self.ENGS:
            base[e] = acc
            acc += DSEM_POOLS[e]
        nsem_total = acc
        for i, op in enumerate(ops):
            if op.bar:
                deps = dict((j, "BAR") for j in last_on.values())
                for j in last_dma_on_sem.values():
                    deps[j] = "BAR"
                for e in self.ENGS:
                    pending_bar[e] = deps
                continue
            deps = {}
            if pending_bar[op.eng] is not None and not op.nobar:
                deps.update(pending_bar[op.eng])
                pending_bar[op.eng] = None
            for r in op.reads:
                if r.lw is not None:
                    deps[r.lw] = "RAW"
            for r in op.writes:
                if r.lw is not None and deps.get(r.lw) != "RAW":
                    deps[r.lw] = "WAW"
                for j in r.readers:
                    deps.setdefault(j, "WAR")
            deps.pop(i, None)
            for r in op.reads:
                r.readers.append(i)
            for r in op.writes:
                r.lw = i
                r.readers = []
            need = {}
            for j, kind in deps.items():
                oj = ops[j]
                if oj.dma:
                    need[j] = kind
                    continue
                if oj.eng == op.eng and not op.dma:
                    if op.eng == "pe" or kind == "BAR":
                        continue
                need[j] = kind
                oj.needs_inc = True
            op.deps = need
            if op.dma:
                k = ndma[op.eng]
                npool = DSEM_POOLS[op.eng]
                op.dsem = base[op.eng] + k % npool
                op.dval = 16 * (k // npool + 1)
                if not op.nobar:
                    last_dma_on_sem[op.dsem] = i
                ndma[op.eng] = k + 1
            elif not op.nobar:
                last_on[op.eng] = i
        cnt = {e: 0 for e in self.ENGS}
        for op in ops:
            if op.bar or op.dma:
                continue
            if op.needs_inc:
                cnt[op.eng] += 1
                op.cnt = cnt[op.eng]
        self.ndma = ndma
        self.counts = cnt
        final_vals = {}
        for op in ops:
            if not op.bar and op.dma:
                final_vals[op.dsem] = max(final_vals.get(op.dsem, 0), op.dval)
        with contextlib.ExitStack() as st:
            esem = {e: st.enter_context(nc.semaphore("s_" + e)) for e in self.ENGS}
            dsem = [st.enter_context(nc.semaphore("d_%d" % k)) for k in range(nsem_total)]
            block = st.enter_context(nc.Block())
            handles = {"pe": block.tensor, "act": block.scalar, "dve": block.vector, "pool": block.gpsimd,
                       "sp": block.sync}

            def make_body(ename):
                def body(eng):
                    waited = {}

                    def wait(sem, key, val):
                        if waited.get(key, 0) >= val:
                            return
                        eng.wait_ge(sem, val)
                        waited[key] = val

                    for op in ops:
                        if op.bar or op.eng != ename:
                            continue
                        for j in op.deps:
                            oj = ops[j]
                            if oj.dma:
                                wait(dsem[oj.dsem], ("d", oj.dsem), oj.dval)
                            else:
                                wait(esem[oj.eng], ("e", oj.eng), oj.cnt)
                        if op.dma:
                            if op.dval > 16:
                                wait(dsem[op.dsem], ("d", op.dsem), op.dval - 16)
                            ins = op.fn(eng)
                            ins.then_inc(dsem[op.dsem], 16)
                        else:
                            ins = op.fn(eng)
                            if op.needs_inc:
                                ins.then_inc(esem[ename], 1)
                    if ename == "sp":
                        for k, v in sorted(final_vals.items()):
                            wait(dsem[k], ("d", k), v)
                return body

            for ename in self.ENGS:
                handles[ename](make_body(ename))


def _consts():
    bf = ml_dtypes.bfloat16
    ident = np.eye(128, dtype=np.float32)
    bones = np.zeros((128, 128), np.float32)
    bones[:64, :64] = 1.0
    bones[64:, 64:] = 1.0
    ones = np.ones((128, 128), np.float32)
    rot = np.zeros((128, 128), np.float32)
    for hb in (0, 64):
        for e in range(8):
            rot[hb + e + 8, hb + e] = -1.0
            rot[hb + e, hb + e + 8] = 1.0
    j = np.arange(128)[:, None]
    i = np.arange(128)[None, :]
    mcur = (j <= i).astype(np.float32)
    mprev = (j >= i).astype(np.float32)
    mask4 = np.concatenate([mcur, mprev, mcur, mprev], axis=1)
    cb = np.concatenate([ident, bones, ones, rot, mask4], axis=1).astype(bf)
    inv_freq = (500000.0 ** (-(np.arange(0, 16, 2, dtype=np.float32) / np.float32(16)))).astype(np.float32)
    ang = np.arange(T, dtype=np.float32)[:, None] * inv_freq[None, :]
    cos, sin = np.cos(ang).astype(np.float32), np.sin(ang).astype(np.float32)
    ctab = np.ones((128, T), np.float32)
    stab = np.zeros((128, T), np.float32)
    for hb in (0, 64):
        for e in range(16):
            ctab[hb + e] = cos[:, e % 8]
            stab[hb + e] = sin[:, e % 8]
    tabs = np.zeros((3, 2, 128, T), np.float32)
    for g, (win, d) in enumerate(DIL):
        nsub = T // d
        pos = np.arange(T)
        tok = (pos % nsub) * d + pos // nsub
        tabs[g, 0] = ctab[:, tok]
        tabs[g, 1] = stab[:, tok]
    return cb, tabs


C_ID, C_BONES, C_ONES, C_ROT, C_MASK = 0, 128, 256, 384, 512


def build(debug=False):
    nc = bass.Bass("TRN2", target_bir_lowering=False)
    P = Prog(nc)
    st = contextlib.ExitStack()

    def MM(out, lhsT, rhs, start, stop, reads, writes):
        P.op("pe", lambda e: e.matmul(out, lhsT=lhsT, rhs=rhs, start=start, stop=stop), reads, writes)

    def TR(out, in_, identity, reads, writes):
        P.op("pe", lambda e: e.transpose(out=out, in_=in_, identity=identity), reads, writes)

    def ACT(out, in_, func, reads, writes, **kw):
        P.op("act", lambda e: e.activation(out=out, in_=in_, func=func, **kw), reads, writes)

    def TT(eng, out, in0, in1, op, reads, writes):
        P.op(eng, lambda e: e.tensor_tensor(out=out, in0=in0, in1=in1, op=op), reads, writes)

    def TS(eng, out, in0, s1, s2, op0, op1, reads, writes):
        if s2 is None:
            P.op(eng, lambda e: e.tensor_scalar(out=out, in0=in0, scalar1=s1, scalar2=None, op0=op0), reads, writes)
        else:
            P.op(eng, lambda e: e.tensor_scalar(out=out, in0=in0, scalar1=s1, scalar2=s2, op0=op0, op1=op1), reads,
                 writes)

    def STT(out, in0, scalar, in1, op0, op1, reads, writes):
        P.op("dve", lambda e: e.scalar_tensor_tensor(out=out, in0=in0, scalar=scalar, in1=in1, op0=op0, op1=op1),
             reads, writes)

    def RECIP(out, in_, reads, writes):
        P.op("dve", lambda e: e.reciprocal(out=out, in_=in_), reads, writes)

    def COPY(eng, out, in_, reads, writes):
        P.op(eng, lambda e: e.tensor_copy(out=out, in_=in_), reads, writes)

    def MEMSET(eng, ap, val, writes):
        P.op(eng, lambda e: e.memset(ap, val), [], writes)

    def skew(n_items, stages, lags, reverse=False):
        order = list(zip(stages, lags))
        if reverse:
            order = order[::-1]
        for step in range(n_items + max(lags)):
            for fn, lag in order:
                i = step - lag
                if 0 <= i < n_items:
                    fn(i)

    def din(name, shape, dt=F32):
        return nc.dram_tensor(name, list(shape), dt, kind="ExternalInput").ap()

    x_d = din("x", [T, D])
    mem_d = din("mem", [256, D])
    g_mix_d = din("g_mix", [D])
    w_in_d = din("w_in", [D, IN_W])
    b_f_d = din("b_f", [8])
    g_qA_d = din("g_qA", [64])
    g_kA_d = din("g_kA", [64])
    g_qB_d = din("g_qB", [64])
    g_kB_d = din("g_kB", [64])
    g_mem_d = din("g_mem", [D])
    w_mkv_d = din("w_mem_kv", [D, 1024])
    g_qM_d = din("g_qM", [128])
    g_kM_d = din("g_kM", [128])
    w_gate_d = din("w_gate", [D, 3072])
    b_gate_d = din("b_gate", [3072])
    w_bra_d = din("w_br_a", [256, D])
    w_brb_d = din("w_br_b", [512, D])
    w_brm_d = din("w_br_m", [512, D])
    w_out_d = din("w_out", [D, D])
    g_mlp_d = din("g_mlp", [D])
    w_up_d = din("w_up", [D, 4096])
    w_down_d = din("w_down", [4096, D])
    cb_d = din("cb", [128, 1024], BF16)
    tabs_d = din("tabs", [3, 2, 128, T])
    out_d = nc.dram_tensor("out", [T, D], F32, kind="ExternalOutput").ap()

    def scratch(name, shape, dt=BF16, dbg=False):
        kind = "ExternalOutput" if (debug and dbg) else "Internal"
        return nc.dram_tensor(name, list(shape), dt, kind=kind).ap()

    ysc = scratch("ysc", [1280, T], dbg=True)
    hsc = scratch("hsc", [128, 8, T])
    augsc = scratch("augsc", [8, 2, 6, T])
    wsc = {
        "in": scratch("wsc_in", [128, 8, IN_W]),
        "mkv": scratch("wsc_mkv", [128, 8, 1024]),
        "gate": scratch("wsc_gate", [8, 128, 8, 384]),
        "br": scratch("wsc_br", [8, 128, 10, 128]),
        "out": scratch("wsc_out", [128, 8, 1024]),
        "up": scratch("wsc_up", [128, 8, 4096]),
        "down": scratch("wsc_down", [128, 32, 1024]),
    }

    def sbt(name, shape, dt=F32):
        return st.enter_context(nc.sbuf_tensor(name, list(shape), dt))

    cb = sbt("cb_sb", [128, 1024], BF16)
    r_cb = R("cb")
    gcols = sbt("gcols", [128, 64])
    r_g = R("gcols")
    ARENA_COLS = 183 * 512
    arena = sbt("arena", [128, ARENA_COLS], BF16)

    class Arena:
        def __init__(self):
            self.top = 0

        def alloc(self, ncols, dt=BF16, parts=128):
            nb = ncols * (2 if dt == BF16 else 4)
            nb = (nb + 63) // 64 * 64
            o = self.top
            self.top += nb // 2
            assert self.top <= ARENA_COLS, ("arena overflow", self.top)
            ap = arena[:, o:o + nb // 2]
            if dt != BF16:
                ap = ap.bitcast(dt)
            return ap[:, 0:ncols]

    A = Arena()
    banks = [st.enter_context(nc.psum_tensor("bank%d" % i, [128, 512], F32)) for i in range(8)]
    rb = [R("bank%d" % i) for i in range(8)]

    ident = cb[:, C_ID:C_ID + 128]
    bones = cb[:, C_BONES:C_BONES + 128]
    ones_bf = cb[:, C_ONES:C_ONES + 128]
    rotm = cb[:, C_ROT:C_ROT + 128]
    mask4 = cb[:, C_MASK:C_MASK + 512]

    P.dma(cb[:], cb_d, writes=[r_cb])

    GC_GMIX, GC_GMLP, GC_GMEM = 0, 8, 16
    GC_QA, GC_KA, GC_QB, GC_KB, GC_QM, GC_KM = 24, 25, 26, 27, 28, 29
    GC_BGATE = 32
    GC_NBF = 56
    GC_RAW = 57
    r_gs = []

    def _rgn():
        r_gs.append(R())
        return r_gs[-1]

    def load_params():
        P.dma(gcols[:, GC_GMIX:GC_GMIX + 8], g_mix_d.rearrange("(kc p) -> p kc", p=128), writes=[_rgn()], allow_slow_non_contiguous=True)
        P.dma(gcols[:, GC_GMLP:GC_GMLP + 8], g_mlp_d.rearrange("(kc p) -> p kc", p=128), writes=[_rgn()], allow_slow_non_contiguous=True)
        P.dma(gcols[:, GC_GMEM:GC_GMEM + 8], g_mem_d.rearrange("(kc p) -> p kc", p=128), writes=[_rgn()], allow_slow_non_contiguous=True)
        P.dma(gcols[:, GC_BGATE:GC_BGATE + 24], b_gate_d.rearrange("(c p) -> p c", p=128), writes=[_rgn()], allow_slow_non_contiguous=True)
        for col, gd in ((0, g_qA_d), (1, g_kA_d), (2, g_qB_d), (3, g_kB_d)):
            for hb in (0, 64):
                P.dma(gcols[hb:hb + 64, GC_RAW + col:GC_RAW + col + 1], gd.rearrange("(p o) -> p o", o=1),
                      writes=[_rgn()], allow_slow_non_contiguous=True)
        P.dma(gcols[:, GC_RAW + 4:GC_RAW + 5], g_qM_d.rearrange("(p o) -> p o", o=1), writes=[_rgn()], allow_slow_non_contiguous=True)
        P.dma(gcols[:, GC_RAW + 5:GC_RAW + 6], g_kM_d.rearrange("(p o) -> p o", o=1), writes=[_rgn()], allow_slow_non_contiguous=True)
        P.dma(gcols[0:8, GC_NBF:GC_NBF + 1], b_f_d.rearrange("(p o) -> p o", o=1), writes=[_rgn()], allow_slow_non_contiguous=True)

        def gscale(dst, src, s):
            P.op("dve", lambda e: e.tensor_scalar(out=gcols[:, dst:dst + 1], in0=gcols[:, src:src + 1],
                                                   scalar1=float(s), scalar2=None, op0=ALU.mult), [r_g] + r_gs, [r_g])
        gscale(GC_QA, GC_RAW + 0, 0.125)
        gscale(GC_KA, GC_RAW + 1, 1.0)
        gscale(GC_QB, GC_RAW + 2, 0.125)
        gscale(GC_KB, GC_RAW + 3, 1.0)
        gscale(GC_QM, GC_RAW + 4, 128.0 ** -0.5)
        gscale(GC_KM, GC_RAW + 5, 1.0)
        P.op("dve", lambda e: e.tensor_scalar(out=gcols[0:8, GC_NBF:GC_NBF + 1], in0=gcols[0:8, GC_NBF:GC_NBF + 1],
                                           scalar1=-1.0, scalar2=None, op0=ALU.mult), [r_g], [r_g])

    Rw = {}
    jobs = []

    def chunks_of(n, cw):
        return [(c, min(c + cw, n)) for c in range(0, n, cw)]

    WCH = {"gate": chunks_of(3072, 512),
           "br": chunks_of(1024, 512), "out": chunks_of(1024, 512), "up": chunks_of(4096, 512),
           "down": chunks_of(1024, 512)}

    def add_jobs(name, src, nkc, gc, src_row0=0, kc0=0):
        for (c0, c1) in WCH[name]:
            for kc in range(nkc):
                r = R("w_%s_%d_%d" % (name, kc0 + kc, c0))
                Rw[(name, kc0 + kc, c0)] = r
                if name == "gate":
                    br_, cc0 = c0 // 1024, (c0 % 1024) // 128
                    dst = wsc[name][cc0:cc0 + 4, :, kc0 + kc, br_ * 128:(br_ + 1) * 128].rearrange("c p n -> p c n")
                elif name == "br":
                    cc0 = c0 // 128
                    dst = wsc[name][cc0:cc0 + 4, :, kc0 + kc, :].rearrange("c p n -> p c n")
                else:
                    dst = wsc[name][:, kc0 + kc, c0:c1]
                jobs.append((src[src_row0 + kc * 128:src_row0 + (kc + 1) * 128, c0:c1],
                             None if gc is None else gc + kc, dst, r, c1 - c0))

    add_jobs("gate", w_gate_d, 8, GC_GMIX)
    add_jobs("br", w_bra_d, 2, None, kc0=0)
    add_jobs("br", w_brb_d, 4, None, kc0=2)
    add_jobs("br", w_brm_d, 4, None, kc0=6)
    add_jobs("out", w_out_d, 8, None)
    add_jobs("up", w_up_d, 8, GC_GMLP)
    add_jobs("down", w_down_d, 32, None)

    NSTG = 3
    stg32 = [sbt("stg32_%d" % i, [128, 1024]) for i in range(NSTG)]
    r_s32 = [R() for _ in range(NSTG)]
    NSTGJ = 4
    jst32 = [sbt("jst32_%d" % i, [128, 512]) for i in range(NSTGJ)]
    jst16 = [sbt("jst16_%d" % i, [128, 512], BF16) for i in range(2)]
    r_j32 = [R() for _ in range(NSTGJ)]
    r_j16 = [R() for _ in range(2)]
    job_state = {"i": 0, "s": 0}

    def _job_load(i):
        src, gc, dst, r, cw = jobs[i]
        s = i % NSTGJ
        P.dma(jst32[s][:, 0:cw], src, writes=[r_j32[s]], eng="pool", nobar=True)

    def pump(n):
        for _ in range(n):
            i = job_state["i"]
            if i >= len(jobs):
                return
            if i == 0:
                _job_load(0)
                if len(jobs) > 1:
                    _job_load(1)
            if i + 2 < len(jobs):
                _job_load(i + 2)
            job_state["i"] = i + 1
            src, gc, dst, r, cw = jobs[i]
            s = i % NSTGJ
            a32, a16 = jst32[s][:, 0:cw], jst16[i % 2][:, 0:cw]
            if gc is None:
                P.op("pool", lambda e, a16=a16, a32=a32: e.tensor_copy(out=a16, in_=a32), [r_j32[s]], [r_j16[i % 2]],
                     nobar=True)
            else:
                P.op("pool", lambda e, a16=a16, a32=a32, gc=gc, cw=cw: e.tensor_tensor(
                    out=a16, in0=a32, in1=gcols[:, gc:gc + 1].to_broadcast([128, cw]), op=ALU.mult),
                    [r_j32[s], r_g], [r_j16[i % 2]], nobar=True)
            if len(dst.shape) == 3:
                a16 = a16.rearrange("p (c n) -> p c n", n=128)
            P.dma(dst, a16, reads=[r_j16[i % 2]], writes=[r], eng="pool", nobar=True)

    def wdirect(dst, src, gc, ncols, nk=8, **kw):
        s = job_state["s"] % NSTG
        job_state["s"] += 1
        a32 = stg32[s][:, 0:nk * ncols].rearrange("p (k c) -> p k c", k=nk)
        P.dma(a32, src.rearrange("(k p) c -> p k c", p=128), writes=[r_s32[s]], **kw)
        return a32, s

    def wdirect_cast(dst, a32, s, gc, ncols, writes, nk=8):
        if gc is None:
            P.op("dve", lambda e: e.tensor_copy(out=dst, in_=a32), [r_s32[s]], writes)
        else:
            P.op("dve", lambda e: e.tensor_tensor(
                out=dst, in0=a32, in1=gcols[:, gc:gc + nk].unsqueeze(2).to_broadcast([128, nk, ncols]), op=ALU.mult),
                [r_s32[s], r_g], writes)

    def wload_in(dst, c0, ncols, writes, **kw):
        a32, s = wdirect(dst, w_in_d[:, c0:c0 + ncols], GC_GMIX, ncols, **kw)
        wdirect_cast(dst, a32, s, GC_GMIX, ncols, writes)

    def wload_in_dma(c0, ncols):
        return wdirect(None, w_in_d[:, c0:c0 + ncols], GC_GMIX, ncols)

    def wload_in_cast(dst, tok, ncols, writes):
        wdirect_cast(dst, tok[0], tok[1], GC_GMIX, ncols, writes)

    def load_w(dst, name, kc0, kc1, c0, c1):
        reads = []
        for (a, b) in WCH[name]:
            if a < c1 and b > c0:
                for kc in range(kc0, kc1):
                    reads.append(Rw[(name, kc, a)])
        if name in ("gate", "br"):
            return None, reads
        return wsc[name][:, kc0:kc1, c0:c1], reads

    hT = A.alloc(8 * T).rearrange("p (k t) -> p k t", k=8)
    r_hT = [R("hT%d" % i) for i in range(NT)]
    mark_h = A.top

    def hT_reads(t0, t1):
        return [r_hT[i] for i in range(t0 // 128, (t1 - 1) // 128 + 1)]

    def rms_rows_to_T(src, src_reads, ss_col, rs_col, stats, r_stats, xn, r_xn, bank_i, dstT, dst_writes, junk,
                      r_junk):
        ACT(junk, src, AF.Square, src_reads + [r_junk], [r_junk, r_stats], accum_out=stats[:, ss_col:ss_col + 1])
        ACT(stats[:, rs_col:rs_col + 1], stats[:, ss_col:ss_col + 1], AF.Sqrt, [r_stats], [r_stats], scale=1.0 / D,
            bias=stats[:, 0:1])
        RECIP(stats[:, rs_col:rs_col + 1], stats[:, rs_col:rs_col + 1], [r_stats], [r_stats])
        TS("dve", xn, src, stats[:, rs_col:rs_col + 1], None, ALU.mult, None, src_reads + [r_stats], [r_xn])
        rows_to_T(xn, r_xn, bank_i, dstT, dst_writes)

    def rms_part1(src, src_reads, ss_col, rs_col, stats, r_stats, xn, r_xn, junk, r_junk):
        ACT(junk, src, AF.Square, src_reads + [r_junk], [r_junk, r_stats], accum_out=stats[:, ss_col:ss_col + 1])
        ACT(stats[:, rs_col:rs_col + 1], stats[:, ss_col:ss_col + 1], AF.Ln, [r_stats], [r_stats], scale=1.0 / D,
            bias=stats[:, 0:1])
        ACT(stats[:, rs_col:rs_col + 1], stats[:, rs_col:rs_col + 1], AF.Exp, [r_stats], [r_stats], scale=-0.5)
        TS("dve", xn, src, stats[:, rs_col:rs_col + 1], None, ALU.mult, None, src_reads + [r_stats], [r_xn])

    def rows_to_T(xn, r_xn, bank_i, dstT, dst_writes):
        bk = banks[bank_i][:].bitcast(BF16)
        for k in range(8):
            TR(bk[:, k * 128:(k + 1) * 128], xn[:, k * 128:(k + 1) * 128], ident, [r_xn, r_cb], [rb[bank_i]])
        COPY("dve", dstT, bk[:, 0:1024].rearrange("p (k t) -> p k t", k=8), [rb[bank_i]], dst_writes)

    statsA = A.alloc(2 * NT + 4, F32)
    r_statsA = R("statsA")
    MEMSET("dve", statsA[:, 0:1], EPS, [r_statsA])
    NXB = 6
    xts = [A.alloc(D, F32) for _ in range(NXB)]
    r_xts = [R() for _ in range(NXB)]
    r_xh = [[R(), R()] for _ in range(NXB)]
    xns = [A.alloc(D) for _ in range(2)]
    r_xns = [R() for _ in range(2)]
    junkA = A.alloc(D)
    r_junkA = R()
    pass
    wfl = A.alloc(8 * 8).rearrange("p (k c) -> p k c", k=8)
    r_wfl = R()
    spb = [A.alloc(512, F32) for _ in range(2)]
    csb = [A.alloc(512, F32) for _ in range(2)]
    r1b = [A.alloc(512, F32) for _ in range(2)]
    posb = [A.alloc(3 * 512).rearrange("p (j t) -> p j t", j=3) for _ in range(2)]
    negb = [A.alloc(3 * 512).rearrange("p (j t) -> p j t", j=3) for _ in range(2)]
    r_spb, r_csb, r_r1b, r_posb, r_negb = [R(), R()], [R(), R()], [R(), R()], [R(), R()], [R(), R()]
    onesr = A.alloc(T)
    r_onesr = R()
    ra_ = []

    def p2_setup():
        wload_in(wfl, OFF_FL, 8, [r_wfl], allow_slow_non_contiguous=True)
        MEMSET("dve", onesr[0:8, :], 1.0, [r_onesr])
        for j in range(3):
            ra_.append(R())
            P.dma(augsc[:, 0, 3 + j, :], onesr[0:8, :], reads=[r_onesr], writes=[ra_[-1]])
            ra_.append(R())
            P.dma(augsc[:, 1, j, :], onesr[0:8, :], reads=[r_onesr], writes=[ra_[-1]])

    def p2_tg(tg):
        b_ = 2 + tg % 2
        k = tg % 2
        sp_, cs_, r1_, pos_, neg_ = spb[k][0:8, :], csb[k][0:8, :], r1b[k][0:8, :], posb[k][0:8, :, :], negb[k][0:8, :, :]
        for kc in range(8):
            MM(banks[b_][0:8, :], wfl[:, kc, :], hT[:, kc, tg * 512:(tg + 1) * 512], kc == 0, kc == 7,
               [r_wfl] + hT_reads(tg * 512, tg * 512 + 512), [rb[b_]])
        ACT(sp_, banks[b_][0:8, :], AF.Exp, [rb[b_], r_g], [r_spb[k]], scale=-1.0, bias=gcols[0:8, GC_NBF:GC_NBF + 1])
        ACT(sp_, sp_, AF.Ln, [r_spb[k]], [r_spb[k]], scale=1.0, bias=1.0)
        init = 0.0 if tg == 0 else csb[1 - k][0:8, 511:512]
        rd = [r_spb[k]] + ([] if tg == 0 else [r_csb[1 - k]])
        P.op("dve", lambda e: e.tensor_tensor_scan(out=cs_, data0=sp_, data1=sp_, initial=init, op0=ALU.add,
                                                    op1=ALU.max), rd, [r_csb[k]])
        COPY("dve", pos_[:, 0, :], cs_, [r_csb[k]], [r_posb[k]])
        TT("dve", r1_, cs_, pos_[:, 0, :], ALU.subtract, [r_csb[k], r_posb[k]], [r_r1b[k]])
        COPY("dve", pos_[:, 1, :], r1_, [r_r1b[k]], [r_posb[k]])
        TT("dve", sp_, r1_, pos_[:, 1, :], ALU.subtract, [r_r1b[k], r_posb[k]], [r_spb[k]])
        COPY("dve", pos_[:, 2, :], sp_, [r_spb[k]], [r_posb[k]])
        TS("dve", neg_, pos_, -1.0, None, ALU.mult, None, [r_posb[k]], [r_negb[k]])
        ts_ = slice(tg * 512, (tg + 1) * 512)
        ra_.append(R())
        P.dma(augsc[:, 0, 0:3, ts_], neg_, reads=[r_negb[k]], writes=[ra_[-1]])
        ra_.append(R())
        P.dma(augsc[:, 1, 3:6, ts_], pos_, reads=[r_posb[k]], writes=[ra_[-1]])

    def a3(i):
        if i % 4 == 3:
            p2_tg(i // 4)

    def a0(i):
        for hf in range(2):
            P.dma(xts[i % NXB][:, hf * 512:(hf + 1) * 512], x_d[i * 128:(i + 1) * 128, hf * 512:(hf + 1) * 512],
                  writes=[r_xh[i % NXB][hf]])
        if i == 3:
            load_params()
        if i == 4:
            p2_setup()

    def a1(i):
        rms_part1(xts[i % NXB], r_xh[i % NXB], 1 + 2 * i, 2 + 2 * i, statsA, r_statsA, xns[i % 2], r_xns[i % 2], junkA,
                  r_junkA)

    def a2(i):
        rows_to_T(xns[i % 2], r_xns[i % 2], i % 2, hT[:, :, i * 128:(i + 1) * 128], [r_hT[i]])

    skew(NT, [a0, a1, a2, a3], [0, 4, 5, 6])
    r_hsc = [R("hsc%d" % k) for k in range(8)]
    hsc_todo = list(range(8))

    def spill_hsc(n):
        for _ in range(n):
            if hsc_todo:
                k = hsc_todo.pop(0)
                P.dma(hsc[:, k, :], hT[:, k, :], reads=r_hT, writes=[r_hsc[k]])
    P.barrier()
    A.top = mark_h

    def proj_fm(bank_i, wt, wreads, tg):
        for kc in range(8):
            MM(banks[bank_i][:, :], wt[:, kc, :], hT[:, kc, tg * 512:(tg + 1) * 512], kc == 0, kc == 7,
               wreads + hT_reads(tg * 512, tg * 512 + 512), [rb[bank_i]])

    def head_norm_a(qbank, ssbank, onesmat, tm, ncol=512):
        ACT(tm["sq"][:, 0:ncol], banks[qbank][:, 0:ncol], AF.Square, [rb[qbank]], [tm["r_sq"]])
        MM(banks[ssbank][:, 0:ncol], onesmat, tm["sq"][:, 0:ncol], True, True, [tm["r_sq"], r_cb], [rb[ssbank]])

    def head_norm_b(ssbank, inv_n, tm, ncol=512):
        ACT(tm["ms"][:, 0:ncol], banks[ssbank][:, 0:ncol], AF.Ln, [rb[ssbank]], [tm["r_ms"]], scale=float(inv_n), bias=EPS)
        ACT(tm["rstd"][:, 0:ncol], tm["ms"][:, 0:ncol], AF.Exp, [tm["r_ms"]], [tm["r_rstd"]], scale=-0.5)

    def head_norm(qbank, ssbank, onesmat, inv_n, tm, ncol=512):
        head_norm_a(qbank, ssbank, onesmat, tm, ncol)
        head_norm_b(ssbank, inv_n, tm, ncol)

    def alloc_norm_tmps(n=2, rope=False):
        t = []
        for _ in range(n):
            dct = dict(sq=A.alloc(512), r_sq=R(), ms=A.alloc(512, F32), r_ms=R(), rstd=A.alloc(512, F32),
                       r_rstd=R())
            dct.update(qn=A.alloc(512), r_qn=R())
            if rope:
                dct.update(t1=A.alloc(512, F32), r_t1=R(), t2=A.alloc(512, F32), r_t2=R())
            t.append(dct)
        return t

    mark_p = A.top

    _rysc = {}

    def ry(key):
        if key not in _rysc:
            _rysc[key] = R("ysc" + str(key))
        return _rysc[key]

    def walloc():
        return A.alloc(8 * 128).rearrange("p (k c) -> p k c", k=8)


    def phase_dil():
        ctabs = [A.alloc(512, F32) for _ in range(2)]
        stabs = [A.alloc(512, F32) for _ in range(2)]
        r_tabs = [R(), R()]
        tmps = alloc_norm_tmps(2, rope=True)
        QT = A.alloc(T)
        KT = A.alloc(T)
        r_QT = [R() for _ in range(NTG)]
        r_KT = [R() for _ in range(NTG)]
        Vt = A.alloc(32 * 256).rearrange("p (b h c) -> p b h c", b=32, h=2)
        r_Vt = [R() for _ in range(8)]
        MEMSET("dve", Vt[:, :, :, 64:128], 1.0, r_Vt)
        acc = [A.alloc(T, F32) for _ in range(2)]
        r_acc = [[R() for _ in range(NTG)] for _ in range(2)]
        wq, wk, wv = [walloc(), walloc()], [walloc(), walloc()], [walloc(), walloc()]
        r_wq, r_wk, r_wv = [R(), R()], [R(), R()], [R(), R()]
        PT = [A.alloc(512) for _ in range(4)]
        r_PT = [R() for _ in range(4)]
        recs = [A.alloc(1024, F32)] * 2
        r_recs = [R()] * 2
        yt = [A.alloc(1024) for _ in range(2)]
        r_yt = [R(), R()]
        iters = [(hp, g) for hp in range(2) for g in range(3)]

        wtok = {}

        def issue_w_dma(k):
            hp_, g_ = iters[k]
            c0_ = g_ * 256 + hp_ * 128
            wtok[k] = [wload_in_dma(off + c0_, 128) for off in (OFF_QA, OFF_KA, OFF_VA)]

        def issue_w_cast(k):
            wb_ = k % 2
            for tok, (wt_, rw_) in zip(wtok[k], ((wq[wb_], r_wq[wb_]), (wk[wb_], r_wk[wb_]), (wv[wb_], r_wv[wb_]))):
                wload_in_cast(wt_, tok, 128, [rw_])

        issue_w_dma(0)
        issue_w_cast(0)
        pending_norm = []
        late_units = []

        def mk_unit(hp, hh, c):
            def unit():
                sl = slice(c * 1024, (c + 1) * 1024)
                ra = [r_acc[hh][2 * c], r_acc[hh][2 * c + 1]]
                rc, r_rc = recs[c % 2], r_recs[c % 2]
                ACT(rc[0:64, :], acc[hh][64:128, sl], AF.Ln, ra, [r_rc])
                ACT(rc[0:64, :], rc[0:64, :], AF.Exp, [r_rc], [r_rc], scale=-1.0)
                yb = yt[c % 2]
                TT("dve", yb[0:64, :], acc[hh][0:64, sl], rc[0:64, :], ALU.mult, ra + [r_rc], [r_yt[c % 2]])
                row0 = (hp * 2 + hh) * 64
                P.dma(ysc[row0:row0 + 64, sl], yb[0:64, :], reads=[r_yt[c % 2]],
                      writes=[ry(("a", hp * 2 + hh, 2 * c)), ry(("a", hp * 2 + hh, 2 * c + 1))])
            return unit
        for it, (hp, g) in enumerate(iters):
            if True:
                win, d = DIL[g]
                nsub = T // d
                nb = nsub // 128
                wb = it % 2
                if it + 1 < len(iters):
                    issue_w_dma(it + 1)
                spill_hsc(2)
                pump(0)
                chains = [(wq[wb], r_wq[wb], GC_QA, QT, r_QT), (wk[wb], r_wk[wb], GC_KA, KT, r_KT)]

                def st0(i):
                    wt, rw, gc, dst, r_dst = chains[i // NTG]
                    proj_fm(i % 3, wt, [rw], i % NTG)

                def st1a(i):
                    head_norm_a(i % 3, 3 + i % 2, bones, tmps[i % 2])

                def st1(i):
                    wt, rw, gc, dst, r_dst = chains[i // NTG]
                    tm = tmps[i % 2]
                    qb_ = i % 3
                    tg = i % NTG
                    P.dma(ctabs[i % 2], tabs_d[0, 0, :, tg * 512:(tg + 1) * 512], writes=[r_tabs[i % 2]])
                    P.dma(stabs[i % 2], tabs_d[0, 1, :, tg * 512:(tg + 1) * 512], writes=[r_tabs[i % 2]])
                    head_norm_b(3 + i % 2, 1.0 / 64, tm)
                    STT(tm["qn"], banks[qb_][:, :], gcols[:, gc:gc + 1], tm["rstd"], ALU.mult, ALU.mult,
                        [rb[qb_], tm["r_rstd"], r_g], [tm["r_qn"]])

                def st2(i):
                    wt, rw, gc, dst, r_dst = chains[i // NTG]
                    tg = i % NTG
                    tm = tmps[i % 2]
                    rb_ = 5 + i % 2
                    ctab, stab, r_tab = ctabs[i % 2], stabs[i % 2], r_tabs[i % 2]
                    MM(banks[rb_][:, :], rotm, tm["qn"], True, True, [tm["r_qn"], r_cb], [rb[rb_]])
                    TT("dve", tm["t1"], banks[rb_][:, :], stab, ALU.mult, [rb[rb_], r_tab], [tm["r_t1"]])
                    TT("dve", tm["t2"], tm["qn"], ctab, ALU.mult, [tm["r_qn"], r_tab], [tm["r_t2"]])
                    n0 = tg * 512 // d
                    dv = dst.rearrange("p (r n) -> p r n", r=d)[:, :, n0:n0 + 512 // d]
                    wr = sorted(set((r * nsub + n0) // 512 for r in range(d)))
                    TT("pool", dv, tm["t1"].rearrange("p (n r) -> p r n", r=d),
                       tm["t2"].rearrange("p (n r) -> p r n", r=d), ALU.add, [tm["r_t1"], tm["r_t2"]],
                       [r_dst[w] for w in wr])

                def sv(i):
                    if i % 2:
                        return
                    B4 = i // 2
                    bk = 7
                    for q in range(4):
                        B = B4 * 4 + q
                        r, b = B // nb, B % nb
                        t0 = r + d * 128 * b
                        t1 = t0 + d * 127 + 1
                        for kc in range(8):
                            MM(banks[bk][:, q * 128:(q + 1) * 128], hT[:, kc, t0:t1:d], wv[wb][:, kc, :], kc == 0,
                               kc == 7, [r_wv[wb]] + hT_reads(t0, t1), [rb[bk]])
                    ACT(Vt[:, B4 * 4:(B4 + 1) * 4, :, 0:64],
                        banks[bk][:, :].rearrange("p (b h c) -> p b h c", b=4, h=2), AF.Copy, [rb[bk]], [r_Vt[B4]])

                def sn(i):
                    if pending_norm:
                        pending_norm.pop(0)()

                skew(2 * NTG, [st0, st1a, st1, sn, st2, sv], [0, 1, 2, 2, 3, 1])
                if it + 1 < len(iters):
                    issue_w_cast(it + 1)
                for hh in range(2):
                    hb = hh * 64

                    def emit_S(pi):
                        sbk = pi % 4
                        for s_, B in enumerate((2 * pi, 2 * pi + 1)):
                            n = 256 if B < 31 else 128
                            MM(banks[sbk][:, s_ * 256:s_ * 256 + n], KT[hb:hb + 64, B * 128:(B + 1) * 128],
                               QT[hb:hb + 64, B * 128:B * 128 + n], True, True,
                               [r_KT[B // 4], r_QT[B // 4], r_QT[(B * 128 + n - 1) // 512]], [rb[sbk]])
                        wdt = 512 if (2 * pi + 1) < 31 else 384
                        ACT(PT[sbk][:, 0:wdt], banks[sbk][:, 0:wdt], AF.Exp, [rb[sbk]], [r_PT[sbk]])
                        TT("dve", PT[sbk][:, 0:wdt], PT[sbk][:, 0:wdt], mask4[:, 0:wdt], ALU.mult, [r_PT[sbk], r_cb],
                           [r_PT[sbk]])

                    def emit_PV(Bq):
                        ob = 4 + (Bq // 4) % 4
                        col = (Bq % 4) * 128
                        terms = []
                        if (Bq % nb) != 0:
                            Bk = Bq - 1
                            terms.append((Bk, (Bk // 2) % 4, (Bk % 2) * 256 + 128))
                        terms.append((Bq, (Bq // 2) % 4, (Bq % 2) * 256))
                        for ti, (Bk, pt, pc) in enumerate(terms):
                            MM(banks[ob][:, col:col + 128], Vt[:, Bk, hh, :], PT[pt][:, pc:pc + 128], ti == 0,
                               ti == len(terms) - 1, [r_Vt[Bk // 4], r_PT[pt]], [rb[ob]])

                    def emit_evac(q4):
                        ob = 4 + q4 % 4
                        Av = acc[hh].rearrange("p (n r) -> p r n", r=d)
                        p0 = 512 * q4
                        if nsub >= 512:
                            r_, n0 = p0 // nsub, p0 % nsub
                            av = Av[:, r_, n0:n0 + 512]
                            src_ = banks[ob][:, :]
                            toks = [r_ + d * n0, r_ + d * (n0 + 511)]
                        else:
                            rr = 512 // nsub
                            r_ = p0 // nsub
                            av = Av[:, r_:r_ + rr, :]
                            src_ = banks[ob][:, :].rearrange("p (r n) -> p r n", r=rr)
                            toks = [r_, r_ + rr - 1 + d * (nsub - 1)]
                        ra = [r_acc[hh][w] for w in range(toks[0] // 512, toks[1] // 512 + 1)]
                        if g == 0:
                            ACT(av, src_, AF.Copy, [rb[ob]], ra)
                        else:
                            TT("dve", av, src_, av, ALU.add, [rb[ob]] + ra, ra)

                    emit_S(0)
                    emit_S(1)
                    for pi in range(16):
                        if pi + 2 < 16:
                            emit_S(pi + 2)
                        for Bq in (2 * pi, 2 * pi + 1):
                            emit_PV(Bq)
                            if Bq % 4 == 3:
                                emit_evac(Bq // 4)
                        if late_units and pi % 3 == 2:
                            late_units.pop(0)()
                    while late_units:
                        late_units.pop(0)()
                    if g == 2 and hh == 0:
                        for c in range(4):
                            late_units.append(mk_unit(hp, 0, c))
                if g == 2:
                    for c in range(4):
                        pending_norm.append(mk_unit(hp, 1, c))
        while pending_norm:
            pending_norm.pop(0)()

    phase_dil()
    P.barrier()
    A.top = mark_p

    def phase_fox():
        tmps = alloc_norm_tmps(2)
        QK = {}
        for nm in ("QA", "QB", "KA", "KB"):
            QK[nm] = (A.alloc(T), [R() for _ in range(NTG)], R())
        Vt = A.alloc(32 * 256).rearrange("p (b h c) -> p b h c", b=32, h=2)
        r_Vt = [R() for _ in range(8)]
        MEMSET("dve", Vt[:, :, :, 64:128], 1.0, r_Vt)
        wq, wk, wv = [walloc(), walloc()], [walloc(), walloc()], [walloc(), walloc()]
        r_wq, r_wk, r_wv = [R(), R()], [R(), R()], [R(), R()]
        NPT = 6
        PT = [A.alloc(512) for _ in range(NPT)]
        r_PT = [R() for _ in range(NPT)]
        rec = A.alloc(512, F32)
        r_rec = R()
        yt = [A.alloc(512) for _ in range(2)]
        r_yt = [R(), R()]
        def issue_w(hp_):
            wb_ = hp_ % 2
            wload_in(wq[wb_], OFF_QB + hp_ * 128, 128, [r_wq[wb_]])
            wload_in(wk[wb_], OFF_KB + hp_ * 128, 128, [r_wk[wb_]])
            wload_in(wv[wb_], OFF_VB + hp_ * 128, 128, [r_wv[wb_]])

        issue_w(0)
        for hp in range(4):
            wb = hp % 2
            for hh, (qn_, kn_) in enumerate((("QA", "KA"), ("QB", "KB"))):
                h = hp * 2 + hh
                P.dma(QK[qn_][0][64:70, :], augsc[h, 0, :, :], reads=ra_, writes=[QK[qn_][2]])
                P.dma(QK[kn_][0][64:70, :], augsc[h, 1, :, :], reads=ra_, writes=[QK[kn_][2]])
            chains = [(wq[wb], r_wq[wb], GC_QB, "QA", "QB"), (wk[wb], r_wk[wb], GC_KB, "KA", "KB")]

            def st0(i):
                wt, rw, gc, nA, nB = chains[i // NTG]
                proj_fm(i % 3, wt, [rw], i % NTG)

            def st1a(i):
                head_norm_a(i % 3, 3 + i % 2, bones, tmps[i % 2])

            def st2(i):
                wt, rw, gc, nA, nB = chains[i // NTG]
                tg = i % NTG
                COPY("dve", QK[nB][0][0:64, tg * 512:(tg + 1) * 512], tmps[i % 2]["qn"][64:128, :],
                     [tmps[i % 2]["r_qn"]], [QK[nB][1][tg]])

            def st1(i):
                wt, rw, gc, nA, nB = chains[i // NTG]
                tg = i % NTG
                tm = tmps[i % 2]
                qb_ = i % 3
                head_norm_b(3 + i % 2, 1.0 / 64, tm)
                sl = slice(tg * 512, (tg + 1) * 512)
                STT(QK[nA][0][0:64, sl], banks[qb_][0:64, :], gcols[0:64, gc:gc + 1], tm["rstd"][0:64, :], ALU.mult,
                    ALU.mult, [rb[qb_], tm["r_rstd"], r_g], [QK[nA][1][tg]])
                STT(tm["qn"][64:128, :], banks[qb_][64:128, :], gcols[64:128, gc:gc + 1], tm["rstd"][64:128, :],
                    ALU.mult, ALU.mult, [rb[qb_], tm["r_rstd"], r_g], [tm["r_qn"]])

            skew(2 * NTG, [st0, st1a, st1, st2], [0, 1, 2, 3])
            if hp + 1 < 4:
                issue_w(hp + 1)
            for B4 in range(8):
                bk = 6 + B4 % 2
                for q in range(4):
                    B = B4 * 4 + q
                    for kc in range(8):
                        MM(banks[bk][:, q * 128:(q + 1) * 128], hT[:, kc, B * 128:(B + 1) * 128], wv[wb][:, kc, :],
                           kc == 0, kc == 7, [r_wv[wb], r_hT[B]], [rb[bk]])
                ACT(Vt[:, B4 * 4:(B4 + 1) * 4, :, 0:64], banks[bk][:, :].rearrange("p (b h c) -> p b h c", b=4, h=2),
                    AF.Copy, [rb[bk]], [r_Vt[B4]])
            for hh, (qn_, kn_) in enumerate((("QA", "KA"), ("QB", "KB"))):
                h = hp * 2 + hh
                Qt, rQ, rQa = QK[qn_]
                Kt, rK, rKa = QK[kn_]
                blocks = []
                for tg in range(NTG):
                    for kb in range(4 * tg + 4):
                        blocks.append((tg, kb))
                nblk = len(blocks)

                def emit_S(bi):
                    tg, kb = blocks[bi]
                    j = max(0, kb - 4 * tg)
                    n = 512 - 128 * j
                    q0 = tg * 512 + 128 * j
                    sbk = bi % NPT
                    MM(banks[sbk][:, 0:n], Kt[0:70, kb * 128:(kb + 1) * 128], Qt[0:70, q0:q0 + n], True, True,
                       [rK[kb // 4], rKa, rQ[tg], rQa], [rb[sbk]])
                    ACT(PT[sbk][:, 0:n], banks[sbk][:, 0:n], AF.Exp, [rb[sbk]], [r_PT[sbk]])
                    if kb >= 4 * tg:
                        TT("dve", PT[sbk][:, 0:128], PT[sbk][:, 0:128], mask4[:, 0:128], ALU.mult, [r_PT[sbk], r_cb],
                           [r_PT[sbk]])

                def emit_PV(bi):
                    tg, kb = blocks[bi]
                    j = max(0, kb - 4 * tg)
                    n = 512 - 128 * j
                    sbk = bi % NPT
                    ob = 6 + tg % 2
                    last = (kb == 4 * tg + 3)
                    MM(banks[ob][:, 128 * j:512], Vt[:, kb, hh, :], PT[sbk][:, 0:n], kb == 0, last,
                       [r_Vt[kb // 4], r_PT[sbk]], [rb[ob]])
                    if last:
                        RECIP(rec[0:64, :], banks[ob][64:128, :], [rb[ob]], [r_rec])
                        yb = yt[tg % 2]
                        TT("dve", yb[0:64, :], banks[ob][0:64, :], rec[0:64, :], ALU.mult, [rb[ob], r_rec],
                           [r_yt[tg % 2]])
                        P.dma(ysc[256 + h * 64:256 + (h + 1) * 64, tg * 512:(tg + 1) * 512], yb[0:64, :],
                              reads=[r_yt[tg % 2]], writes=[ry(("b", h, tg))])

                LA = 4
                for bi in range(min(LA, nblk)):
                    emit_S(bi)
                for bi in range(nblk):
                    if bi + LA < nblk:
                        emit_S(bi + LA)
                    emit_PV(bi)
                pump(24)

    phase_fox()
    P.barrier()
    A.top = mark_p

    def phase_mem():
        tmps = alloc_norm_tmps(2)
        statsM = A.alloc(8, F32)
        r_statsM = R()
        MEMSET("dve", statsM[:, 0:1], EPS, [r_statsM])
        memT = A.alloc(8 * 256).rearrange("p (k t) -> p k t", k=8)
        r_memT = [R(), R()]
        mt = [A.alloc(D, F32) for _ in range(2)]
        r_mt = [R(), R()]
        mn = [A.alloc(D) for _ in range(2)]
        r_mn = [R(), R()]
        junk = A.alloc(D)
        r_junk = R()
        for i in range(2):
            P.dma(mt[i], mem_d[i * 128:(i + 1) * 128, :], writes=[r_mt[i]])
            rms_rows_to_T(mt[i], [r_mt[i]], 1 + 2 * i, 2 + 2 * i, statsM, r_statsM, mn[i], r_mn[i], 6,
                          memT[:, :, i * 128:(i + 1) * 128], [r_memT[i]], junk, r_junk)
        wkv = A.alloc(8 * 1024).rearrange("p (k c) -> p k c", k=8)
        r_wkv = R()
        for cc in range(8):
            a32_, s_ = wdirect(None, w_mkv_d[:, cc * 128:(cc + 1) * 128], GC_GMEM, 128)
            wdirect_cast(wkv[:, :, cc * 128:(cc + 1) * 128], a32_, s_, GC_GMEM, 128, [r_wkv])
        KmT = A.alloc(4 * 256).rearrange("p (h t) -> p h t", h=4)
        r_KmT = [R() for _ in range(4)]
        Vm = A.alloc(2 * 512).rearrange("p (b c) -> p b c", b=2)
        r_Vm = R()
        for h in range(4):
            tm = tmps[h % 2]
            qb_, sb_ = h % 2, 2 + h % 2
            for kc in range(8):
                MM(banks[qb_][:, 0:256], wkv[:, kc, h * 128:(h + 1) * 128], memT[:, kc, :], kc == 0, kc == 7,
                   [r_wkv] + r_memT, [rb[qb_]])
            head_norm(qb_, sb_, ones_bf, 1.0 / 128, tm, ncol=256)
            STT(KmT[:, h, :], banks[qb_][:, 0:256], gcols[:, GC_KM:GC_KM + 1], tm["rstd"][:, 0:256], ALU.mult, ALU.mult,
                [rb[qb_], tm["r_rstd"], r_g], [r_KmT[h]])
        for kb in range(2):
            bk = 6 + kb
            for kc in range(8):
                MM(banks[bk][:, :], memT[:, kc, kb * 128:(kb + 1) * 128], wkv[:, kc, 512:1024], kc == 0, kc == 7,
                   [r_wkv, r_memT[kb]], [rb[bk]])
            ACT(Vm[:, kb, :], banks[bk][:, :], AF.Copy, [rb[bk]], [r_Vm])
        wqm = [walloc(), walloc()]
        r_wqm = [R(), R()]
        PT = [A.alloc(512) for _ in range(4)]
        r_PT = [R() for _ in range(4)]
        rec = A.alloc(512, F32)
        r_rec = R()
        yt = [A.alloc(512) for _ in range(2)]
        r_yt = [R(), R()]
        QmT = [A.alloc(512) for _ in range(2)]
        r_QmT = [R(), R()]
        wqm_all = [walloc(), walloc()]
        r_wqm_all = [R(), R()]
        for h in range(2):
            wload_in(wqm[h], OFF_QM + h * 128, 128, [r_wqm[h]])
        for h in range(2):
            wload_in(wqm_all[h], OFF_QM + (2 + h) * 128, 128, [r_wqm_all[h]])
        wts = [(wqm[0], r_wqm[0]), (wqm[1], r_wqm[1]), (wqm_all[0], r_wqm_all[0]), (wqm_all[1], r_wqm_all[1])]
        n_it = 4 * NTG

        def m0(i):
            h, tg = i // NTG, i % NTG
            proj_fm(i % 3, wts[h][0], [wts[h][1]], tg)

        def m1a(i):
            head_norm_a(i % 3, 3, ones_bf, tmps[i % 2])

        def m1b(i):
            tm = tmps[i % 2]
            head_norm_b(3, 1.0 / 128, tm)
            STT(QmT[i % 2], banks[i % 3][:, :], gcols[:, GC_QM:GC_QM + 1], tm["rstd"], ALU.mult, ALU.mult,
                [rb[i % 3], tm["r_rstd"], r_g], [r_QmT[i % 2]])

        def m2(i):
            h = i // NTG
            for kb in range(2):
                sbk = 4 + kb
                pt = (i * 2 + kb) % 4
                MM(banks[sbk][:, :], KmT[:, h, kb * 128:(kb + 1) * 128], QmT[i % 2], True, True,
                   [r_KmT[h], r_QmT[i % 2]], [rb[sbk]])
                ACT(PT[pt], banks[sbk][:, :], AF.Exp, [rb[sbk]], [r_PT[pt]])

        def m3(i):
            h, tg = i // NTG, i % NTG
            for kb in range(2):
                pt = (i * 2 + kb) % 4
                MM(banks[6][:, :], Vm[:, kb, h * 128:(h + 1) * 128], PT[pt], kb == 0, kb == 1, [r_Vm, r_PT[pt]],
                   [rb[6]])
            for kb in range(2):
                pt = (i * 2 + kb) % 4
                MM(banks[7][:, :], ones_bf, PT[pt], kb == 0, kb == 1, [r_cb, r_PT[pt]], [rb[7]])

        def m3b(i):
            h, tg = i // NTG, i % NTG
            ACT(rec, banks[7][:, :], AF.Ln, [rb[7]], [r_rec])
            ACT(rec, rec, AF.Exp, [r_rec], [r_rec], scale=-1.0)
            yb = yt[i % 2]
            TT("dve", yb, banks[6][:, :], rec, ALU.mult, [rb[6], r_rec], [r_yt[i % 2]])
            P.dma(ysc[768 + h * 128:768 + (h + 1) * 128, tg * 512:(tg + 1) * 512], yb, reads=[r_yt[i % 2]],
                  writes=[ry(("m", h, tg))])
            if tg == NTG - 1:
                pump(8)

        skew(n_it, [m3b, m1b, m2, m0, m1a, m3], [5, 2, 3, 0, 1, 4])

    phase_mem()
    pump(10000)
    P.barrier()
    A.top = 0

    def phase_D():
        statsD = A.alloc(16, F32)
        r_statsD = R()
        MEMSET("dve", statsD[:, 0:1], EPS, [r_statsD])
        hTg = [A.alloc(8 * 512).rearrange("p (k t) -> p k t", k=8) for _ in range(2)]
        r_hTg = [R(), R()]
        ytile = A.alloc(10 * 512).rearrange("p (k t) -> p k t", k=10)
        r_ytile = R()
        xt = A.alloc(4 * D, F32).rearrange("p (a c) -> p a c", a=4)
        r_xt = [R() for _ in range(4)]
        gw = [A.alloc(8 * 384).rearrange("p (k c) -> p k c", k=8) for _ in range(2)]
        r_gw = [R(), R()]
        bw = [A.alloc(10 * 128).rearrange("p (k c) -> p k c", k=10) for _ in range(2)]
        r_bw = [R(), R()]
        G = [A.alloc(512) for _ in range(3)]
        r_G = [R() for _ in range(3)]
        tt_ = [A.alloc(512, F32) for _ in range(3)]
        r_tt = [R() for _ in range(3)]
        mergedT = A.alloc(8 * 512).rearrange("p (k t) -> p k t", k=8)
        r_merged = [R() for _ in range(8)]
        wo = [A.alloc(8 * 512).rearrange("p (k c) -> p k c", k=8) for _ in range(2)]
        r_wo = [R(), R()]
        h2 = [A.alloc(D) for _ in range(2)]
        r_h2 = [R(), R()]
        junk = A.alloc(D)
        r_junk = R()
        h2T = A.alloc(8 * 512).rearrange("p (k t) -> p k t", k=8)
        r_h2T = [R() for _ in range(4)]
        wu = [A.alloc(8 * 512).rearrange("p (k c) -> p k c", k=8) for _ in range(3)]
        r_wu = [R(), R(), R()]
        aT = A.alloc(32 * 512).rearrange("p (f t) -> p f t", f=32)
        r_aT = [R() for _ in range(32)]
        rl = [A.alloc(512) for _ in range(2)]
        r_rl = [R(), R()]
        wd = [A.alloc(4 * 1024).rearrange("p (f c) -> p f c", f=4) for _ in range(2)]
        r_wd = [R(), R()]

        def load_tg_inputs(tg):
            hb_ = tg % 2
            ts = slice(tg * 512, (tg + 1) * 512)
            P.dma(hTg[hb_], hsc[:, :, ts], reads=r_hsc, writes=[r_hTg[hb_]])
            P.dma(ytile, ysc[:, ts].rearrange("(k p) t -> p k t", p=128),
                  reads=[r for (k_, r) in _rysc.items() if k_[2] == tg], writes=[r_ytile])

        def load_chunk_w(c):
            wbi = c % 2
            rd = []
            for br in range(3):
                rd += load_w(None, "gate", 0, 8, br * 1024 + c * 128, br * 1024 + (c + 1) * 128)[1]
            P.dma(gw[wbi], wsc["gate"][c], reads=rd, writes=[r_gw[wbi]])
            rd = load_w(None, "br", 0, 10, c * 128, (c + 1) * 128)[1]
            P.dma(bw[wbi], wsc["br"][c], reads=rd, writes=[r_bw[wbi]])

        def load_x(tg):
            for a in range(4):
                P.dma(xt[:, a, :], x_d[tg * 512 + a * 128:tg * 512 + (a + 1) * 128, :], writes=[r_xt[a]])

        load_tg_inputs(0)
        load_chunk_w(0)
        load_chunk_w(1)
        load_x(0)
        for tg in range(NTG):
            hb_ = tg % 2
            for c in range(8):
                wbi = c % 2
                for br in range(3):
                    for kc in range(8):
                        MM(banks[br][:, :], gw[wbi][:, kc, br * 128:(br + 1) * 128], hTg[hb_][:, kc, :], kc == 0, kc == 7,
                           [r_gw[wbi], r_hTg[hb_]], [rb[br]])
                    col = GC_BGATE + br * 8 + c
                    ACT(G[br], banks[br][:, :], AF.Sigmoid, [rb[br], r_g], [r_G[br]], bias=gcols[:, col:col + 1])
                kr = ((0, 2), (2, 6), (6, 10))
                for br in range(3):
                    k0, k1 = kr[br]
                    for kc in range(k0, k1):
                        MM(banks[3 + br][:, :], bw[wbi][:, kc, :], ytile[:, kc, :], kc == k0, kc == k1 - 1,
                           [r_bw[wbi], r_ytile], [rb[3 + br]])
                    TT("dve", tt_[br], banks[3 + br][:, :], G[br], ALU.mult, [rb[3 + br], r_G[br]], [r_tt[br]])
                TT("pool", tt_[0], tt_[0], tt_[1], ALU.add, [r_tt[0], r_tt[1]], [r_tt[0]])
                TT("pool", mergedT[:, c, :], tt_[0], tt_[2], ALU.add, [r_tt[0], r_tt[2]], [r_merged[c]])
                if c + 2 < 8:
                    load_chunk_w(c + 2)
            for ch in range(2):
                src, rd = load_w(None, "out", 0, 8, ch * 512, (ch + 1) * 512)
                P.dma(wo[ch], src, reads=rd, writes=[r_wo[ch]])
            for g_ in range(2):
                src, rd = load_w(None, "up", 0, 8, g_ * 512, (g_ + 1) * 512)
                P.dma(wu[g_], src, reads=rd, writes=[r_wu[g_]])

            def d0(a):
                for ch in range(2):
                    bk = 4 + 2 * (a % 2) + ch
                    for kc in range(8):
                        MM(banks[bk][:, :], mergedT[:, kc, a * 128:(a + 1) * 128], wo[ch][:, kc, :], kc == 0, kc == 7,
                           [r_merged[kc], r_wo[ch]], [rb[bk]])
                    xs = xt[:, a, ch * 512:(ch + 1) * 512]
                    TT("dve", xs, banks[bk][:, :], xs, ALU.add, [rb[bk], r_xt[a]], [r_xt[a]])

            def d1(a):
                rms_part1(xt[:, a, :], [r_xt[a]], 1 + 2 * a, 2 + 2 * a, statsD, r_statsD, h2[a % 2], r_h2[a % 2], junk,
                          r_junk)

            def d2(a):
                rows_to_T(h2[a % 2], r_h2[a % 2], a % 2, h2T[:, :, a * 128:(a + 1) * 128], [r_h2T[a]])

            skew(4, [d0, d1, d2], [0, 1, 2])
            for fg in range(8):
                wbi = fg % 3
                if fg + 2 < 8:
                    src, rd = load_w(None, "up", 0, 8, (fg + 2) * 512, (fg + 3) * 512)
                    P.dma(wu[(fg + 2) % 3], src, reads=rd, writes=[r_wu[(fg + 2) % 3]])
                if fg in (4, 6):
                    g_ = (fg - 4) // 2
                    src, rd = load_w(None, "down", g_ * 4, g_ * 4 + 4, 0, 1024)
                    P.dma(wd[g_], src, reads=rd, writes=[r_wd[g_]])
                for f4 in range(4):
                    fc = fg * 4 + f4
                    bk = 2 + fc % 4
                    for kc in range(8):
                        MM(banks[bk][:, :], wu[wbi][:, kc, f4 * 128:(f4 + 1) * 128], h2T[:, kc, :], kc == 0, kc == 7,
                           [r_wu[wbi]] + r_h2T, [rb[bk]])
                    ACT(rl[fc % 2], banks[bk][:, :], AF.Relu, [rb[bk]], [r_rl[fc % 2]])
                    TT("dve", aT[:, fc, :], rl[fc % 2], rl[fc % 2], ALU.mult, [r_rl[fc % 2]], [r_aT[fc]])
            if tg + 1 < NTG:
                load_tg_inputs(tg + 1)
                load_chunk_w(0)
                load_chunk_w(1)
            for fg in range(8):
                wbi = fg % 2
                for a in range(4):
                    for ch in range(2):
                        bk = a * 2 + ch
                        for f4 in range(4):
                            fc = fg * 4 + f4
                            MM(banks[bk][:, :], aT[:, fc, a * 128:(a + 1) * 128], wd[wbi][:, f4, ch * 512:(ch + 1) * 512],
                               fc == 0, fc == 31, [r_aT[fc], r_wd[wbi]], [rb[bk]])
                if fg + 2 < 8:
                    src, rd = load_w(None, "down", (fg + 2) * 4, (fg + 2) * 4 + 4, 0, 1024)
                    P.dma(wd[wbi], src, reads=rd, writes=[r_wd[wbi]])
            for a in range(4):
                for ch in range(2):
                    bk = a * 2 + ch
                    xs = xt[:, a, ch * 512:(ch + 1) * 512]
                    TT("dve", xs, banks[bk][:, :], xs, ALU.add, [rb[bk], r_xt[a]], [r_xt[a]])
                P.dma(out_d[tg * 512 + a * 128:tg * 512 + (a + 1) * 128, :], xt[:, a, :], reads=[r_xt[a]], writes=[R()])
            if tg + 1 < NTG:
                load_x(tg + 1)

    phase_D()

    P.finalize()
    st.close()
    return nc


_CACHE = {}


def _get_nc(debug=False):
    if debug not in _CACHE:
        _CACHE[debug] = build(debug)
    return _CACHE[debug]


def make_in_maps(inputs, cores):
    cb, tabs = _consts()
    shared = {}
    for k, v in inputs.items():
        if k in ("x", "mem"):
            continue
        a = np.ascontiguousarray(np.asarray(v, dtype=np.float32))
        shared[k] = a[0]
    shared["cb"] = cb
    shared["tabs"] = tabs
    maps = []
    for c in cores:
        m = dict(shared)
        m["x"] = np.ascontiguousarray(np.asarray(inputs["x"][c], dtype=np.float32))
        m["mem"] = np.ascontiguousarray(np.asarray(inputs["mem"][c], dtype=np.float32))
        maps.append(m)
    return maps


def kernel(**inputs):
    nc = _get_nc(False)
    cores = list(range(8))
    in_maps = make_in_maps(inputs, cores)
    res = run_bass_kernel_spmd(nc, in_maps, core_ids=cores)
    out = np.stack([np.asarray(r["out"], dtype=np.float32) for r in res.results], axis=0)
    return out
```

```python
import contextlib
import numpy as np
import ml_dtypes
import concourse.bass as bass
import concourse.mybir as mybir
from concourse.bass_utils import run_bass_kernel_spmd

F32 = mybir.dt.float32
BF16 = mybir.dt.bfloat16
ALU = mybir.AluOpType
AF = mybir.ActivationFunctionType

T = 4096
D = 1024
NT = T // 128
NTG = T // 512
EPS = 1e-6
IN_W = 4360
OFF_QA, OFF_KA, OFF_VA, OFF_QB, OFF_KB, OFF_VB, OFF_FL, OFF_QM = 0, 768, 1536, 2304, 2816, 3328, 3840, 3848
DIL = ((128, 1), (512, 4), (2048, 16))


class R:
    __slots__ = ("name", "lw", "readers")

    def __init__(self, name=""):
        self.name = name
        self.lw = None
        self.readers = []


class _Op:
    __slots__ = ("eng", "fn", "reads", "writes", "dma", "deps", "needs_inc", "cnt", "dsem", "dval", "bar", "nobar")

    def __init__(self, eng, fn, reads, writes, dma, nobar=False):
        self.eng = eng
        self.fn = fn
        self.reads = reads
        self.writes = writes
        self.dma = dma
        self.deps = {}
        self.needs_inc = False
        self.cnt = None
        self.bar = False
        self.nobar = nobar


DSEM_POOLS = {"sp": 30, "pool": 10, "act": 2, "pe": 1, "dve": 1}


class Prog:
    ENGS = ("pe", "act", "dve", "pool", "sp")

    def __init__(self, nc):
        self.nc = nc
        self.ops = []

    def op(self, eng, fn, reads=(), writes=(), dma=False, nobar=False):
        self.ops.append(_Op(eng, fn, list(reads), list(writes), dma, nobar))

    def dma(self, out, in_, reads=(), writes=(), eng="sp", nobar=False, **kw):
        self.op(eng, lambda e: e.dma_start(out=out, in_=in_, **kw), reads, writes, dma=True, nobar=nobar)

    def barrier(self):
        o = _Op(None, None, [], [], False)
        o.bar = True
        self.ops.append(o)

    def finalize(self):
        nc = self.nc
        ops = self.ops
        last_on = {}
        last_dma_on_sem = {}
        pending_bar = {e: None for e in self.ENGS}
        ndma = {e: 0 for e in self.ENGS}
        base = {}
        acc = 0
        for e in self.ENGS:
            base[e] = acc
            acc += DSEM_POOLS[e]
        nsem_total = acc
        for i, op in enumerate(ops):
            if op.bar:
                deps = dict((j, "BAR") for j in last_on.values())
                for j in last_dma_on_sem.values():
                    deps[j] = "BAR"
                for e in self.ENGS:
                    pending_bar[e] = deps
                continue
            deps = {}
            if pending_bar[op.eng] is not None and not op.nobar:
                deps.update(pending_bar[op.eng])
                pending_bar[op.eng] = None
            for r in op.reads:
                if r.lw is not None:
                    deps[r.lw] = "RAW"
            for r in op.writes:
                if r.lw is not None and deps.get(r.lw) != "RAW":
                    deps[r.lw] = "WAW"
                for j in r.readers:
                    deps.setdefault(j, "WAR")
            deps.pop(i, None)
            for r in op.reads:
                r.readers.append(i)
            for r in op.writes:
                r.lw = i
                r.readers = []
            need = {}
            for j, kind in deps.items():
                oj = ops[j]
                if oj.dma:
                    need[j] = kind
                    continue
                if oj.eng == op.eng and not op.dma:
                    if op.eng == "pe" or kind == "BAR":
                        continue
                need[j] = kind
                oj.needs_inc = True
            op.deps = need
            if op.dma:
                k = ndma[op.eng]
                npool = DSEM_POOLS[op.eng]
                op.dsem = base[op.eng] + k % npool
                op.dval = 16 * (k // npool + 1)
                if not op.nobar:
                    last_dma_on_sem[op.dsem] = i
                ndma[op.eng] = k + 1
            elif not op.nobar:
                last_on[op.eng] = i
        cnt = {e: 0 for e in self.ENGS}
        for op in ops:
            if op.bar or op.dma:
                continue
            if op.needs_inc:
                cnt[op.eng] += 1
                op.cnt = cnt[op.eng]
        self.ndma = ndma
        self.counts = cnt
        final_vals = {}
        for op in ops:
            if not op.bar and op.dma:
                final_vals[op.dsem] = max(final_vals.get(op.dsem, 0), op.dval)
        with contextlib.ExitStack() as st:
            esem = {e: st.enter_context(nc.semaphore("s_" + e)) for e in self.ENGS}
            dsem = [st.enter_context(nc.semaphore("d_%d" % k)) for k in range(nsem_total)]
            block = st.enter_context(nc.Block())
            handles = {"pe": block.tensor, "act": block.scalar, "dve": block.vector, "pool": block.gpsimd,
                       "sp": block.sync}

            def make_body(ename):
                def body(eng):
                    waited = {}

                    def wait(sem, key, val):
                        if waited.get(key, 0) >= val:
                            return
                        eng.wait_ge(sem, val)
                        waited[key] = val

                    for op in ops:
                        if op.bar or op.eng != ename:
                            continue
                        for j in op.deps:
                            oj = ops[j]
                            if oj.dma:
                                wait(dsem[oj.dsem], ("d", oj.dsem), oj.dval)
                            else:
                                wait(esem[oj.eng], ("e", oj.eng), oj.cnt)
                        if op.dma:
                            if op.dval > 16:
                                wait(dsem[op.dsem], ("d", op.dsem), op.dval - 16)
                            ins = op.fn(eng)
                            ins.then_inc(dsem[op.dsem], 16)
                        else:
                            ins = op.fn(eng)
                            if op.needs_inc:
                                ins.then_inc(esem[ename], 1)
                    if ename == "sp":
                        for k, v in sorted(final_vals.items()):
                            wait(dsem[k], ("d", k), v)
                return body

            for ename in self.ENGS:
                handles[ename](make_body(ename))


def _consts():
    bf = ml_dtypes.bfloat16
    ident = np.eye(128, dtype=np.float32)
    bones = np.zeros((128, 128), np.float32)
    bones[:64, :64] = 1.0
    bones[64:, 64:] = 1.0
    ones = np.ones((128, 128), np.float32)
    rot = np.zeros((128, 128), np.float32)
    for hb in (0, 64):
        for e in range(8):
            rot[hb + e + 8, hb + e] = -1.0
            rot[hb + e, hb + e + 8] = 1.0
    j = np.arange(128)[:, None]
    i = np.arange(128)[None, :]
    mcur = (j <= i).astype(np.float32)
    mprev = (j >= i).astype(np.float32)
    mask4 = np.concatenate([mcur, mprev, mcur, mprev], axis=1)
    cb = np.concatenate([ident, bones, ones, rot, mask4], axis=1).astype(bf)
    inv_freq = (500000.0 ** (-(np.arange(0, 16, 2, dtype=np.float32) / np.float32(16)))).astype(np.float32)
    ang = np.arange(T, dtype=np.float32)[:, None] * inv_freq[None, :]
    cos, sin = np.cos(ang).astype(np.float32), np.sin(ang).astype(np.float32)
    ctab = np.ones((128, T), np.float32)
    stab = np.zeros((128, T), np.float32)
    for hb in (0, 64):
        for e in range(16):
            ctab[hb + e] = cos[:, e % 8]
            stab[hb + e] = sin[:, e % 8]
    tabs = np.zeros((3, 2, 128, T), np.float32)
    for g, (win, d) in enumerate(DIL):
        nsub = T // d
        pos = np.arange(T)
        tok = (pos % nsub) * d + pos // nsub
        tabs[g, 0] = ctab[:, tok]
        tabs[g, 1] = stab[:, tok]
    return cb, tabs


C_ID, C_BONES, C_ONES, C_ROT, C_MASK = 0, 128, 256, 384, 512


def build(debug=False):
    nc = bass.Bass("TRN2", target_bir_lowering=False)
    P = Prog(nc)
    st = contextlib.ExitStack()

    def MM(out, lhsT, rhs, start, stop, reads, writes):
        P.op("pe", lambda e: e.matmul(out, lhsT=lhsT, rhs=rhs, start=start, stop=stop), reads, writes)

    def TR(out, in_, identity, reads, writes):
        P.op("pe", lambda e: e.transpose(out=out, in_=in_, identity=identity), reads, writes)

    def ACT(out, in_, func, reads, writes, **kw):
        P.op("act", lambda e: e.activation(out=out, in_=in_, func=func, **kw), reads, writes)

    def TT(eng, out, in0, in1, op, reads, writes):
        P.op(eng, lambda e: e.tensor_tensor(out=out, in0=in0, in1=in1, op=op), reads, writes)

    def TS(eng, out, in0, s1, s2, op0, op1, reads, writes):
        if s2 is None:
            P.op(eng, lambda e: e.tensor_scalar(out=out, in0=in0, scalar1=s1, scalar2=None, op0=op0), reads, writes)
        else:
            P.op(eng, lambda e: e.tensor_scalar(out=out, in0=in0, scalar1=s1, scalar2=s2, op0=op0, op1=op1), reads,
                 writes)

    def STT(out, in0, scalar, in1, op0, op1, reads, writes):
        P.op("dve", lambda e: e.scalar_tensor_tensor(out=out, in0=in0, scalar=scalar, in1=in1, op0=op0, op1=op1),
             reads, writes)

    def RECIP(out, in_, reads, writes):
        P.op("dve", lambda e: e.reciprocal(out=out, in_=in_), reads, writes)

    def COPY(eng, out, in_, reads, writes):
        P.op(eng, lambda e: e.tensor_copy(out=out, in_=in_), reads, writes)

    def MEMSET(eng, ap, val, writes):
        P.op(eng, lambda e: e.memset(ap, val), [], writes)

    def skew(n_items, stages, lags, reverse=False):
        order = list(zip(stages, lags))
        if reverse:
            order = order[::-1]
        for step in range(n_items + max(lags)):
            for fn, lag in order:
                i = step - lag
                if 0 <= i < n_items:
                    fn(i)

    def din(name, shape, dt=F32):
        return nc.dram_tensor(name, list(shape), dt, kind="ExternalInput").ap()

    x_d = din("x", [T, D])
    mem_d = din("mem", [256, D])
    g_mix_d = din("g_mix", [D])
    w_in_d = din("w_in", [D, IN_W])
    b_f_d = din("b_f", [8])
    g_qA_d = din("g_qA", [64])
    g_kA_d = din("g_kA", [64])
    g_qB_d = din("g_qB", [64])
    g_kB_d = din("g_kB", [64])
    g_mem_d = din("g_mem", [D])
    w_mkv_d = din("w_mem_kv", [D, 1024])
    g_qM_d = din("g_qM", [128])
    g_kM_d = din("g_kM", [128])
    w_gate_d = din("w_gate", [D, 3072])
    b_gate_d = din("b_gate", [3072])
    w_bra_d = din("w_br_a", [256, D])
    w_brb_d = din("w_br_b", [512, D])
    w_brm_d = din("w_br_m", [512, D])
    w_out_d = din("w_out", [D, D])
    g_mlp_d = din("g_mlp", [D])
    w_up_d = din("w_up", [D, 4096])
    w_down_d = din("w_down", [4096, D])
    cb_d = din("cb", [128, 1024], BF16)
    tabs_d = din("tabs", [3, 2, 128, T])
    out_d = nc.dram_tensor("out", [T, D], F32, kind="ExternalOutput").ap()

    def scratch(name, shape, dt=BF16, dbg=False):
        kind = "ExternalOutput" if (debug and dbg) else "Internal"
        return nc.dram_tensor(name, list(shape), dt, kind=kind).ap()

    ysc = scratch("ysc", [1280, T], dbg=True)
    hsc = scratch("hsc", [128, 8, T])
    augsc = scratch("augsc", [8, 2, 6, T])
    wsc = {
        "in": scratch("wsc_in", [128, 8, IN_W]),
        "mkv": scratch("wsc_mkv", [128, 8, 1024]),
        "gate": scratch("wsc_gate", [8, 128, 8, 384]),
        "br": scratch("wsc_br", [8, 128, 10, 128]),
        "out": scratch("wsc_out", [128, 8, 1024]),
        "up": scratch("wsc_up", [128, 8, 4096]),
        "down": scratch("wsc_down", [128, 32, 1024]),
    }

    def sbt(name, shape, dt=F32):
        return st.enter_context(nc.sbuf_tensor(name, list(shape), dt))

    cb = sbt("cb_sb", [128, 1024], BF16)
    r_cb = R("cb")
    gcols = sbt("gcols", [128, 64])
    r_g = R("gcols")
    ARENA_COLS = 183 * 512
    arena = sbt("arena", [128, ARENA_COLS], BF16)

    class Arena:
        def __init__(self):
            self.top = 0

        def alloc(self, ncols, dt=BF16, parts=128):
            nb = ncols * (2 if dt == BF16 else 4)
            nb = (nb + 63) // 64 * 64
            o = self.top
            self.top += nb // 2
            assert self.top <= ARENA_COLS, ("arena overflow", self.top)
            ap = arena[:, o:o + nb // 2]
            if dt != BF16:
                ap = ap.bitcast(dt)
            return ap[:, 0:ncols]

    A = Arena()
    banks = [st.enter_context(nc.psum_tensor("bank%d" % i, [128, 512], F32)) for i in range(8)]
    rb = [R("bank%d" % i) for i in range(8)]

    ident = cb[:, C_ID:C_ID + 128]
    bones = cb[:, C_BONES:C_BONES + 128]
    ones_bf = cb[:, C_ONES:C_ONES + 128]
    rotm = cb[:, C_ROT:C_ROT + 128]
    mask4 = cb[:, C_MASK:C_MASK + 512]

    P.dma(cb[:], cb_d, writes=[r_cb])

    GC_GMIX, GC_GMLP, GC_GMEM = 0, 8, 16
    GC_QA, GC_KA, GC_QB, GC_KB, GC_QM, GC_KM = 24, 25, 26, 27, 28, 29
    GC_BGATE = 32
    GC_NBF = 56
    GC_RAW = 57
    r_gs = []

    def _rgn():
        r_gs.append(R())
        return r_gs[-1]

    def load_params():
        P.dma(gcols[:, GC_GMIX:GC_GMIX + 8], g_mix_d.rearrange("(kc p) -> p kc", p=128), writes=[_rgn()], allow_slow_non_contiguous=True)
        P.dma(gcols[:, GC_GMLP:GC_GMLP + 8], g_mlp_d.rearrange("(kc p) -> p kc", p=128), writes=[_rgn()], allow_slow_non_contiguous=True)
        P.dma(gcols[:, GC_GMEM:GC_GMEM + 8], g_mem_d.rearrange("(kc p) -> p kc", p=128), writes=[_rgn()], allow_slow_non_contiguous=True)
        P.dma(gcols[:, GC_BGATE:GC_BGATE + 24], b_gate_d.rearrange("(c p) -> p c", p=128), writes=[_rgn()], allow_slow_non_contiguous=True)
        for col, gd in ((0, g_qA_d), (1, g_kA_d), (2, g_qB_d), (3, g_kB_d)):
            for hb in (0, 64):
                P.dma(gcols[hb:hb + 64, GC_RAW + col:GC_RAW + col + 1], gd.rearrange("(p o) -> p o", o=1),
                      writes=[_rgn()], allow_slow_non_contiguous=True)
        P.dma(gcols[:, GC_RAW + 4:GC_RAW + 5], g_qM_d.rearrange("(p o) -> p o", o=1), writes=[_rgn()], allow_slow_non_contiguous=True)
        P.dma(gcols[:, GC_RAW + 5:GC_RAW + 6], g_kM_d.rearrange("(p o) -> p o", o=1), writes=[_rgn()], allow_slow_non_contiguous=True)
        P.dma(gcols[0:8, GC_NBF:GC_NBF + 1], b_f_d.rearrange("(p o) -> p o", o=1), writes=[_rgn()], allow_slow_non_contiguous=True)

    def load_params_ops():
        def gscale(dst, src, s):
            P.op("dve", lambda e: e.tensor_scalar(out=gcols[:, dst:dst + 1], in0=gcols[:, src:src + 1],
                                                   scalar1=float(s), scalar2=None, op0=ALU.mult), [r_g] + r_gs, [r_g])
        gscale(GC_QA, GC_RAW + 0, 0.125)
        gscale(GC_KA, GC_RAW + 1, 1.0)
        gscale(GC_QB, GC_RAW + 2, 0.125)
        gscale(GC_KB, GC_RAW + 3, 1.0)
        gscale(GC_QM, GC_RAW + 4, 128.0 ** -0.5)
        gscale(GC_KM, GC_RAW + 5, 1.0)
        P.op("dve", lambda e: e.tensor_scalar(out=gcols[0:8, GC_NBF:GC_NBF + 1], in0=gcols[0:8, GC_NBF:GC_NBF + 1],
                                           scalar1=-1.0, scalar2=None, op0=ALU.mult), [r_g], [r_g])

    Rw = {}
    jobs = []

    def chunks_of(n, cw):
        return [(c, min(c + cw, n)) for c in range(0, n, cw)]

    WCH = {"gate": chunks_of(3072, 512),
           "br": chunks_of(1024, 512), "out": chunks_of(1024, 512), "up": chunks_of(4096, 512),
           "down": chunks_of(1024, 512)}

    def add_jobs(name, src, nkc, gc, src_row0=0, kc0=0):
        for (c0, c1) in WCH[name]:
            for kc in range(nkc):
                r = R("w_%s_%d_%d" % (name, kc0 + kc, c0))
                Rw[(name, kc0 + kc, c0)] = r
                if name == "gate":
                    br_, cc0 = c0 // 1024, (c0 % 1024) // 128
                    dst = wsc[name][cc0:cc0 + 4, :, kc0 + kc, br_ * 128:(br_ + 1) * 128].rearrange("c p n -> p c n")
                elif name == "br":
                    cc0 = c0 // 128
                    dst = wsc[name][cc0:cc0 + 4, :, kc0 + kc, :].rearrange("c p n -> p c n")
                else:
                    dst = wsc[name][:, kc0 + kc, c0:c1]
                jobs.append((src[src_row0 + kc * 128:src_row0 + (kc + 1) * 128, c0:c1],
                             None if gc is None else gc + kc, dst, r, c1 - c0))

    add_jobs("gate", w_gate_d, 8, GC_GMIX)
    add_jobs("br", w_bra_d, 2, None, kc0=0)
    add_jobs("br", w_brb_d, 4, None, kc0=2)
    add_jobs("br", w_brm_d, 4, None, kc0=6)
    add_jobs("out", w_out_d, 8, None)
    add_jobs("up", w_up_d, 8, GC_GMLP)
    add_jobs("down", w_down_d, 32, None)

    NSTG = 3
    stg32 = [sbt("stg32_%d" % i, [128, 1024]) for i in range(NSTG)]
    r_s32 = [R() for _ in range(NSTG)]
    NSTGJ = 4
    jst32 = [sbt("jst32_%d" % i, [128, 512]) for i in range(NSTGJ)]
    jst16 = [sbt("jst16_%d" % i, [128, 512], BF16) for i in range(2)]
    r_j32 = [R() for _ in range(NSTGJ)]
    r_j16 = [R() for _ in range(2)]
    job_state = {"i": 0, "s": 0}

    def _job_load(i):
        src, gc, dst, r, cw = jobs[i]
        s = i % NSTGJ
        P.dma(jst32[s][:, 0:cw], src, writes=[r_j32[s]], eng="pool", nobar=True)

    def pump(n):
        for _ in range(n):
            i = job_state["i"]
            if i >= len(jobs):
                return
            if i == 0:
                _job_load(0)
                if len(jobs) > 1:
                    _job_load(1)
            if i + 2 < len(jobs):
                _job_load(i + 2)
            job_state["i"] = i + 1
            src, gc, dst, r, cw = jobs[i]
            s = i % NSTGJ
            a32, a16 = jst32[s][:, 0:cw], jst16[i % 2][:, 0:cw]
            if gc is None:
                P.op("pool", lambda e, a16=a16, a32=a32: e.tensor_copy(out=a16, in_=a32), [r_j32[s]], [r_j16[i % 2]],
                     nobar=True)
            else:
                P.op("pool", lambda e, a16=a16, a32=a32, gc=gc, cw=cw: e.tensor_tensor(
                    out=a16, in0=a32, in1=gcols[:, gc:gc + 1].to_broadcast([128, cw]), op=ALU.mult),
                    [r_j32[s], r_g], [r_j16[i % 2]], nobar=True)
            if len(dst.shape) == 3:
                a16 = a16.rearrange("p (c n) -> p c n", n=128)
            P.dma(dst, a16, reads=[r_j16[i % 2]], writes=[r], eng="pool", nobar=True)

    def wdirect(dst, src, gc, ncols, nk=8, **kw):
        s = job_state["s"] % NSTG
        job_state["s"] += 1
        a32 = stg32[s][:, 0:nk * ncols].rearrange("p (k c) -> p k c", k=nk)
        P.dma(a32, src.rearrange("(k p) c -> p k c", p=128), writes=[r_s32[s]], **kw)
        return a32, s

    def wdirect_cast(dst, a32, s, gc, ncols, writes, nk=8):
        if gc is None:
            P.op("dve", lambda e: e.tensor_copy(out=dst, in_=a32), [r_s32[s]], writes)
        else:
            P.op("dve", lambda e: e.tensor_tensor(
                out=dst, in0=a32, in1=gcols[:, gc:gc + nk].unsqueeze(2).to_broadcast([128, nk, ncols]), op=ALU.mult),
                [r_s32[s], r_g], writes)

    def wload_in(dst, c0, ncols, writes, **kw):
        a32, s = wdirect(dst, w_in_d[:, c0:c0 + ncols], GC_GMIX, ncols, **kw)
        wdirect_cast(dst, a32, s, GC_GMIX, ncols, writes)

    def wload_in_dma(c0, ncols, **kw):
        return wdirect(None, w_in_d[:, c0:c0 + ncols], GC_GMIX, ncols, **kw)

    def wload_in_cast(dst, tok, ncols, writes):
        wdirect_cast(dst, tok[0], tok[1], GC_GMIX, ncols, writes)

    def load_w(dst, name, kc0, kc1, c0, c1):
        reads = []
        for (a, b) in WCH[name]:
            if a < c1 and b > c0:
                for kc in range(kc0, kc1):
                    reads.append(Rw[(name, kc, a)])
        if name in ("gate", "br"):
            return None, reads
        return wsc[name][:, kc0:kc1, c0:c1], reads

    hT = A.alloc(8 * T).rearrange("p (k t) -> p k t", k=8)
    r_hT = [R("hT%d" % i) for i in range(NT)]
    mark_h = A.top

    def hT_reads(t0, t1):
        return [r_hT[i] for i in range(t0 // 128, (t1 - 1) // 128 + 1)]

    def rms_rows_to_T(src, src_reads, ss_col, rs_col, stats, r_stats, xn, r_xn, bank_i, dstT, dst_writes, junk,
                      r_junk):
        ACT(junk, src, AF.Square, src_reads + [r_junk], [r_junk, r_stats], accum_out=stats[:, ss_col:ss_col + 1])
        ACT(stats[:, rs_col:rs_col + 1], stats[:, ss_col:ss_col + 1], AF.Sqrt, [r_stats], [r_stats], scale=1.0 / D,
            bias=stats[:, 0:1])
        RECIP(stats[:, rs_col:rs_col + 1], stats[:, rs_col:rs_col + 1], [r_stats], [r_stats])
        TS("dve", xn, src, stats[:, rs_col:rs_col + 1], None, ALU.mult, None, src_reads + [r_stats], [r_xn])
        rows_to_T(xn, r_xn, bank_i, dstT, dst_writes)

    def rms_part1(src, src_reads, ss_col, rs_col, stats, r_stats, xn, r_xn, junk, r_junk):
        ACT(junk, src, AF.Square, src_reads + [r_junk], [r_junk, r_stats], accum_out=stats[:, ss_col:ss_col + 1])
        ACT(stats[:, rs_col:rs_col + 1], stats[:, ss_col:ss_col + 1], AF.Ln, [r_stats], [r_stats], scale=1.0 / D,
            bias=stats[:, 0:1])
        ACT(stats[:, rs_col:rs_col + 1], stats[:, rs_col:rs_col + 1], AF.Exp, [r_stats], [r_stats], scale=-0.5)
        TS("dve", xn, src, stats[:, rs_col:rs_col + 1], None, ALU.mult, None, src_reads + [r_stats], [r_xn])

    def rows_to_T(xn, r_xn, bank_i, dstT, dst_writes):
        bk = banks[bank_i][:].bitcast(BF16)
        for k in range(8):
            TR(bk[:, k * 128:(k + 1) * 128], xn[:, k * 128:(k + 1) * 128], ident, [r_xn, r_cb], [rb[bank_i]])
        COPY("dve", dstT, bk[:, 0:1024].rearrange("p (k t) -> p k t", k=8), [rb[bank_i]], dst_writes)

    statsA = A.alloc(2 * NT + 4, F32)
    r_statsA = R("statsA")
    MEMSET("dve", statsA[:, 0:1], EPS, [r_statsA])
    NXB = 6
    xts = [A.alloc(D, F32) for _ in range(NXB)]
    r_xts = [R() for _ in range(NXB)]
    r_xh = [[R(), R()] for _ in range(NXB)]
    xns = [A.alloc(D) for _ in range(2)]
    r_xns = [R() for _ in range(2)]
    junkA = A.alloc(D)
    r_junkA = R()
    pass
    wfl = A.alloc(8 * 8).rearrange("p (k c) -> p k c", k=8)
    r_wfl = R()
    spb = [A.alloc(512, F32) for _ in range(2)]
    csb = [A.alloc(512, F32) for _ in range(2)]
    r1b = [A.alloc(512, F32) for _ in range(2)]
    posb = [A.alloc(3 * 512).rearrange("p (j t) -> p j t", j=3) for _ in range(2)]
    negb = [A.alloc(3 * 512).rearrange("p (j t) -> p j t", j=3) for _ in range(2)]
    r_spb, r_csb, r_r1b, r_posb, r_negb = [R(), R()], [R(), R()], [R(), R()], [R(), R()], [R(), R()]
    onesr = A.alloc(T)
    r_onesr = R()
    ra_ = []

    p2tok = []

    def p2_setup():
        p2tok.append(wload_in_dma(OFF_FL, 8, allow_slow_non_contiguous=True))
        MEMSET("dve", onesr[0:8, :], 1.0, [r_onesr])
        for j in range(3):
            ra_.append(R())
            P.dma(augsc[:, 0, 3 + j, :], onesr[0:8, :], reads=[r_onesr], writes=[ra_[-1]])
            ra_.append(R())
            P.dma(augsc[:, 1, j, :], onesr[0:8, :], reads=[r_onesr], writes=[ra_[-1]])

    def p2_tg(tg):
        b_ = 2 + tg % 2
        k = tg % 2
        sp_, cs_, r1_, pos_, neg_ = spb[k][0:8, :], csb[k][0:8, :], r1b[k][0:8, :], posb[k][0:8, :, :], negb[k][0:8, :, :]
        for kc in range(8):
            MM(banks[b_][0:8, :], wfl[:, kc, :], hT[:, kc, tg * 512:(tg + 1) * 512], kc == 0, kc == 7,
               [r_wfl] + hT_reads(tg * 512, tg * 512 + 512), [rb[b_]])
        ACT(sp_, banks[b_][0:8, :], AF.Exp, [rb[b_], r_g], [r_spb[k]], scale=-1.0, bias=gcols[0:8, GC_NBF:GC_NBF + 1])
        ACT(sp_, sp_, AF.Ln, [r_spb[k]], [r_spb[k]], scale=1.0, bias=1.0)
        init = 0.0 if tg == 0 else csb[1 - k][0:8, 511:512]
        rd = [r_spb[k]] + ([] if tg == 0 else [r_csb[1 - k]])
        P.op("dve", lambda e: e.tensor_tensor_scan(out=cs_, data0=sp_, data1=sp_, initial=init, op0=ALU.add,
                                                    op1=ALU.max), rd, [r_csb[k]])
        COPY("dve", pos_[:, 0, :], cs_, [r_csb[k]], [r_posb[k]])
        TT("dve", r1_, cs_, pos_[:, 0, :], ALU.subtract, [r_csb[k], r_posb[k]], [r_r1b[k]])
        COPY("dve", pos_[:, 1, :], r1_, [r_r1b[k]], [r_posb[k]])
        TT("dve", sp_, r1_, pos_[:, 1, :], ALU.subtract, [r_r1b[k], r_posb[k]], [r_spb[k]])
        COPY("dve", pos_[:, 2, :], sp_, [r_spb[k]], [r_posb[k]])
        TS("dve", neg_, pos_, -1.0, None, ALU.mult, None, [r_posb[k]], [r_negb[k]])
        ts_ = slice(tg * 512, (tg + 1) * 512)
        ra_.append(R())
        P.dma(augsc[:, 0, 0:3, ts_], neg_, reads=[r_negb[k]], writes=[ra_[-1]])
        ra_.append(R())
        P.dma(augsc[:, 1, 3:6, ts_], pos_, reads=[r_posb[k]], writes=[ra_[-1]])

    def a3(i):
        if i % 4 == 3:
            p2_tg(i // 4)

    def a0(i):
        for hf in range(2):
            P.dma(xts[i % NXB][:, hf * 512:(hf + 1) * 512], x_d[i * 128:(i + 1) * 128, hf * 512:(hf + 1) * 512],
                  writes=[r_xh[i % NXB][hf]])
        if i == 3:
            load_params()
        if i == 4:
            p2_setup()
        if i == 8:
            load_params_ops()
            wload_in_cast(wfl, p2tok[0], 8, [r_wfl])

    def a1(i):
        rms_part1(xts[i % NXB], r_xh[i % NXB], 1 + 2 * i, 2 + 2 * i, statsA, r_statsA, xns[i % 2], r_xns[i % 2], junkA,
                  r_junkA)

    def a2(i):
        rows_to_T(xns[i % 2], r_xns[i % 2], i % 2, hT[:, :, i * 128:(i + 1) * 128], [r_hT[i]])

    skew(NT, [a0, a1, a2, a3], [0, 4, 5, 6])
    r_hsc = [R("hsc%d" % k) for k in range(8)]
    hsc_todo = list(range(8))

    def spill_hsc(n):
        for _ in range(n):
            if hsc_todo:
                k = hsc_todo.pop(0)
                P.dma(hsc[:, k, :], hT[:, k, :], reads=r_hT, writes=[r_hsc[k]])
    P.barrier()
    A.top = mark_h

    def proj_fm(bank_i, wt, wreads, tg):
        for kc in range(8):
            MM(banks[bank_i][:, :], wt[:, kc, :], hT[:, kc, tg * 512:(tg + 1) * 512], kc == 0, kc == 7,
               wreads + hT_reads(tg * 512, tg * 512 + 512), [rb[bank_i]])

    def head_norm_a(qbank, ssbank, onesmat, tm, ncol=512):
        ACT(tm["sq"][:, 0:ncol], banks[qbank][:, 0:ncol], AF.Square, [rb[qbank]], [tm["r_sq"]])
        MM(banks[ssbank][:, 0:ncol], onesmat, tm["sq"][:, 0:ncol], True, True, [tm["r_sq"], r_cb], [rb[ssbank]])

    def head_norm_b(ssbank, inv_n, tm, ncol=512):
        ACT(tm["ms"][:, 0:ncol], banks[ssbank][:, 0:ncol], AF.Ln, [rb[ssbank]], [tm["r_ms"]], scale=float(inv_n), bias=EPS)
        ACT(tm["rstd"][:, 0:ncol], tm["ms"][:, 0:ncol], AF.Exp, [tm["r_ms"]], [tm["r_rstd"]], scale=-0.5)

    def head_norm(qbank, ssbank, onesmat, inv_n, tm, ncol=512):
        head_norm_a(qbank, ssbank, onesmat, tm, ncol)
        head_norm_b(ssbank, inv_n, tm, ncol)

    def alloc_norm_tmps(n=2, rope=False):
        t = []
        for _ in range(n):
            dct = dict(sq=A.alloc(512), r_sq=R(), ms=A.alloc(512, F32), r_ms=R(), rstd=A.alloc(512, F32),
                       r_rstd=R())
            dct.update(qn=A.alloc(512), r_qn=R())
            if rope:
                dct.update(t1=A.alloc(512, F32), r_t1=R(), t2=A.alloc(512, F32), r_t2=R())
            t.append(dct)
        return t

    mark_p = A.top

    _rysc = {}

    def ry(key):
        if key not in _rysc:
            _rysc[key] = R("ysc" + str(key))
        return _rysc[key]

    def walloc():
        return A.alloc(8 * 128).rearrange("p (k c) -> p k c", k=8)


    def phase_dil():
        ctabs = [A.alloc(512, F32) for _ in range(2)]
        stabs = [A.alloc(512, F32) for _ in range(2)]
        r_tabs = [R(), R()]
        tmps = alloc_norm_tmps(2, rope=True)
        QT = A.alloc(T)
        KT = A.alloc(T)
        r_QT = [R() for _ in range(NTG)]
        r_KT = [R() for _ in range(NTG)]
        Vt = A.alloc(32 * 256).rearrange("p (b h c) -> p b h c", b=32, h=2)
        r_Vt = [R() for _ in range(8)]
        MEMSET("dve", Vt[:, :, :, 64:128], 1.0, r_Vt)
        acc = [A.alloc(T, F32) for _ in range(2)]
        r_acc = [[R() for _ in range(NTG)] for _ in range(2)]
        wq, wk, wv = [walloc(), walloc()], [walloc(), walloc()], [walloc(), walloc()]
        r_wq, r_wk, r_wv = [R(), R()], [R(), R()], [R(), R()]
        PT = [A.alloc(512) for _ in range(4)]
        r_PT = [R() for _ in range(4)]
        recs = [A.alloc(1024, F32)] * 2
        r_recs = [R()] * 2
        yt = [A.alloc(1024) for _ in range(2)]
        r_yt = [R(), R()]
        iters = [(hp, g) for hp in range(2) for g in range(3)]

        wtok = {}

        def issue_w_dma(k):
            hp_, g_ = iters[k]
            c0_ = g_ * 256 + hp_ * 128
            wtok[k] = [wload_in_dma(off + c0_, 128) for off in (OFF_QA, OFF_KA, OFF_VA)]

        def issue_w_cast(k):
            wb_ = k % 2
            for tok, (wt_, rw_) in zip(wtok[k], ((wq[wb_], r_wq[wb_]), (wk[wb_], r_wk[wb_]), (wv[wb_], r_wv[wb_]))):
                wload_in_cast(wt_, tok, 128, [rw_])

        issue_w_dma(0)
        issue_w_cast(0)
        pending_norm = []
        late_units = []

        def mk_unit(hp, hh, c):
            def unit():
                sl = slice(c * 1024, (c + 1) * 1024)
                ra = [r_acc[hh][2 * c], r_acc[hh][2 * c + 1]]
                rc, r_rc = recs[c % 2], r_recs[c % 2]
                ACT(rc[0:64, :], acc[hh][64:128, sl], AF.Ln, ra, [r_rc])
                ACT(rc[0:64, :], rc[0:64, :], AF.Exp, [r_rc], [r_rc], scale=-1.0)
                yb = yt[c % 2]
                TT("dve", yb[0:64, :], acc[hh][0:64, sl], rc[0:64, :], ALU.mult, ra + [r_rc], [r_yt[c % 2]])
                row0 = (hp * 2 + hh) * 64
                P.dma(ysc[row0:row0 + 64, sl], yb[0:64, :], reads=[r_yt[c % 2]],
                      writes=[ry(("a", hp * 2 + hh, 2 * c)), ry(("a", hp * 2 + hh, 2 * c + 1))])
            return unit
        for it, (hp, g) in enumerate(iters):
            if True:
                win, d = DIL[g]
                nsub = T // d
                nb = nsub // 128
                wb = it % 2
                if it + 1 < len(iters):
                    issue_w_dma(it + 1)
                spill_hsc(2)
                pump(0)
                chains = [(wq[wb], r_wq[wb], GC_QA, QT, r_QT), (wk[wb], r_wk[wb], GC_KA, KT, r_KT)]

                def st0(i):
                    wt, rw, gc, dst, r_dst = chains[i // NTG]
                    proj_fm(i % 3, wt, [rw], i % NTG)

                def st1a(i):
                    head_norm_a(i % 3, 3 + i % 2, bones, tmps[i % 2])

                def st1(i):
                    wt, rw, gc, dst, r_dst = chains[i // NTG]
                    tm = tmps[i % 2]
                    qb_ = i % 3
                    tg = i % NTG
                    P.dma(ctabs[i % 2], tabs_d[0, 0, :, tg * 512:(tg + 1) * 512], writes=[r_tabs[i % 2]])
                    P.dma(stabs[i % 2], tabs_d[0, 1, :, tg * 512:(tg + 1) * 512], writes=[r_tabs[i % 2]])
                    head_norm_b(3 + i % 2, 1.0 / 64, tm)
                    STT(tm["qn"], banks[qb_][:, :], gcols[:, gc:gc + 1], tm["rstd"], ALU.mult, ALU.mult,
                        [rb[qb_], tm["r_rstd"], r_g], [tm["r_qn"]])

                def st2(i):
                    wt, rw, gc, dst, r_dst = chains[i // NTG]
                    tg = i % NTG
                    tm = tmps[i % 2]
                    rb_ = 5 + i % 2
                    ctab, stab, r_tab = ctabs[i % 2], stabs[i % 2], r_tabs[i % 2]
                    MM(banks[rb_][:, :], rotm, tm["qn"], True, True, [tm["r_qn"], r_cb], [rb[rb_]])
                    TT("dve", tm["t1"], banks[rb_][:, :], stab, ALU.mult, [rb[rb_], r_tab], [tm["r_t1"]])
                    TT("dve", tm["t2"], tm["qn"], ctab, ALU.mult, [tm["r_qn"], r_tab], [tm["r_t2"]])
                    n0 = tg * 512 // d
                    dv = dst.rearrange("p (r n) -> p r n", r=d)[:, :, n0:n0 + 512 // d]
                    wr = sorted(set((r * nsub + n0) // 512 for r in range(d)))
                    TT("pool", dv, tm["t1"].rearrange("p (n r) -> p r n", r=d),
                       tm["t2"].rearrange("p (n r) -> p r n", r=d), ALU.add, [tm["r_t1"], tm["r_t2"]],
                       [r_dst[w] for w in wr])

                def sv(i):
                    if i % 2:
                        return
                    B4 = i // 2
                    bk = 7
                    for q in range(4):
                        B = B4 * 4 + q
                        r, b = B // nb, B % nb
                        t0 = r + d * 128 * b
                        t1 = t0 + d * 127 + 1
                        for kc in range(8):
                            MM(banks[bk][:, q * 128:(q + 1) * 128], hT[:, kc, t0:t1:d], wv[wb][:, kc, :], kc == 0,
                               kc == 7, [r_wv[wb]] + hT_reads(t0, t1), [rb[bk]])
                    ACT(Vt[:, B4 * 4:(B4 + 1) * 4, :, 0:64],
                        banks[bk][:, :].rearrange("p (b h c) -> p b h c", b=4, h=2), AF.Copy, [rb[bk]], [r_Vt[B4]])

                def sn(i):
                    if pending_norm:
                        pending_norm.pop(0)()

                skew(2 * NTG, [st0, st1a, st1, sn, st2, sv], [0, 1, 2, 2, 3, 1])
                if it + 1 < len(iters):
                    issue_w_cast(it + 1)
                for hh in range(2):
                    hb = hh * 64

                    def emit_S(pi):
                        sbk = pi % 4
                        for s_, B in enumerate((2 * pi, 2 * pi + 1)):
                            n = 256 if B < 31 else 128
                            MM(banks[sbk][:, s_ * 256:s_ * 256 + n], KT[hb:hb + 64, B * 128:(B + 1) * 128],
                               QT[hb:hb + 64, B * 128:B * 128 + n], True, True,
                               [r_KT[B // 4], r_QT[B // 4], r_QT[(B * 128 + n - 1) // 512]], [rb[sbk]])
                        wdt = 512 if (2 * pi + 1) < 31 else 384
                        ACT(PT[sbk][:, 0:wdt], banks[sbk][:, 0:wdt], AF.Exp, [rb[sbk]], [r_PT[sbk]])
                        TT("dve", PT[sbk][:, 0:wdt], PT[sbk][:, 0:wdt], mask4[:, 0:wdt], ALU.mult, [r_PT[sbk], r_cb],
                           [r_PT[sbk]])

                    def emit_PV(Bq):
                        ob = 4 + (Bq // 4) % 4
                        col = (Bq % 4) * 128
                        terms = []
                        if (Bq % nb) != 0:
                            Bk = Bq - 1
                            terms.append((Bk, (Bk // 2) % 4, (Bk % 2) * 256 + 128))
                        terms.append((Bq, (Bq // 2) % 4, (Bq % 2) * 256))
                        for ti, (Bk, pt, pc) in enumerate(terms):
                            MM(banks[ob][:, col:col + 128], Vt[:, Bk, hh, :], PT[pt][:, pc:pc + 128], ti == 0,
                               ti == len(terms) - 1, [r_Vt[Bk // 4], r_PT[pt]], [rb[ob]])

                    def emit_evac(q4):
                        ob = 4 + q4 % 4
                        Av = acc[hh].rearrange("p (n r) -> p r n", r=d)
                        p0 = 512 * q4
                        if nsub >= 512:
                            r_, n0 = p0 // nsub, p0 % nsub
                            av = Av[:, r_, n0:n0 + 512]
                            src_ = banks[ob][:, :]
                            toks = [r_ + d * n0, r_ + d * (n0 + 511)]
                        else:
                            rr = 512 // nsub
                            r_ = p0 // nsub
                            av = Av[:, r_:r_ + rr, :]
                            src_ = banks[ob][:, :].rearrange("p (r n) -> p r n", r=rr)
                            toks = [r_, r_ + rr - 1 + d * (nsub - 1)]
                        ra = [r_acc[hh][w] for w in range(toks[0] // 512, toks[1] // 512 + 1)]
                        if g == 0:
                            ACT(av, src_, AF.Copy, [rb[ob]], ra)
                        else:
                            TT("dve", av, src_, av, ALU.add, [rb[ob]] + ra, ra)

                    emit_S(0)
                    emit_S(1)
                    for pi in range(16):
                        if pi + 2 < 16:
                            emit_S(pi + 2)
                        for Bq in (2 * pi, 2 * pi + 1):
                            emit_PV(Bq)
                            if Bq % 4 == 3:
                                emit_evac(Bq // 4)
                        if late_units and pi % 3 == 2:
                            late_units.pop(0)()
                    while late_units:
                        late_units.pop(0)()
                    if g == 2 and hh == 0:
                        for c in range(4):
                            late_units.append(mk_unit(hp, 0, c))
                if g == 2:
                    for c in range(4):
                        pending_norm.append(mk_unit(hp, 1, c))
        while pending_norm:
            pending_norm.pop(0)()

    phase_dil()
    P.barrier()
    A.top = mark_p

    def phase_fox():
        tmps = alloc_norm_tmps(2)
        QK = {}
        for nm in ("QA", "QB", "KA", "KB"):
            QK[nm] = (A.alloc(T), [R() for _ in range(NTG)], R())
        Vt = A.alloc(32 * 256).rearrange("p (b h c) -> p b h c", b=32, h=2)
        r_Vt = [R() for _ in range(8)]
        MEMSET("dve", Vt[:, :, :, 64:128], 1.0, r_Vt)
        wq, wk, wv = [walloc(), walloc()], [walloc(), walloc()], [walloc(), walloc()]
        r_wq, r_wk, r_wv = [R(), R()], [R(), R()], [R(), R()]
        NPT = 6
        PT = [A.alloc(512) for _ in range(NPT)]
        r_PT = [R() for _ in range(NPT)]
        rec = A.alloc(512, F32)
        r_rec = R()
        yt = [A.alloc(512) for _ in range(2)]
        r_yt = [R(), R()]
        def issue_w(hp_):
            wb_ = hp_ % 2
            wload_in(wq[wb_], OFF_QB + hp_ * 128, 128, [r_wq[wb_]])
            wload_in(wk[wb_], OFF_KB + hp_ * 128, 128, [r_wk[wb_]])
            wload_in(wv[wb_], OFF_VB + hp_ * 128, 128, [r_wv[wb_]])

        issue_w(0)
        for hp in range(4):
            wb = hp % 2
            for hh, (qn_, kn_) in enumerate((("QA", "KA"), ("QB", "KB"))):
                h = hp * 2 + hh
                P.dma(QK[qn_][0][64:70, :], augsc[h, 0, :, :], reads=ra_, writes=[QK[qn_][2]])
                P.dma(QK[kn_][0][64:70, :], augsc[h, 1, :, :], reads=ra_, writes=[QK[kn_][2]])
            chains = [(wq[wb], r_wq[wb], GC_QB, "QA", "QB"), (wk[wb], r_wk[wb], GC_KB, "KA", "KB")]

            def st0(i):
                wt, rw, gc, nA, nB = chains[i // NTG]
                proj_fm(i % 3, wt, [rw], i % NTG)

            def st1a(i):
                head_norm_a(i % 3, 3 + i % 2, bones, tmps[i % 2])

            def st2(i):
                wt, rw, gc, nA, nB = chains[i // NTG]
                tg = i % NTG
                COPY("dve", QK[nB][0][0:64, tg * 512:(tg + 1) * 512], tmps[i % 2]["qn"][64:128, :],
                     [tmps[i % 2]["r_qn"]], [QK[nB][1][tg]])

            def st1(i):
                wt, rw, gc, nA, nB = chains[i // NTG]
                tg = i % NTG
                tm = tmps[i % 2]
                qb_ = i % 3
                head_norm_b(3 + i % 2, 1.0 / 64, tm)
                sl = slice(tg * 512, (tg + 1) * 512)
                STT(QK[nA][0][0:64, sl], banks[qb_][0:64, :], gcols[0:64, gc:gc + 1], tm["rstd"][0:64, :], ALU.mult,
                    ALU.mult, [rb[qb_], tm["r_rstd"], r_g], [QK[nA][1][tg]])
                STT(tm["qn"][64:128, :], banks[qb_][64:128, :], gcols[64:128, gc:gc + 1], tm["rstd"][64:128, :],
                    ALU.mult, ALU.mult, [rb[qb_], tm["r_rstd"], r_g], [tm["r_qn"]])

            skew(2 * NTG, [st0, st1a, st1, st2], [0, 1, 2, 3])
            if hp + 1 < 4:
                issue_w(hp + 1)
            for B4 in range(8):
                bk = 6 + B4 % 2
                for q in range(4):
                    B = B4 * 4 + q
                    for kc in range(8):
                        MM(banks[bk][:, q * 128:(q + 1) * 128], hT[:, kc, B * 128:(B + 1) * 128], wv[wb][:, kc, :],
                           kc == 0, kc == 7, [r_wv[wb], r_hT[B]], [rb[bk]])
                ACT(Vt[:, B4 * 4:(B4 + 1) * 4, :, 0:64], banks[bk][:, :].rearrange("p (b h c) -> p b h c", b=4, h=2),
                    AF.Copy, [rb[bk]], [r_Vt[B4]])
            for hh, (qn_, kn_) in enumerate((("QA", "KA"), ("QB", "KB"))):
                h = hp * 2 + hh
                Qt, rQ, rQa = QK[qn_]
                Kt, rK, rKa = QK[kn_]
                blocks = []
                for tg in range(NTG):
                    for kb in range(4 * tg + 4):
                        blocks.append((tg, kb))
                nblk = len(blocks)

                def emit_S(bi):
                    tg, kb = blocks[bi]
                    j = max(0, kb - 4 * tg)
                    n = 512 - 128 * j
                    q0 = tg * 512 + 128 * j
                    sbk = bi % NPT
                    MM(banks[sbk][:, 0:n], Kt[0:70, kb * 128:(kb + 1) * 128], Qt[0:70, q0:q0 + n], True, True,
                       [rK[kb // 4], rKa, rQ[tg], rQa], [rb[sbk]])
                    ACT(PT[sbk][:, 0:n], banks[sbk][:, 0:n], AF.Exp, [rb[sbk]], [r_PT[sbk]])
                    if kb >= 4 * tg:
                        TT("dve", PT[sbk][:, 0:128], PT[sbk][:, 0:128], mask4[:, 0:128], ALU.mult, [r_PT[sbk], r_cb],
                           [r_PT[sbk]])

                def emit_PV(bi):
                    tg, kb = blocks[bi]
                    j = max(0, kb - 4 * tg)
                    n = 512 - 128 * j
                    sbk = bi % NPT
                    ob = 6 + tg % 2
                    last = (kb == 4 * tg + 3)
                    MM(banks[ob][:, 128 * j:512], Vt[:, kb, hh, :], PT[sbk][:, 0:n], kb == 0, last,
                       [r_Vt[kb // 4], r_PT[sbk]], [rb[ob]])
                    if last:
                        RECIP(rec[0:64, :], banks[ob][64:128, :], [rb[ob]], [r_rec])
                        yb = yt[tg % 2]
                        TT("dve", yb[0:64, :], banks[ob][0:64, :], rec[0:64, :], ALU.mult, [rb[ob], r_rec],
                           [r_yt[tg % 2]])
                        P.dma(ysc[256 + h * 64:256 + (h + 1) * 64, tg * 512:(tg + 1) * 512], yb[0:64, :],
                              reads=[r_yt[tg % 2]], writes=[ry(("b", h, tg))])

                LA = 4
                for bi in range(min(LA, nblk)):
                    emit_S(bi)
                for bi in range(nblk):
                    if bi + LA < nblk:
                        emit_S(bi + LA)
                    emit_PV(bi)
                pump(24)

    phase_fox()
    P.barrier()
    A.top = mark_p

    def phase_mem():
        tmps = alloc_norm_tmps(2)
        statsM = A.alloc(8, F32)
        r_statsM = R()
        MEMSET("dve", statsM[:, 0:1], EPS, [r_statsM])
        memT = A.alloc(8 * 256).rearrange("p (k t) -> p k t", k=8)
        r_memT = [R(), R()]
        mt = [A.alloc(D, F32) for _ in range(2)]
        r_mt = [R(), R()]
        mn = [A.alloc(D) for _ in range(2)]
        r_mn = [R(), R()]
        junk = A.alloc(D)
        r_junk = R()
        for i in range(2):
            P.dma(mt[i], mem_d[i * 128:(i + 1) * 128, :], writes=[r_mt[i]])
            rms_rows_to_T(mt[i], [r_mt[i]], 1 + 2 * i, 2 + 2 * i, statsM, r_statsM, mn[i], r_mn[i], 6,
                          memT[:, :, i * 128:(i + 1) * 128], [r_memT[i]], junk, r_junk)
        wkv = A.alloc(8 * 1024).rearrange("p (k c) -> p k c", k=8)
        r_wkv = R()
        for cc in range(8):
            a32_, s_ = wdirect(None, w_mkv_d[:, cc * 128:(cc + 1) * 128], GC_GMEM, 128)
            wdirect_cast(wkv[:, :, cc * 128:(cc + 1) * 128], a32_, s_, GC_GMEM, 128, [r_wkv])
        KmT = A.alloc(4 * 256).rearrange("p (h t) -> p h t", h=4)
        r_KmT = [R() for _ in range(4)]
        Vm = A.alloc(2 * 512).rearrange("p (b c) -> p b c", b=2)
        r_Vm = R()
        for h in range(4):
            tm = tmps[h % 2]
            qb_, sb_ = h % 2, 2 + h % 2
            for kc in range(8):
                MM(banks[qb_][:, 0:256], wkv[:, kc, h * 128:(h + 1) * 128], memT[:, kc, :], kc == 0, kc == 7,
                   [r_wkv] + r_memT, [rb[qb_]])
            head_norm(qb_, sb_, ones_bf, 1.0 / 128, tm, ncol=256)
            STT(KmT[:, h, :], banks[qb_][:, 0:256], gcols[:, GC_KM:GC_KM + 1], tm["rstd"][:, 0:256], ALU.mult, ALU.mult,
                [rb[qb_], tm["r_rstd"], r_g], [r_KmT[h]])
        for kb in range(2):
            bk = 6 + kb
            for kc in range(8):
                MM(banks[bk][:, :], memT[:, kc, kb * 128:(kb + 1) * 128], wkv[:, kc, 512:1024], kc == 0, kc == 7,
                   [r_wkv, r_memT[kb]], [rb[bk]])
            ACT(Vm[:, kb, :], banks[bk][:, :], AF.Copy, [rb[bk]], [r_Vm])
        wqm = [walloc(), walloc()]
        r_wqm = [R(), R()]
        PT = [A.alloc(512) for _ in range(4)]
        r_PT = [R() for _ in range(4)]
        rec = A.alloc(512, F32)
        r_rec = R()
        yt = [A.alloc(512) for _ in range(2)]
        r_yt = [R(), R()]
        QmT = [A.alloc(512) for _ in range(2)]
        r_QmT = [R(), R()]
        wqm_all = [walloc(), walloc()]
        r_wqm_all = [R(), R()]
        for h in range(2):
            wload_in(wqm[h], OFF_QM + h * 128, 128, [r_wqm[h]])
        for h in range(2):
            wload_in(wqm_all[h], OFF_QM + (2 + h) * 128, 128, [r_wqm_all[h]])
        wts = [(wqm[0], r_wqm[0]), (wqm[1], r_wqm[1]), (wqm_all[0], r_wqm_all[0]), (wqm_all[1], r_wqm_all[1])]
        n_it = 4 * NTG

        def m0(i):
            h, tg = i // NTG, i % NTG
            proj_fm(i % 3, wts[h][0], [wts[h][1]], tg)

        def m1a(i):
            head_norm_a(i % 3, 3, ones_bf, tmps[i % 2])

        def m1b(i):
            tm = tmps[i % 2]
            head_norm_b(3, 1.0 / 128, tm)
            STT(QmT[i % 2], banks[i % 3][:, :], gcols[:, GC_QM:GC_QM + 1], tm["rstd"], ALU.mult, ALU.mult,
                [rb[i % 3], tm["r_rstd"], r_g], [r_QmT[i % 2]])

        def m2(i):
            h = i // NTG
            for kb in range(2):
                sbk = 4 + kb
                pt = (i * 2 + kb) % 4
                MM(banks[sbk][:, :], KmT[:, h, kb * 128:(kb + 1) * 128], QmT[i % 2], True, True,
                   [r_KmT[h], r_QmT[i % 2]], [rb[sbk]])
                ACT(PT[pt], banks[sbk][:, :], AF.Exp, [rb[sbk]], [r_PT[pt]])

        def m3(i):
            h, tg = i // NTG, i % NTG
            for kb in range(2):
                pt = (i * 2 + kb) % 4
                MM(banks[6][:, :], Vm[:, kb, h * 128:(h + 1) * 128], PT[pt], kb == 0, kb == 1, [r_Vm, r_PT[pt]],
                   [rb[6]])
            for kb in range(2):
                pt = (i * 2 + kb) % 4
                MM(banks[7][:, :], ones_bf, PT[pt], kb == 0, kb == 1, [r_cb, r_PT[pt]], [rb[7]])

        def m3b(i):
            h, tg = i // NTG, i % NTG
            ACT(rec, banks[7][:, :], AF.Ln, [rb[7]], [r_rec])
            ACT(rec, rec, AF.Exp, [r_rec], [r_rec], scale=-1.0)
            yb = yt[i % 2]
            TT("dve", yb, banks[6][:, :], rec, ALU.mult, [rb[6], r_rec], [r_yt[i % 2]])
            P.dma(ysc[768 + h * 128:768 + (h + 1) * 128, tg * 512:(tg + 1) * 512], yb, reads=[r_yt[i % 2]],
                  writes=[ry(("m", h, tg))])
            if tg == NTG - 1:
                pump(8)

        skew(n_it, [m3b, m1b, m2, m0, m1a, m3], [5, 2, 3, 0, 1, 4])

    phase_mem()
    pump(10000)
    P.barrier()
    A.top = 0

    def phase_D():
        statsD = A.alloc(16, F32)
        r_statsD = R()
        MEMSET("dve", statsD[:, 0:1], EPS, [r_statsD])
        hTg = [A.alloc(8 * 512).rearrange("p (k t) -> p k t", k=8) for _ in range(2)]
        r_hTg = [R(), R()]
        ytile = A.alloc(10 * 512).rearrange("p (k t) -> p k t", k=10)
        r_ytile = R()
        xt = A.alloc(4 * D, F32).rearrange("p (a c) -> p a c", a=4)
        r_xt = [R() for _ in range(4)]
        gw = [A.alloc(8 * 384).rearrange("p (k c) -> p k c", k=8) for _ in range(2)]
        r_gw = [R(), R()]
        bw = [A.alloc(10 * 128).rearrange("p (k c) -> p k c", k=10) for _ in range(2)]
        r_bw = [R(), R()]
        G = [A.alloc(512) for _ in range(3)]
        r_G = [R() for _ in range(3)]
        tt_ = [A.alloc(512, F32) for _ in range(3)]
        r_tt = [R() for _ in range(3)]
        mergedT = A.alloc(8 * 512).rearrange("p (k t) -> p k t", k=8)
        r_merged = [R() for _ in range(8)]
        wo = [A.alloc(8 * 512).rearrange("p (k c) -> p k c", k=8) for _ in range(2)]
        r_wo = [R(), R()]
        h2 = [A.alloc(D) for _ in range(2)]
        r_h2 = [R(), R()]
        junk = A.alloc(D)
        r_junk = R()
        h2T = A.alloc(8 * 512).rearrange("p (k t) -> p k t", k=8)
        r_h2T = [R() for _ in range(4)]
        wu = [A.alloc(8 * 512).rearrange("p (k c) -> p k c", k=8) for _ in range(3)]
        r_wu = [R(), R(), R()]
        aT = A.alloc(32 * 512).rearrange("p (f t) -> p f t", f=32)
        r_aT = [R() for _ in range(32)]
        rl = [A.alloc(512) for _ in range(2)]
        r_rl = [R(), R()]
        wd = [A.alloc(4 * 1024).rearrange("p (f c) -> p f c", f=4) for _ in range(2)]
        r_wd = [R(), R()]

        def load_tg_inputs(tg):
            hb_ = tg % 2
            ts = slice(tg * 512, (tg + 1) * 512)
            P.dma(hTg[hb_], hsc[:, :, ts], reads=r_hsc, writes=[r_hTg[hb_]])
            P.dma(ytile, ysc[:, ts].rearrange("(k p) t -> p k t", p=128),
                  reads=[r for (k_, r) in _rysc.items() if k_[2] == tg], writes=[r_ytile])

        def load_chunk_w(c):
            wbi = c % 2
            rd = []
            for br in range(3):
                rd += load_w(None, "gate", 0, 8, br * 1024 + c * 128, br * 1024 + (c + 1) * 128)[1]
            P.dma(gw[wbi], wsc["gate"][c], reads=rd, writes=[r_gw[wbi]])
            rd = load_w(None, "br", 0, 10, c * 128, (c + 1) * 128)[1]
            P.dma(bw[wbi], wsc["br"][c], reads=rd, writes=[r_bw[wbi]])

        def load_x(tg):
            for a in range(4):
                P.dma(xt[:, a, :], x_d[tg * 512 + a * 128:tg * 512 + (a + 1) * 128, :], writes=[r_xt[a]])

        load_tg_inputs(0)
        load_chunk_w(0)
        load_chunk_w(1)
        load_x(0)
        for tg in range(NTG):
            hb_ = tg % 2
            for c in range(8):
                wbi = c % 2
                for br in range(3):
                    for kc in range(8):
                        MM(banks[br][:, :], gw[wbi][:, kc, br * 128:(br + 1) * 128], hTg[hb_][:, kc, :], kc == 0, kc == 7,
                           [r_gw[wbi], r_hTg[hb_]], [rb[br]])
                    col = GC_BGATE + br * 8 + c
                    ACT(G[br], banks[br][:, :], AF.Sigmoid, [rb[br], r_g], [r_G[br]], bias=gcols[:, col:col + 1])
                kr = ((0, 2), (2, 6), (6, 10))
                for br in range(3):
                    k0, k1 = kr[br]
                    for kc in range(k0, k1):
                        MM(banks[3 + br][:, :], bw[wbi][:, kc, :], ytile[:, kc, :], kc == k0, kc == k1 - 1,
                           [r_bw[wbi], r_ytile], [rb[3 + br]])
                    TT("dve", tt_[br], banks[3 + br][:, :], G[br], ALU.mult, [rb[3 + br], r_G[br]], [r_tt[br]])
                TT("pool", tt_[0], tt_[0], tt_[1], ALU.add, [r_tt[0], r_tt[1]], [r_tt[0]])
                TT("pool", mergedT[:, c, :], tt_[0], tt_[2], ALU.add, [r_tt[0], r_tt[2]], [r_merged[c]])
                if c + 2 < 8:
                    load_chunk_w(c + 2)
            for ch in range(2):
                src, rd = load_w(None, "out", 0, 8, ch * 512, (ch + 1) * 512)
                P.dma(wo[ch], src, reads=rd, writes=[r_wo[ch]])
            for g_ in range(2):
                src, rd = load_w(None, "up", 0, 8, g_ * 512, (g_ + 1) * 512)
                P.dma(wu[g_], src, reads=rd, writes=[r_wu[g_]])

            def d0(a):
                for ch in range(2):
                    bk = 4 + 2 * (a % 2) + ch
                    for kc in range(8):
                        MM(banks[bk][:, :], mergedT[:, kc, a * 128:(a + 1) * 128], wo[ch][:, kc, :], kc == 0, kc == 7,
                           [r_merged[kc], r_wo[ch]], [rb[bk]])
                    xs = xt[:, a, ch * 512:(ch + 1) * 512]
                    TT("dve", xs, banks[bk][:, :], xs, ALU.add, [rb[bk], r_xt[a]], [r_xt[a]])

            def d1(a):
                rms_part1(xt[:, a, :], [r_xt[a]], 1 + 2 * a, 2 + 2 * a, statsD, r_statsD, h2[a % 2], r_h2[a % 2], junk,
                          r_junk)

            def d2(a):
                rows_to_T(h2[a % 2], r_h2[a % 2], a % 2, h2T[:, :, a * 128:(a + 1) * 128], [r_h2T[a]])

            skew(4, [d0, d1, d2], [0, 1, 2])
            for fg in range(8):
                wbi = fg % 3
                if fg + 2 < 8:
                    src, rd = load_w(None, "up", 0, 8, (fg + 2) * 512, (fg + 3) * 512)
                    P.dma(wu[(fg + 2) % 3], src, reads=rd, writes=[r_wu[(fg + 2) % 3]])
                if fg in (4, 6):
                    g_ = (fg - 4) // 2
                    src, rd = load_w(None, "down", g_ * 4, g_ * 4 + 4, 0, 1024)
                    P.dma(wd[g_], src, reads=rd, writes=[r_wd[g_]])
                for f4 in range(4):
                    fc = fg * 4 + f4
                    bk = 2 + fc % 4
                    for kc in range(8):
                        MM(banks[bk][:, :], wu[wbi][:, kc, f4 * 128:(f4 + 1) * 128], h2T[:, kc, :], kc == 0, kc == 7,
                           [r_wu[wbi]] + r_h2T, [rb[bk]])
                    ACT(rl[fc % 2], banks[bk][:, :], AF.Relu, [rb[bk]], [r_rl[fc % 2]])
                    TT("dve", aT[:, fc, :], rl[fc % 2], rl[fc % 2], ALU.mult, [r_rl[fc % 2]], [r_aT[fc]])
            if tg + 1 < NTG:
                load_tg_inputs(tg + 1)
                load_chunk_w(0)
                load_chunk_w(1)
            for fg in range(8):
                wbi = fg % 2
                for a in range(4):
                    for ch in range(2):
                        bk = a * 2 + ch
                        for f4 in range(4):
                            fc = fg * 4 + f4
                            MM(banks[bk][:, :], aT[:, fc, a * 128:(a + 1) * 128], wd[wbi][:, f4, ch * 512:(ch + 1) * 512],
                               fc == 0, fc == 31, [r_aT[fc], r_wd[wbi]], [rb[bk]])
                if fg + 2 < 8:
                    src, rd = load_w(None, "down", (fg + 2) * 4, (fg + 2) * 4 + 4, 0, 1024)
                    P.dma(wd[wbi], src, reads=rd, writes=[r_wd[wbi]])
            for a in range(4):
                for ch in range(2):
                    bk = a * 2 + ch
                    xs = xt[:, a, ch * 512:(ch + 1) * 512]
                    TT("dve", xs, banks[bk][:, :], xs, ALU.add, [rb[bk], r_xt[a]], [r_xt[a]])
                P.dma(out_d[tg * 512 + a * 128:tg * 512 + (a + 1) * 128, :], xt[:, a, :], reads=[r_xt[a]], writes=[R()])
            if tg + 1 < NTG:
                load_x(tg + 1)

    phase_D()

    P.finalize()
    st.close()
    return nc


_CACHE = {}


def _get_nc(debug=False):
    if debug not in _CACHE:
        _CACHE[debug] = build(debug)
    return _CACHE[debug]


def make_in_maps(inputs, cores):
    cb, tabs = _consts()
    shared = {}
    for k, v in inputs.items():
        if k in ("x", "mem"):
            continue
        a = np.ascontiguousarray(np.asarray(v, dtype=np.float32))
        shared[k] = a[0]
    shared["cb"] = cb
    shared["tabs"] = tabs
    maps = []
    for c in cores:
        m = dict(shared)
        m["x"] = np.ascontiguousarray(np.asarray(inputs["x"][c], dtype=np.float32))
        m["mem"] = np.ascontiguousarray(np.asarray(inputs["mem"][c], dtype=np.float32))
        maps.append(m)
    return maps


def kernel(**inputs):
    nc = _get_nc(False)
    cores = list(range(8))
    in_maps = make_in_maps(inputs, cores)
    res = run_bass_kernel_spmd(nc, in_maps, core_ids=cores)
    out = np.stack([np.asarray(r["out"], dtype=np.float32) for r in res.results], axis=0)
    return out
```
